# Optimizing a Trainium2 kernel written in Bass

```python
import jax, jax.numpy as jnp
from jax import lax
import numpy as np

D_MODEL = 2048
BATCH = 4
SEQ = 2048
DEPTH = 2

BRANCH_W = D_MODEL // 2
N_BRANCH = 3
Q_BLOCK = 128
ROPE_THETA = 10000.0
HEAD_DIM = 128
MOBA_HEADS = BRANCH_W // HEAD_DIM
MOBA_BLOCK = 256
MOBA_TOPK = 3
RWKV_HEAD = 64
RWKV_HEADS = BRANCH_W // RWKV_HEAD
RWKV_W = RWKV_HEADS * RWKV_HEAD
DECAY_LORA = 64
ICLR_LORA = 64
VRES_LORA = 32
GATE_LORA = 160
RWKV_GN_EPS = 64e-5
QK_NOPE = 128
QK_ROPE = 64
V_HEAD = 128
MLA_HEADS = BRANCH_W // V_HEAD
Q_LORA = 512
KV_LORA = 512
D_FF = 5632
N_EXPERTS = 8
TOP_K = 2
LN_EPS = 1e-5
RMS_EPS = 1e-6
DEEPNORM_ALPHA = (2 * DEPTH) ** 0.25
DEEPNORM_BETA = (8 * DEPTH) ** -0.25
N_DENSE = (DEPTH + 1) // 2
N_MOE = DEPTH // 2
RWKV_SPLIT = (RWKV_W, RWKV_W, RWKV_W, DECAY_LORA, ICLR_LORA, GATE_LORA)
RWKV_COLS = 3 * RWKV_W + DECAY_LORA + ICLR_LORA + GATE_LORA
MLA_COLS = Q_LORA + KV_LORA + QK_ROPE
IN_COLS = 3 * BRANCH_W + RWKV_COLS + MLA_COLS + N_BRANCH * D_MODEL

kernel_name = 'moba_rwkv7_mla_gated_hybrid_deepnorm_moe'


def split_cols(h, sizes):
    return jnp.split(h, np.cumsum(sizes)[:-1].tolist(), axis=-1)


def layer_norm(x, g, b):
    xf = x.astype(jnp.float32)
    mu = xf.mean(-1, keepdims=True)
    var = jnp.square(xf - mu).mean(-1, keepdims=True)
    return ((xf - mu) * lax.rsqrt(var + LN_EPS) * g + b).astype(x.dtype)


def rms_norm(x, g):
    xf = x.astype(jnp.float32)
    return (xf * lax.rsqrt(jnp.square(xf).mean(-1, keepdims=True) + RMS_EPS) * g).astype(x.dtype)


def rope_tables(seq, dim):
    inv = 1.0 / (ROPE_THETA ** (jnp.arange(0, dim, 2, dtype=jnp.float32) / dim))
    ang = jnp.arange(seq, dtype=jnp.float32)[:, None] * inv[None, :]
    return jnp.cos(ang), jnp.sin(ang)


def apply_rope(x, cos, sin):
    x1, x2 = jnp.split(x.astype(jnp.float32), 2, axis=-1)
    c = cos[None, :, None, :]
    s = sin[None, :, None, :]
    return jnp.concatenate([x1 * c - x2 * s, x2 * c + x1 * s], axis=-1).astype(x.dtype)


def token_shift_mix(h, mu):
    prev = jnp.pad(h, ((0, 0), (1, 0), (0, 0)))[:, :-1]
    return h + (prev - h) * mu


def moba_attention(q, k, v):
    B, S, H, Dh = q.shape
    nblk = -(-S // MOBA_BLOCK)
    sp = nblk * MOBA_BLOCK
    topk = min(MOBA_TOPK, nblk)
    scale = Dh ** -0.5
    pad = ((0, 0), (0, sp - S), (0, 0), (0, 0))
    q, k, v = (jnp.pad(t, pad).transpose(0, 2, 1, 3) for t in (q, k, v))
    kb = k.reshape(B, H, nblk, MOBA_BLOCK, Dh)
    vb = v.reshape(B, H, nblk, MOBA_BLOCK, Dh)
    k_mean = kb.astype(jnp.float32).mean(axis=3)
    gather = jax.vmap(jax.vmap(lambda blocks, idx: blocks[idx]))
    blk_ids = jnp.arange(nblk)
    n_sel = topk * MOBA_BLOCK

    def one_block(i):
        start = i * Q_BLOCK
        qc = lax.dynamic_slice_in_dim(q, start, Q_BLOCK, axis=2)
        qpos = start + jnp.arange(Q_BLOCK)
        own = start // MOBA_BLOCK
        gate = jnp.einsum('bhqd,bhnd->bhqn', qc.astype(jnp.float32), k_mean)
        fully_past = blk_ids[None, :] < (qpos // MOBA_BLOCK)[:, None]
        gate = jnp.where(fully_past, gate, -jnp.inf)
        g_val, g_idx = lax.top_k(gate, topk)
        k_sel = gather(kb, g_idx)
        v_sel = gather(vb, g_idx)
        s_sel = jnp.einsum('bhqd,bhqtkd->bhqtk', qc, k_sel).astype(jnp.float32) * scale
        s_sel = jnp.where(jnp.isfinite(g_val)[..., None], s_sel, -jnp.inf)
        k_own = lax.dynamic_slice_in_dim(k, own * MOBA_BLOCK, MOBA_BLOCK, axis=2)
        v_own = lax.dynamic_slice_in_dim(v, own * MOBA_BLOCK, MOBA_BLOCK, axis=2)
        s_own = jnp.einsum('bhqd,bhkd->bhqk', qc, k_own).astype(jnp.float32) * scale
        kpos = own * MOBA_BLOCK + jnp.arange(MOBA_BLOCK)
        s_own = jnp.where(kpos[None, :] <= qpos[:, None], s_own, -jnp.inf)
        s = jnp.concatenate([s_sel.reshape(B, H, Q_BLOCK, n_sel), s_own], axis=-1)
        p = jax.nn.softmax(s, axis=-1).astype(v.dtype)
        p_sel = p[..., :n_sel].reshape(B, H, Q_BLOCK, topk, MOBA_BLOCK)
        return (jnp.einsum('bhqtk,bhqtkd->bhqd', p_sel, v_sel)
                + jnp.einsum('bhqk,bhkd->bhqd', p[..., n_sel:], v_own))

    out = lax.map(one_block, jnp.arange(sp // Q_BLOCK))
    return out.transpose(1, 0, 3, 2, 4).reshape(B, sp, H, Dh)[:, :S]


def causal_attention(q, k, v):
    B, S, H, Dq = q.shape
    Dv = v.shape[-1]
    scale = Dq ** -0.5
    nq = S // Q_BLOCK
    qb = q.reshape(B, nq, Q_BLOCK, H, Dq).transpose(1, 0, 3, 2, 4)
    kpos = jnp.arange(S)

    def one_block(args):
        i, qc = args
        s = jnp.einsum('bhqd,bkhd->bhqk', qc, k).astype(jnp.float32) * scale
        qpos = i * Q_BLOCK + jnp.arange(Q_BLOCK)
        s = jnp.where(kpos[None, :] <= qpos[:, None], s, -jnp.inf)
        p = jax.nn.softmax(s, axis=-1).astype(v.dtype)
        return jnp.einsum('bhqk,bkhd->bqhd', p, v)

    out = lax.map(one_block, (jnp.arange(nq), qb))
    return out.transpose(1, 0, 2, 3, 4).reshape(B, S, H, Dv)


def wkv7_scan(r, decay, k, v, kk, a):
    B, S, H, N = r.shape

    def step(state, inp):
        r_t, w_t, k_t, v_t, kk_t, a_t = inp
        sa = jnp.einsum('bhij,bhj->bhi', state, -kk_t)
        state = (state * w_t[:, :, None, :] + sa[..., None] * (kk_t * a_t)[:, :, None, :]
                 + v_t[..., None] * k_t[:, :, None, :])
        return state, jnp.einsum('bhij,bhj->bhi', state, r_t)

    xs = tuple(t.transpose(1, 0, 2, 3) for t in (r, decay, k, v, kk, a))
    _, ys = lax.scan(step, jnp.zeros((B, H, N, N), jnp.float32), xs)
    return ys.transpose(1, 0, 2, 3)


def rwkv7_time_mix(h, w0, w_lora, a0, a_lora, g_lora, k_k, k_a, r_k, gn_g, gn_b,
                   v_first, xv_lo, v0, v_lora):
    B, S, _ = h.shape
    f32 = jnp.float32
    r, k, v, xw, xa, xg = split_cols(h, RWKV_SPLIT)
    w_log = -jax.nn.softplus(-(w0 + jnp.tanh(xw) @ w_lora)) - 0.5
    decay = jnp.exp(-jnp.exp(w_log.astype(f32)))
    a = jax.nn.sigmoid(a0 + xa @ a_lora)
    g = jax.nn.sigmoid(xg) @ g_lora
    v_raw = v
    if v_first is not None:
        v = v + (v_first - v) * jax.nn.sigmoid(v0 + xv_lo @ v_lora)
    heads = lambda t: t.reshape(B, S, RWKV_HEADS, RWKV_HEAD).astype(f32)
    r, k, v, a, decay = heads(r), heads(k), heads(v), heads(a), heads(decay)
    kk = k * k_k.reshape(RWKV_HEADS, RWKV_HEAD).astype(f32)
    kk = kk / jnp.maximum(jnp.sqrt(jnp.sum(kk * kk, axis=-1, keepdims=True)), 1e-12)
    k = k * (1.0 + (a - 1.0) * k_a.reshape(RWKV_HEADS, RWKV_HEAD).astype(f32))
    y = wkv7_scan(r, decay, k, v, kk, a)
    mu = y.mean(-1, keepdims=True)
    var = jnp.square(y - mu).mean(-1, keepdims=True)
    y = ((y - mu) * lax.rsqrt(var + RWKV_GN_EPS)).reshape(B, S, RWKV_W) * gn_g + gn_b
    bonus = (jnp.sum(r * k * r_k.astype(f32), axis=-1, keepdims=True) * v).reshape(B, S, RWKV_W)
    return ((y + bonus) * g).astype(h.dtype), v_raw


def mla_attention(q_dn, kv_dn, k_pe, q_norm, w_uq, kv_norm, w_ukv, cos, sin):
    B, S, _ = q_dn.shape
    q = (rms_norm(q_dn, q_norm) @ w_uq).reshape(B, S, MLA_HEADS, QK_NOPE + QK_ROPE)
    q_nope, q_pe = jnp.split(q, [QK_NOPE], axis=-1)
    kv = (rms_norm(kv_dn, kv_norm) @ w_ukv).reshape(B, S, MLA_HEADS, QK_NOPE + V_HEAD)
    k_nope, v = jnp.split(kv, [QK_NOPE], axis=-1)
    k_pe = apply_rope(k_pe[:, :, None, :], cos, sin)
    q = jnp.concatenate([q_nope, apply_rope(q_pe, cos, sin)], axis=-1)
    k = jnp.concatenate([k_nope, jnp.broadcast_to(k_pe, (B, S, MLA_HEADS, QK_ROPE))], axis=-1)
    return causal_attention(q, k, v).reshape(B, S, MLA_HEADS * V_HEAD)


def swiglu(h, wg, wu, wd):
    return (jax.nn.silu(h @ wg) * (h @ wu)) @ wd


def moe_swiglu(h, router, wg, wu, wd):
    B, S, D = h.shape
    t = h.reshape(B * S, D)
    logits = (t @ router).astype(jnp.float32)
    top_v, top_i = lax.top_k(logits, TOP_K)
    w = jax.nn.softmax(top_v, axis=-1)
    combine = jnp.sum(jax.nn.one_hot(top_i, N_EXPERTS, dtype=jnp.float32) * w[..., None], axis=1)

    def expert(acc, e):
        wg_e, wu_e, wd_e, c_e = e
        return acc + c_e[:, None].astype(t.dtype) * swiglu(t, wg_e, wu_e, wd_e), None

    out, _ = lax.scan(expert, jnp.zeros_like(t), (wg, wu, wd, combine.T))
    return out.reshape(B, S, D)


def setup_inputs(seed: int = 0) -> dict:
    key = jax.random.key(seed)
    ks = iter(jax.random.split(key, 48))
    f32 = jnp.float32

    def nrm(shape, scale):
        return jax.random.normal(next(ks), shape, f32) * scale

    def uni(shape, lo, hi):
        return jax.random.uniform(next(ks), shape, f32, lo, hi)

    L, V, D = DEPTH, DEPTH - 1, D_MODEL
    beta = DEEPNORM_BETA
    return {
        'x': jax.random.normal(next(ks), (BATCH, SEQ, D), f32),
        'w_in': nrm((L, D, IN_COLS), D ** -0.5),
        'w_in_vres': nrm((V, D, VRES_LORA), D ** -0.5),
        'rwkv_mu': uni((L, RWKV_COLS), 0.0, 1.0),
        'rwkv_mu_vres': uni((V, VRES_LORA), 0.0, 1.0),
        'rwkv_w0': uni((L, RWKV_W), -6.0, -1.0),
        'rwkv_w_lora': nrm((L, DECAY_LORA, RWKV_W), 0.1 * DECAY_LORA ** -0.5),
        'rwkv_a0': nrm((L, RWKV_W), 0.5),
        'rwkv_a_lora': nrm((L, ICLR_LORA, RWKV_W), 0.1 * ICLR_LORA ** -0.5),
        'rwkv_g_lora': nrm((L, GATE_LORA, RWKV_W), GATE_LORA ** -0.5),
        'rwkv_v0': 1.0 + nrm((V, RWKV_W), 0.1),
        'rwkv_v_lora': nrm((V, VRES_LORA, RWKV_W), 0.1 * VRES_LORA ** -0.5),
        'rwkv_k_k': 0.85 + nrm((L, RWKV_W), 0.02),
        'rwkv_k_a': 1.0 + nrm((L, RWKV_W), 0.02),
        'rwkv_r_k': nrm((L, RWKV_HEADS, RWKV_HEAD), 0.1),
        'rwkv_gn_g': 1.0 + nrm((L, RWKV_W), 0.02),
        'rwkv_gn_b': nrm((L, RWKV_W), 0.02),
        'mla_q_norm': 1.0 + nrm((L, Q_LORA), 0.02),
        'mla_w_uq': nrm((L, Q_LORA, MLA_HEADS * (QK_NOPE + QK_ROPE)), Q_LORA ** -0.5),
        'mla_kv_norm': 1.0 + nrm((L, KV_LORA), 0.02),
        'mla_w_ukv': nrm((L, KV_LORA, MLA_HEADS * (QK_NOPE + V_HEAD)), KV_LORA ** -0.5),
        'branch_out': nrm((L, N_BRANCH, BRANCH_W, D), beta * BRANCH_W ** -0.5),
        'w_out': nrm((L, D, D), beta * D ** -0.5),
        'ln1_g': 1.0 + nrm((L, D), 0.02),
        'ln1_b': nrm((L, D), 0.02),
        'ffn_wg': nrm((N_DENSE, D, D_FF), beta * D ** -0.5),
        'ffn_wu': nrm((N_DENSE, D, D_FF), beta * D ** -0.5),
        'ffn_wd': nrm((N_DENSE, D_FF, D), beta * D_FF ** -0.5),
        'moe_router': nrm((N_MOE, D, N_EXPERTS), D ** -0.5),
        'moe_wg': nrm((N_MOE, N_EXPERTS, D, D_FF), beta * D ** -0.5),
        'moe_wu': nrm((N_MOE, N_EXPERTS, D, D_FF), beta * D ** -0.5),
        'moe_wd': nrm((N_MOE, N_EXPERTS, D_FF, D), beta * D_FF ** -0.5),
        'ln2_g': 1.0 + nrm((L, D), 0.02),
        'ln2_b': nrm((L, D), 0.02),
    }


def reference(x, w_in, w_in_vres, rwkv_mu, rwkv_mu_vres, rwkv_w0, rwkv_w_lora, rwkv_a0,
              rwkv_a_lora, rwkv_g_lora, rwkv_v0, rwkv_v_lora, rwkv_k_k, rwkv_k_a, rwkv_r_k,
              rwkv_gn_g, rwkv_gn_b, mla_q_norm, mla_w_uq, mla_kv_norm, mla_w_ukv, branch_out,
              w_out, ln1_g, ln1_b, ffn_wg, ffn_wu, ffn_wd, moe_router, moe_wg, moe_wu, moe_wd,
              ln2_g, ln2_b):
    B, S, D = x.shape
    cos_a, sin_a = rope_tables(S, HEAD_DIM)
    cos_m, sin_m = rope_tables(S, QK_ROPE)
    v_first = None
    for l in range(DEPTH):
        if l == 0:
            proj = x @ w_in[0]
        else:
            proj = x @ jnp.concatenate([w_in[l], w_in_vres[l - 1]], axis=1)
        moba_h, rwkv_h, mla_h, gate_h = split_cols(
            proj[..., :IN_COLS], (3 * BRANCH_W, RWKV_COLS, MLA_COLS, N_BRANCH * D_MODEL))
        qa, ka, va = (t.reshape(B, S, MOBA_HEADS, HEAD_DIM) for t in jnp.split(moba_h, 3, axis=-1))
        y_a = moba_attention(apply_rope(qa, cos_a, sin_a), apply_rope(ka, cos_a, sin_a), va)
        y_a = y_a.reshape(B, S, BRANCH_W)
        rwkv_h = token_shift_mix(rwkv_h, rwkv_mu[l])
        if l == 0:
            y_b, v_first = rwkv7_time_mix(rwkv_h, rwkv_w0[l], rwkv_w_lora[l], rwkv_a0[l], rwkv_a_lora[l],
                                          rwkv_g_lora[l], rwkv_k_k[l], rwkv_k_a[l], rwkv_r_k[l],
                                          rwkv_gn_g[l], rwkv_gn_b[l], None, None, None, None)
        else:
            xv_lo = token_shift_mix(proj[..., IN_COLS:], rwkv_mu_vres[l - 1])
            y_b, _ = rwkv7_time_mix(rwkv_h, rwkv_w0[l], rwkv_w_lora[l], rwkv_a0[l], rwkv_a_lora[l],
                                    rwkv_g_lora[l], rwkv_k_k[l], rwkv_k_a[l], rwkv_r_k[l],
                                    rwkv_gn_g[l], rwkv_gn_b[l], v_first, xv_lo,
                                    rwkv_v0[l - 1], rwkv_v_lora[l - 1])
        q_dn, kv_dn, k_pe = split_cols(mla_h, (Q_LORA, KV_LORA, QK_ROPE))
        y_c = mla_attention(q_dn, kv_dn, k_pe, mla_q_norm[l], mla_w_uq[l], mla_kv_norm[l],
                            mla_w_ukv[l], cos_m, sin_m)
        ys = jnp.stack([y_a, y_b, y_c], axis=2)
        branch = jnp.einsum('bsnc,ncd->bsnd', ys, branch_out[l])
        gates = jax.nn.sigmoid(gate_h.reshape(B, S, N_BRANCH, D_MODEL))
        mix = jnp.sum(gates * branch, axis=2) @ w_out[l]
        x = layer_norm(DEEPNORM_ALPHA * x + mix, ln1_g[l], ln1_b[l])
        j = l // 2
        if l % 2 == 0:
            f = swiglu(x, ffn_wg[j], ffn_wu[j], ffn_wd[j])
        else:
            f = moe_swiglu(x, moe_router[j], moe_wg[j], moe_wu[j], moe_wd[j])
        x = layer_norm(DEEPNORM_ALPHA * x + f, ln2_g[l], ln2_b[l])
    return x
```

```python
import contextlib
import numpy as np
import concourse.bass as bass
import concourse.mybir as mybir
from concourse.bass_utils import run_bass_kernel_spmd

F32 = mybir.dt.float32
BF16 = mybir.dt.bfloat16
F32R = mybir.dt.float32r
AF = mybir.ActivationFunctionType
ALU = mybir.AluOpType
AX = mybir.AxisListType

COMPUTE = ('pe', 'act', 'dve', 'pool')
DMA_RING = 6


class Tile:
    def __init__(self, h, name):
        self.h = h
        self.name = name
        self.w = {}
        self.r = {}

    def __getitem__(self, k):
        return self.h[k]

    def all_w(self):
        return self.w.items()

    def all_r(self):
        return self.r.items()


class Prog:
    def __init__(self, name="k"):
        self.nc = bass.Bass("TRN2", target_bir_lowering=False)
        self.es = contextlib.ExitStack()
        self.ops = {e: [] for e in COMPUTE + ('sp',)}
        self.ncomp = {e: 0 for e in COMPUTE}
        self.waited = {e: {} for e in COMPUTE + ('sp',)}
        self.sems = {}
        self.dma_n = {}
        self.ntile = 0
        self.out_tokens = []
        for e in COMPUTE:
            self.sems[e] = self.es.enter_context(self.nc.semaphore("s_" + e))

    def dram(self, name, shape, dt, kind):
        return self.nc.dram_tensor(name, list(shape), dt, kind=kind).ap()

    def sb(self, shape, dt, name=None):
        self.ntile += 1
        name = name or f"t{self.ntile}"
        h = self.es.enter_context(self.nc.sbuf_tensor(name, list(shape), dt))
        return Tile(h, name)

    def ps(self, shape, dt=F32, name=None):
        self.ntile += 1
        name = name or f"p{self.ntile}"
        h = self.es.enter_context(self.nc.psum_tensor(name, list(shape), dt))
        return Tile(h, name)

    def _deps(self, eng, reads, writes, is_dma=False):
        deps = []
        for t in reads:
            for tok in t.all_w():
                deps.append((tok, True))
        for t in writes:
            for tok in t.all_w():
                if is_dma and tok[0].startswith("d_"):
                    continue
                deps.append((tok, False))
            for tok in t.all_r():
                deps.append((tok, False))
        need = {}
        for (sem, val), raw in deps:
            if sem == eng and not raw:
                continue
            if self.waited[eng].get(sem, 0) >= val:
                continue
            need[sem] = max(need.get(sem, 0), val)
        for sem, val in need.items():
            self.waited[eng][sem] = val
        return list(need.items())

    def _record(self, tok, reads, writes):
        sem, val = tok
        for t in reads:
            if t.r.get(sem, 0) < val:
                t.r[sem] = val
        for t in writes:
            if t.w.get(sem, 0) < val:
                t.w[sem] = val

    def op(self, eng, fn, reads=(), writes=()):
        waits = self._deps(eng, reads, writes)
        self.ncomp[eng] += 1
        tok = (eng, self.ncomp[eng])
        self.ops[eng].append((waits, fn, (eng, 1)))
        self._record(tok, reads, writes)
        return tok

    def dma(self, q, out, in_, reads=(), writes=(), is_output=False, ring=None, **kw):
        rk = ring or q
        DMA_RING = 8 if ring else 6
        n = self.dma_n.get(rk, 0)
        self.dma_n[rk] = n + 1
        slot = n % DMA_RING
        key = f"d_{rk}{slot}"
        if key not in self.sems:
            self.sems[key] = self.es.enter_context(self.nc.semaphore(key))
        prev = 16 * (n // DMA_RING)
        waits = self._deps(q, reads, writes, is_dma=True)
        if prev > 0 and self.waited[q].get(key, 0) < prev:
            waits.append((key, prev))
            self.waited[q][key] = prev
        tok = (key, prev + 16)
        self.ops[q].append((waits, I('dma_start', out=out, in_=in_, **kw), (key, 16)))
        self._record(tok, reads, writes)
        if is_output:
            self.out_tokens.append(tok)
        return tok

    def build(self):
        nc = self.nc
        fin = {}
        for sem, val in self.out_tokens:
            fin[sem] = max(fin.get(sem, 0), val)
        ops = self.ops
        sems = self.sems

        def emit(engname):
            def f(e):
                for waits, fn, inc in ops[engname]:
                    for sem, val in waits:
                        e.wait_ge(sems[sem], val)
                    ins = fn(e)
                    if inc is not None:
                        ins.then_inc(sems[inc[0]], inc[1])
                if engname == 'sp':
                    for sem, val in fin.items():
                        e.wait_ge(sems[sem], val)
            return f

        with nc.allow_low_precision("fp32r (11-bit mantissa) rounding of single-pass matmul operands"), nc.Block() as block:
            block.sync(emit('sp'))
            block.tensor(emit('pe'))
            block.scalar(emit('act'))
            block.vector(emit('dve'))
            block.gpsimd(emit('pool'))
        self.es.close()
        return nc


D = 2048
S = 2048
B = 4
NTOK = 1024
BW = 1024
DFF = 5632
NE = 8
NCAT = 13824
GATE_CHUNK0 = 60
ALPHA = 4 ** 0.25
LN_EPS = 1e-5


ROUND_F32R = [False]


def I(method, *args, **kw):
    if ROUND_F32R[0] and method in ("activation", "copy", "tensor_copy", "tensor_tensor", "tensor_scalar", "tensor_scalar_mul", "scalar_tensor_tensor", "tensor_scalar_max", "reciprocal", "memset") and args:
        out = args[0]
        if out.dtype == F32 and type(out.tensor).__name__.startswith("SB"):
            args = (out.bitcast(F32R),) + tuple(args[1:])
    return lambda e: getattr(e, method)(*args, **kw)


def mm(ps_ap, lhsT, rhs, start, stop, r=False):
    if r and USE_F32R:
        lhsT = lhsT.bitcast(F32R)
        rhs = rhs.bitcast(F32R)
    return I('matmul', ps_ap, lhsT, rhs, start=start, stop=stop)


class Gemm:
    def __init__(self, p, KC, wcols=512, nbuf=3, q='pool'):
        self.p = p
        self.KC = KC
        self.wcols = wcols
        self.wb = [p.sb([128, KC, wcols], BF16) for _ in range(nbuf)]
        self.i = 0
        self.q = q

    def load(self, w_ap, c0, ncols):
        wt = self.wb[self.i % len(self.wb)]
        self.i += 1
        wv = w_ap.rearrange("(c p) n -> p c n", p=128)
        self.p.dma(self.q, wt[:, :, 0:ncols], wv[:, :, c0:c0 + ncols], writes=[wt])
        return wt


def build_proj():
    p = Prog()
    KC = D // 128
    xT = p.dram("xT", [D, NTOK], F32, "ExternalInput")
    w = p.dram("w", [D, NCAT], F32, "ExternalInput")
    oT = p.dram("oT", [NCAT, NTOK], F32, "ExternalOutput")
    xs = p.sb([128, KC, NTOK], BF16, "xs")
    xv = xT.rearrange("(c p) t -> p c t", p=128)
    for i in range(4):
        p.dma('pool', xs[:, i * 4:(i + 1) * 4, :], xv[:, i * 4:(i + 1) * 4, :], writes=[xs])
    g = Gemm(p, KC)
    pss = [p.ps([128, 512], F32) for _ in range(6)]
    ob = [p.sb([128, NTOK], F32) for _ in range(3)]
    pi = 0
    for nb in range(NCAT // 512):
        wt = g.load(w, nb * 512, 512)
        for j in range(4):
            n = nb * 4 + j
            o = ob[n % 3]
            for th in range(NTOK // 512):
                ps = pss[pi % 6]
                pi += 1
                for k in range(KC):
                    p.op('pe', mm(ps[:, :], wt[:, k, j * 128:(j + 1) * 128], xs[:, k, th * 512:(th + 1) * 512],
                                  k == 0, k == KC - 1), reads=[wt, xs], writes=[ps])
                osl = o[:, th * 512:(th + 1) * 512]
                if n >= GATE_CHUNK0:
                    p.op('act', I('activation', osl, ps[:, :], AF.Sigmoid),
                         reads=[ps], writes=[o])
                elif th == 0:
                    p.op('act', I('copy', osl, ps[:, :]), reads=[ps], writes=[o])
                else:
                    p.op('dve', I('tensor_copy', osl, ps[:, :]), reads=[ps], writes=[o])
            p.dma('sp', oT[n * 128:(n + 1) * 128, :], o[:, :], reads=[o], is_output=True)
    return p.build()


_W_IN_COLS = 13664


def make_wcat(w_in_l, w_vres_l):
    z = lambda n: np.zeros((D, n), np.float32)
    segs = [
        w_in_l[:, 0:3072],
        w_in_l[:, 3072:6144],
        w_in_l[:, 6144:6432],
        w_vres_l if w_vres_l is not None else z(32),
        z(64),
        w_in_l[:, 6432:7520],
        z(64),
        w_in_l[:, 7520:13664],
    ]
    return np.ascontiguousarray(np.concatenate(segs, axis=1))


_NC_CACHE = {}


def get_nc(name, builder):
    if name not in _NC_CACHE:
        _NC_CACHE[name] = builder()
    return _NC_CACHE[name]


def launch(name, builder, in_maps):
    nc = get_nc(name, builder)
    res = run_bass_kernel_spmd(nc, in_maps, core_ids=list(range(len(in_maps))))
    return res.results


BIG = 30000.0


def const_tables():
    c = {}
    c["ident"] = np.eye(128, dtype=np.float32)
    k = np.arange(128)[:, None]
    q = np.arange(128)[None, :]
    c["tri"] = np.where(k <= q, 0.0, -BIG).astype(np.float32)
    sel = np.zeros((8, 8, 128), np.float32)
    for n in range(8):
        sel[n, n, :] = 1.0
    c["selE"] = sel.reshape(8, 8 * 128)
    pos = np.arange(S, dtype=np.float32)

    def tabs(dim):
        inv = (1.0 / (10000.0 ** (np.arange(0, dim, 2, dtype=np.float32) / dim))).astype(np.float32)
        ang = (pos[:, None] * inv[None, :]).astype(np.float32)
        co, si = np.cos(ang).astype(np.float32).T, np.sin(ang).astype(np.float32).T
        return (np.ascontiguousarray(np.concatenate([co, co], 0)),
                np.ascontiguousarray(np.concatenate([-si, si], 0)))
    c["cosA"], c["sinA"] = tabs(128)
    c["cosM"], c["sinM"] = tabs(64)
    gpen = np.zeros((16, 8), np.float32)
    own = np.zeros((16, 8), np.float32)
    for i in range(16):
        gpen[i, i // 2:] = -1e30
        own[i, i // 2] = 1.0
    c["gpen"] = np.ascontiguousarray(np.broadcast_to(gpen.reshape(1, 128), (128, 128)))
    c["ownm"] = np.ascontiguousarray(np.broadcast_to(own.reshape(1, 128), (128, 128)))
    return c


class Ctx:
    def __init__(self, p, cd):
        self.p = p
        self.cd = cd
        self.pb = [p.ps([128, 512], F32, f"pb{i}") for i in range(8)]
        self.ident = self.load_const("ident", [128, 128])
        self.ones = p.sb([128, 128], F32, "ones")
        self.ones_bf = p.sb([128, 128], BF16, "ones_bf")
        p.op('pool', I('memset', self.ones[:, :], 1.0), writes=[self.ones])
        p.op('pool', I('memset', self.ones_bf[:, :], 1.0), writes=[self.ones_bf])

    def load_const(self, name, shape, dt=F32, tag=""):
        t = self.p.sb(shape, dt, "k_" + name + tag + ("_bf" if dt == BF16 else ("_r" if dt == F32R else "")))
        src = self.cd[name]
        if dt == F32:
            self.p.dma('sp', t[:, :], src, writes=[t])
        else:
            self.p.dma('pool', t[:, :], src, writes=[t])
        return t


NSLAB = 21


class View:
    def __init__(self, tile, ap, rng, state=None):
        self.tile = tile
        self.ap = ap
        self.name = tile.name
        self.rng = rng
        if state is None:
            self.w, self.r = {}, {}
            tile.views.append(self)
        else:
            self.w, self.r = state

    def derive(self, ap):
        return View(self.tile, ap, self.rng, (self.w, self.r))

    def __getitem__(self, k):
        return self.ap[k]

    def _overlapping(self):
        r0, r1, b0, b1 = self.rng
        for v in self.tile.views:
            q0, q1, c0, c1 = v.rng
            if q0 < r1 and r0 < q1 and c0 < b1 and b0 < c1:
                yield v

    def all_w(self):
        for v in self._overlapping():
            yield from v.w.items()

    def all_r(self):
        for v in self._overlapping():
            yield from v.r.items()


class Slabs:
    def __init__(self, p):
        self.t = [p.sb([128, 2048], F32, f"slab{i}") for i in range(NSLAB)]
        for t in self.t:
            t.views = []
        self.cache = {}

    def _get(self, key, mk):
        if key not in self.cache:
            self.cache[key] = mk()
        return self.cache[key]

    def f32(self, i, rows=128, cols=2048, off=0):
        return self._get((i, "f", rows, cols, off),
                         lambda: View(self.t[i], self.t[i].h[0:rows, off:off + cols], (0, rows, off * 4, (off + cols) * 4)))

    def bf(self, i, rows=128, cols=4096, off=0):
        return self._get((i, "b", rows, cols, off),
                         lambda: View(self.t[i], self.t[i].h.bitcast(BF16)[0:rows, off:off + cols], (0, rows, off * 2, (off + cols) * 2)))


class AttnRes:
    def __init__(self, p, cx, sl):
        self.tri = cx.load_const("tri", [128, 128], BF16)
        self.selE = cx.load_const("selE", [8, 1024], BF16)
        self.ident_bf = cx.load_const("ident", [128, 128], BF16)
        self.RTb = p.sb([8, S], BF16, "a_RTb")
        self.gpen = cx.load_const("gpen", [128, 128])
        self.ownm = cx.load_const("ownm", [128, 128])
        self.raw = [sl.f32(0), sl.f32(1)]
        self.rot = [sl.f32(2), sl.f32(3)]
        self.t1 = sl.f32(4)
        self.t2 = sl.f32(5)
        self.qf = sl.f32(6)
        self.kf = sl.f32(7)
        self.qb = [sl.bf(8, cols=2048), sl.bf(8, cols=2048, off=2048)]
        self.kb = [sl.bf(9, cols=2048), sl.bf(9, cols=2048, off=2048)]
        vv = sl.bf(10, cols=2048)
        self.v = vv.derive(vv.ap.rearrange("p (b d) -> p b d", d=128))
        self.pt = [sl.bf(10, cols=512, off=2048 + 512 * i) for i in range(3)]
        self.krow = sl.f32(11, rows=1)
        self.RT = sl.f32(12, rows=8)
        self.rs = [sl.f32(13, cols=512, off=0), sl.f32(13, cols=512, off=512)]
        self.o = [sl.f32(13, cols=512, off=1024), sl.f32(13, cols=512, off=1536)]
        self.cos = sl.f32(14)
        self.sin = sl.f32(15)
        self.small = {n: p.sb([128, 128], F32, "a_s_" + n) for n in
                      ("qq", "nb", "gate", "mx", "sel", "R", "kmT", "km2", "kmbc", "qq2")}


def rope_fm(p, out_f, raw, rot, cos, sin, t1, t2, nrow=128):
    p.op('dve', I('tensor_tensor', t1[0:nrow, :], raw[0:nrow, :], cos[0:nrow, :], ALU.mult),
         reads=[raw, cos], writes=[t1])
    p.op('pool', I('tensor_tensor', t2[0:nrow, :], rot[0:nrow, :], sin[0:nrow, :], ALU.mult),
         reads=[rot, sin], writes=[t2])
    p.op('dve', I('tensor_tensor', out_f[0:nrow, :], t1[0:nrow, :], t2[0:nrow, :], ALU.add),
         reads=[t1, t2], writes=[out_f])


def attn_core(p, cx, ar, qparts, kparts, scale, out_dram, out_tile, gating):
    pb = cx.pb
    sm = ar.small
    sq = ar.t1
    first = True
    for (_, nr, qf) in qparts:
        p.op('act', I('activation', sq[0:nr, :], qf[0:nr, :], AF.Square),
             reads=[qf], writes=[sq])
        for i in range(16):
            p.op('pe', mm(pb[0][:, i:i + 1], sq[0:nr, i * 128:(i + 1) * 128], cx.ones[0:nr, 0:1],
                          True, True), reads=[sq, cx.ones], writes=[pb[0]])
        if first:
            p.op('dve', I('tensor_copy', sm["qq"][:, 0:16], pb[0][:, 0:16]), reads=[pb[0]], writes=[sm["qq"]])
        else:
            p.op('dve', I('tensor_tensor', sm["qq"][:, 0:16], sm["qq"][:, 0:16], pb[0][:, 0:16], ALU.add),
                 reads=[pb[0], sm["qq"]], writes=[sm["qq"]])
        first = False
    sk = ar.t2
    for pi_, (_, nr, kf) in enumerate(kparts):
        p.op('act', I('activation', sk[0:nr, :], kf[0:nr, :], AF.Square),
             reads=[kf], writes=[sk])
        for c in range(4):
            p.op('pe', mm(pb[1 + c][0:1, :], cx.ones[0:nr, 0:1], sk[0:nr, c * 512:(c + 1) * 512],
                          True, True), reads=[sk, cx.ones], writes=[pb[1 + c]])
        for c in range(4):
            if pi_ == 0:
                p.op('dve', I('tensor_copy', ar.krow[0:1, c * 512:(c + 1) * 512], pb[1 + c][0:1, :]),
                     reads=[pb[1 + c]], writes=[ar.krow])
            else:
                p.op('dve', I('tensor_tensor', ar.krow[0:1, c * 512:(c + 1) * 512],
                                                          ar.krow[0:1, c * 512:(c + 1) * 512], pb[1 + c][0:1, :], ALU.add),
                     reads=[pb[1 + c], ar.krow], writes=[ar.krow])
    p.op('dve', I('reduce_max', sm["km2"][0:1, 0:1], ar.krow[0:1, :], AX.X), reads=[ar.krow], writes=[sm["km2"]])
    p.op('pe', mm(pb[0][:, 32:33], cx.ones[0:1, :], sm["km2"][0:1, 0:1], True, True),
         reads=[cx.ones, sm["km2"]], writes=[pb[0]])
    p.op('dve', I('tensor_copy', sm["kmbc"][:, 0:1], pb[0][:, 32:33]), reads=[pb[0]], writes=[sm["kmbc"]])
    p.op('dve', I('tensor_scalar_mul', sm["qq2"][:, 0:16], sm["qq"][:, 0:16], sm["kmbc"][:, 0:1]),
         reads=[sm["qq"], sm["kmbc"]], writes=[sm["qq2"]])
    p.op('act', I('activation', sm["nb"][:, 0:16], sm["qq2"][:, 0:16], AF.Sqrt),
         reads=[sm["qq2"]], writes=[sm["nb"]])
    R3 = sm["R"][:, :].rearrange("p (i n) -> p i n", n=8)
    nb3 = sm["nb"][:, 0:16].unsqueeze(2).to_broadcast([128, 16, 8])
    if gating:
        kf = kparts[0][2]
        qf = qparts[0][2]
        p.op('dve', I('reduce_sum', sm["kmT"][:, 0:8], kf[:, :].rearrange("p (n k) -> p n k", k=256), AX.X),
             reads=[kf], writes=[sm["kmT"]])
        p.op('dve', I('tensor_scalar_mul', sm["kmT"][:, 0:8], sm["kmT"][:, 0:8], 1.0 / 256.0),
             reads=[sm["kmT"]], writes=[sm["kmT"]])
        for i in range(16):
            p.op('pe', mm(pb[5][:, i * 8:(i + 1) * 8], qf[:, i * 128:(i + 1) * 128], sm["kmT"][:, 0:8], True, True),
                 reads=[qf, sm["kmT"]], writes=[pb[5]])
        p.op('dve', I('tensor_tensor', sm["gate"][:, :], pb[5][:, 0:128], ar.gpen[:, :], ALU.add),
             reads=[pb[5], ar.gpen], writes=[sm["gate"]])
        for i in range(16):
            p.op('dve', I('max', sm["mx"][:, i * 8:(i + 1) * 8], sm["gate"][:, i * 8:(i + 1) * 8]),
                 reads=[sm["gate"]], writes=[sm["mx"]])
        g3 = sm["gate"][:, :].rearrange("p (i n) -> p i n", n=8)
        thr3 = sm["mx"][:, :].rearrange("p (i n) -> p i n", n=8)[:, :, 2:3].to_broadcast([128, 16, 8])
        s3 = sm["sel"][:, :].rearrange("p (i n) -> p i n", n=8)
        p.op('dve', I('tensor_tensor', s3, g3, thr3, ALU.is_ge), reads=[sm["gate"], sm["mx"]], writes=[sm["sel"]])
        p.op('dve', I('tensor_tensor', sm["sel"][:, :], sm["sel"][:, :], ar.ownm[:, :], ALU.max),
             reads=[sm["sel"], ar.ownm], writes=[sm["sel"]])
        p.op('dve', I('tensor_scalar', sm["R"][:, :], sm["sel"][:, :], -1.0, BIG, ALU.add, ALU.mult),
             reads=[sm["sel"]], writes=[sm["R"]])
        p.op('dve', I('tensor_tensor', R3, R3, nb3, ALU.subtract), reads=[sm["R"], sm["nb"]], writes=[sm["R"]])
    else:
        p.op('dve', I('memset', sm["R"][:, :], 0.0), writes=[sm["R"]])
        p.op('dve', I('tensor_tensor', R3, R3, nb3, ALU.subtract), reads=[sm["R"], sm["nb"]], writes=[sm["R"]])
    for i in range(16):
        bank = pb[1 + i // 4]
        p.op('pe', I('transpose', bank[0:8, (i % 4) * 128:(i % 4 + 1) * 128],
                                                          sm["R"][:, i * 8:(i + 1) * 8], cx.ident[:, :]),
             reads=[sm["R"], cx.ident], writes=[bank])
    for c in range(4):
        p.op('dve', I('tensor_copy', ar.RTb[0:8, c * 512:(c + 1) * 512], pb[1 + c][0:8, :]),
             reads=[pb[1 + c]], writes=[ar.RTb])
    steps = []
    sti = 0
    for c in range(4):
        nj = 4 * c + 4
        for j in range(nj):
            steps.append((c, j, nj, sti))
            sti += 1

    def stage1(c, j, nj, si):
        q0 = max(0, j - 4 * c) * 128
        cols = slice(q0, 512)
        gcols = slice(c * 512 + q0, (c + 1) * 512)
        st = pb[si % 4]
        pt = ar.pt[si % 3]
        for pi_, ((qb, nr, _), (kb, _, _)) in enumerate(zip(qparts, kparts)):
            p.op('pe', mm(st[:, cols], kb[0:nr, j * 128:(j + 1) * 128], qb[0:nr, gcols], pi_ == 0, False),
                 reads=[kb, qb], writes=[st])
        n = j // 2
        diag = j >= 4 * c
        p.op('pe', mm(st[:, cols], ar.selE[0:8, n * 128:(n + 1) * 128], ar.RTb[0:8, gcols], False, not diag),
             reads=[ar.selE, ar.RTb], writes=[st])
        if diag:
            p.op('pe', mm(st[:, q0:q0 + 128], ar.ident_bf[:, :], ar.tri[:, :], False, True),
                 reads=[ar.ident_bf, ar.tri], writes=[st])
        p.op('act', I('activation', pt[:, cols], st[:, cols], AF.Exp, scale=scale), reads=[st], writes=[pt])

    def stage2(c, j, nj, si):
        q0 = max(0, j - 4 * c) * 128
        cols = slice(q0, 512)
        OT = pb[4 + (c % 2)]
        SM = pb[6 + (c % 2)]
        pt = ar.pt[si % 3]
        p.op('pe', mm(OT[:, cols], ar.v[:, j, :], pt[:, cols], j == 0, j == nj - 1), reads=[ar.v, pt], writes=[OT])
        p.op('pe', mm(SM[:, cols], cx.ones_bf[:, :], pt[:, cols], j == 0, j == nj - 1),
             reads=[cx.ones_bf, pt], writes=[SM])
        if j == nj - 1:
            rs = ar.rs[c % 2]
            o = ar.o[c % 2]
            p.op('dve', I('reciprocal', rs[:, :], SM[:, :]), reads=[SM], writes=[rs])
            p.op('dve', I('tensor_tensor', o[:, :], OT[:, :], rs[:, :], ALU.mult), reads=[OT, rs], writes=[o])
            p.dma('sp', out_dram[:, c * 512:(c + 1) * 512], o[:, :], reads=[o], writes=[out_tile],
                  is_output=(out_tile.name == "EXT"))

    LOOK = 2
    for i in range(len(steps) + LOOK):
        if i < len(steps):
            stage1(*steps[i])
        if i >= LOOK:
            stage2(*steps[i - LOOK])


def load_rot(p, q, dst, src, half):
    p.dma(q, dst[0:half, :], src[half:2 * half, :], writes=[dst])
    p.dma(q, dst[half:2 * half, :], src[0:half, :], writes=[dst])


def moba_head(p, cx, ar, cosA, sinA, q_src, k_src, v_src, src_tile, out_dram, out_tile, v_tile=None):
    for (src, ff, bb, ri) in ((q_src, ar.qf, ar.qb[0], 0), (k_src, ar.kf, ar.kb[0], 1)):
        raw, rot = ar.raw[ri], ar.rot[ri]
        p.dma('sp', raw[:, :], src, reads=[src_tile], writes=[raw])
        p.dma('act', rot[0:64, :], src[64:128, :], reads=[src_tile], writes=[rot])
        p.dma('act', rot[64:128, :], src[0:64, :], reads=[src_tile], writes=[rot])
        rope_fm(p, ff, raw, rot, cosA, sinA, ar.t1, ar.t2)
        p.op('act', I('copy', bb[:, :], ff[:, :]), reads=[ff], writes=[bb])
    p.dma('pool', ar.v[:, :, :], v_src.rearrange("(b p) d -> p b d", p=128), reads=[v_tile or src_tile], writes=[ar.v])
    attn_core(p, cx, ar, [(ar.qb[0], 128, ar.qf)], [(ar.kb[0], 128, ar.kf)], 128 ** -0.5, out_dram, out_tile, True)


RMS_EPS = 1e-6
ROW_MLA = 6528
ROW_GATE = 7680


class MlaRes:
    def __init__(self, p, sl):
        self.xnq = [sl.bf(16 + c // 2, cols=2048, off=(c % 2) * 2048) for c in range(4)]
        self.xnkv = [sl.bf(18 + c // 2, cols=2048, off=(c % 2) * 2048) for c in range(4)]
        self.kpe_f = sl.f32(20, rows=64)
        self.kpe_b = p.sb([64, S], BF16, "kpe_b")
        self.wq = p.sb([128, 4, 256], BF16, "m_wq")
        self.wkv = p.sb([128, 4, 256], BF16, "m_wkv")
        self.gq = p.sb([128, 4], F32, "m_gq")
        self.gkv = p.sb([128, 4], F32, "m_gkv")
        self.row = p.sb([1, 512], F32, "m_row")
        self.eps = p.sb([1, 1], F32, "m_eps")
        p.op('pool', I('memset', self.eps[:, :], RMS_EPS), writes=[self.eps])


def mla_stage(p, cx, ar, mr, projT, proj_t, w_uq, w_ukv, qn, kvn, cosM, sinM, ysT, ys_t, row0, heads=range(8), hook=None):
    pb = cx.pb
    p.dma('sp', ar.cos[0:64, :], cosM, writes=[ar.cos])
    p.dma('sp', ar.sin[0:64, :], sinM, writes=[ar.sin])
    p.dma('sp', mr.gq[:, :], qn.rearrange("(c p) -> p c", p=128), writes=[mr.gq], allow_slow_non_contiguous=True)
    p.dma('sp', mr.gkv[:, :], kvn.rearrange("(c p) -> p c", p=128), writes=[mr.gkv], allow_slow_non_contiguous=True)
    xt = ar.raw[0].derive(ar.raw[0].ap.rearrange("p (c t) -> p c t", c=4))
    sq = ar.raw[1].derive(ar.raw[1].ap.rearrange("p (c t) -> p c t", c=4))
    for (r0, g, xn) in ((ROW_MLA, mr.gq, mr.xnq), (ROW_MLA + 512, mr.gkv, mr.xnkv)):
        for ch in range(4):
            cs = slice(ch * 512, (ch + 1) * 512)
            p.dma('sp', xt[:, :, :], projT[r0:r0 + 512, cs].rearrange("(c p) t -> p c t", p=128),
                  reads=[proj_t], writes=[xt])
            p.op('act', I('activation', sq[:, :, :], xt[:, :, :], AF.Square), reads=[xt], writes=[sq])
            for c in range(4):
                p.op('pe', mm(pb[0][0:1, :], cx.ones[:, 0:1], sq[:, c, :], c == 0, c == 3),
                     reads=[cx.ones, sq], writes=[pb[0]])
            p.op('act', I('activation', mr.row[0:1, :], pb[0][0:1, :], AF.Sqrt, bias=mr.eps[0:1, 0:1],
                                               scale=1.0 / 512.0), reads=[pb[0], mr.eps], writes=[mr.row])
            p.op('dve', I('reciprocal', mr.row[0:1, :], mr.row[0:1, :]), reads=[mr.row], writes=[mr.row])
            p.op('pe', mm(pb[1][:, :], cx.ones[0:1, :], mr.row[0:1, :], True, True),
                 reads=[cx.ones, mr.row], writes=[pb[1]])
            for c in range(4):
                p.op('dve', I('scalar_tensor_tensor',
                    xn[c][:, cs], xt[:, c, :], g[:, c:c + 1], pb[1][:, :], ALU.mult, ALU.mult),
                    reads=[xt, g, pb[1]], writes=[xn[c]])
    rk = ROW_MLA + 1024
    p.dma('sp', ar.raw[0][0:64, :], projT[rk:rk + 64, :], reads=[proj_t], writes=[ar.raw[0]])
    p.dma('act', ar.rot[0][0:32, :], projT[rk + 32:rk + 64, :], reads=[proj_t], writes=[ar.rot[0]])
    p.dma('act', ar.rot[0][32:64, :], projT[rk:rk + 32, :], reads=[proj_t], writes=[ar.rot[0]])
    rope_fm(p, mr.kpe_f, ar.raw[0], ar.rot[0], ar.cos, ar.sin, ar.t1, ar.t2, nrow=64)
    p.op('act', I('copy', mr.kpe_b[:, :], mr.kpe_f[0:64, :]), reads=[mr.kpe_f], writes=[mr.kpe_b])
    uqv = w_uq.rearrange("(c p) n -> p c n", p=128)
    ukvv = w_ukv.rearrange("(c p) n -> p c n", p=128)
    bi = 0
    for h in heads:
        if hook is not None:
            hook()
        b0 = h * 192
        p.dma('pool', mr.wq[:, :, 0:192], uqv[:, :, b0:b0 + 192], writes=[mr.wq])
        p.dma('pool', mr.wq[:, :, 192:224], uqv[:, :, b0 + 160:b0 + 192], writes=[mr.wq])
        p.dma('pool', mr.wq[:, :, 224:256], uqv[:, :, b0 + 128:b0 + 160], writes=[mr.wq])
        p.dma('pool', mr.wkv[:, :, :], ukvv[:, :, h * 256:(h + 1) * 256], writes=[mr.wkv])

        def proj_fm(wt, c0, ncol, xn, dst, nrow):
            nonlocal bi
            for ch in range(4):
                ps = pb[bi % 4]
                bi += 1
                cs = slice(ch * 512, (ch + 1) * 512)
                for c in range(4):
                    p.op('pe', mm(ps[0:nrow, :], wt[:, c, c0:c0 + ncol], xn[c][:, cs], c == 0, c == 3),
                         reads=[wt, xn[c]], writes=[ps])
                p.op('act', I('copy', dst[0:nrow, cs], ps[0:nrow, :]), reads=[ps], writes=[dst])

        proj_fm(mr.wq, 0, 128, mr.xnq, ar.qf, 128)
        p.op('dve', I('tensor_copy', ar.qb[0][:, :], ar.qf[:, :]), reads=[ar.qf], writes=[ar.qb[0]])
        proj_fm(mr.wq, 128, 64, mr.xnq, ar.raw[0], 64)
        proj_fm(mr.wq, 192, 64, mr.xnq, ar.rot[0], 64)
        rope_fm(p, ar.raw[1], ar.raw[0], ar.rot[0], ar.cos, ar.sin, ar.t1, ar.t2, nrow=64)
        p.op('dve', I('tensor_copy', ar.qb[1][0:64, :], ar.raw[1][0:64, :]), reads=[ar.raw[1]], writes=[ar.qb[1]])
        proj_fm(mr.wkv, 0, 128, mr.xnkv, ar.kf, 128)
        p.op('dve', I('tensor_copy', ar.kb[0][:, :], ar.kf[:, :]), reads=[ar.kf], writes=[ar.kb[0]])
        for g4 in range(4):
            ps = pb[4 + g4 % 2]
            for t4 in range(4):
                tb = g4 * 4 + t4
                for c in range(4):
                    p.op('pe', mm(ps[:, t4 * 128:(t4 + 1) * 128], mr.xnkv[c][:, tb * 128:(tb + 1) * 128],
                                  mr.wkv[:, c, 128:256], c == 0, c == 3), reads=[mr.xnkv[c], mr.wkv], writes=[ps])
            p.op('act', I('copy',
                ar.v[:, g4 * 4:(g4 + 1) * 4, :], ps[:, :].rearrange("p (b d) -> p b d", d=128)),
                reads=[ps], writes=[ar.v])
        attn_core(p, cx, ar, [(ar.qb[0], 128, ar.qf), (ar.qb[1], 64, ar.raw[1])],
                  [(ar.kb[0], 128, ar.kf), (mr.kpe_b, 64, mr.kpe_f)], 192 ** -0.5,
                  ysT[row0 + h * 128:row0 + (h + 1) * 128, :], ys_t, False)


ROW_RWKV = 3072
ROW_LORA = 6144
GN_EPS = 64e-5
USE_F32R = False


def RV(ap):
    return ap.bitcast(F32R) if USE_F32R else ap


def rwkv_tables():
    c = {}
    t = np.arange(128)
    same = (t[:, None] // 32) == (t[None, :] // 32)
    strict = (same & (t[:, None] < t[None, :])).astype(np.float32)
    incl = (same & (t[:, None] <= t[None, :])).astype(np.float32)
    c["mask5"] = np.ascontiguousarray(np.stack([strict, strict.T, strict, incl, incl], 1).reshape(128, 640))
    cm = (t[:, None] // 32 == np.arange(4)[None, :]).astype(np.float32)
    c["cmask"] = np.ascontiguousarray(cm)
    c["tmask"] = np.ascontiguousarray(np.broadcast_to(cm.T.reshape(1, 512), (64, 512)))
    blk = np.zeros((128, 128), np.float32)
    blk[:64, :64] = 1
    blk[64:, 64:] = 1
    c["blk"] = blk
    return c


class RwkvRes:
    def __init__(self, p, cx, sl):
        self.mask5 = cx.load_const("mask5", [128, 640])
        self.cmask = cx.load_const("cmask", [128, 4])
        self.tmask = cx.load_const("tmask", [64, 512])
        self.blk = cx.load_const("blk", [128, 128], F32R if USE_F32R else F32)
        self.ident_r = cx.load_const("ident", [128, 128], F32R) if USE_F32R else cx.ident
        self.sl = sl
        s = lambda n, shape: p.sb(shape, F32, "r_" + n)

        def v3(i, rows, off, a, b_):
            vw = sl.f32(i, rows=rows, cols=a * b_, off=off)
            return vw.derive(vw.ap.rearrange("p (a b) -> p a b", b=b_))

        self.prm = s("prm", [128, 16, 8])
        self.lw1 = sl.f32(19, cols=1024, off=0)
        self.lw2 = sl.f32(19, cols=1024, off=1024)
        self.lw3 = sl.f32(20, rows=64, cols=1024, off=0)
        self.cst = s("cst", [128, 4])
        for i, v in enumerate((-0.5, GN_EPS, 1.0)):
            p.op('pool', I('memset', self.cst[:, i:i + 1], v), writes=[self.cst])
        self.pc = s("pc", [128, 64])
        self.mul = s("mul", [128, 3])
        self.cumc = s("cumc", [128, 64])
        def b3(i, rows, off, a, b_):
            vw = sl.bf(i, rows=rows, cols=a * b_, off=off)
            return vw.derive(vw.ap.rearrange("p (a b) -> p a b", b=b_))

        self.ident_bf = cx.load_const("ident", [128, 128], BF16, tag="_rw")
        self.tm = [b3(9, 128, 640 * i, 5, 128) for i in range(2)]
        self.dg = [v3(18, 128, 512 * i, 4, 128) for i in range(2)]
        self.z = [v3(18, 64, 1024 + 512 * i, 8, 64) for i in range(2)]
        self.hd = []
        for hh in range(2):
            sa, sb_ = ((1, 3), (7, 8))[hh]
            d = {}
            d["am"] = b3(sa, 128, 0, 5, 128)
            d["utm"] = b3(sa, 128, 640, 4, 128)
            d["pt"] = sl.bf(sa, cols=128, off=1152)
            d["pt2"] = sl.bf(sa, cols=128, off=1280)
            d["ut"] = sl.bf(sa, cols=128, off=1408)
            d["xa"] = b3(sa, 128, 1536, 2, 128)
            d["xb"] = b3(sa, 128, 1792, 2, 128)
            d["w0"] = sl.bf(sa, cols=64, off=2048)
            d["vm"] = b3(sa, 128, 2112, 4, 64)
            d["qtm"] = v3(sb_, 64, 0, 4, 128)
            d["gc"] = v3(sb_, 64, 512, 4, 64)
            d["mt"] = v3(sb_, 64, 768, 4, 64)
            self.hd.append(d)
        self.ysb = s("ysb", [128, 128])
        self.ysq = s("ysq", [128, 128])
        self.st = {n: s("st_" + n, [128, 2]) for n in ("s1", "s2", "m", "msq", "var", "sd", "rstd")}
        self.yn = s("yn", [128, 128])
        self.yo = s("yo", [128, 128])


PRM_NAMES = ["mu_r", "mu_k", "mu_v", "w0", "a0", "v0", "k_k", "k_a", "r_k", "gn_g", "gn_b", "negw0", "omka"]


def rwkv_stage(p, cx, rr, projT, proj_t, prm_d, ysT, ys_t, row0, vfT, vf_t, has_vres, gs=range(8), dbg=None, hook=None):
    sl = rr.sl
    pb = cx.pb
    S_ = sl.f32
    ext_t = Tile(None, "EXT")
    ROUND_F32R[0] = USE_F32R

    def dump(name, view, sl_):
        if dbg is not None and name in dbg:
            p.dma('sp', dbg.pop(name), view[sl_], reads=[view], writes=[ext_t], is_output=True)
    PI = {n: i for i, n in enumerate(PRM_NAMES)}

    def prm(n, g):
        return rr.prm[:, PI[n], g:g + 1]

    nsl = lambda ap: ap.rearrange("(g p) -> p g", p=128)
    for n, src in (("mu_r", prm_d["mu"][0:1024]), ("mu_k", prm_d["mu"][1024:2048]), ("mu_v", prm_d["mu"][2048:3072]),
                   ("w0", prm_d["w0"]), ("a0", prm_d["a0"]), ("v0", prm_d["v0"]), ("k_k", prm_d["k_k"]),
                   ("k_a", prm_d["k_a"]), ("r_k", prm_d["r_k"]), ("gn_g", prm_d["gn_g"]), ("gn_b", prm_d["gn_b"])):
        p.dma('sp', rr.prm[:, PI[n], :], nsl(src), writes=[rr.prm], allow_slow_non_contiguous=True)
    p.op('dve', I('tensor_scalar_mul', rr.prm[:, PI["negw0"], :], rr.prm[:, PI["w0"], :], -1.0),
         reads=[rr.prm], writes=[rr.prm])
    p.op('dve', I('tensor_scalar', rr.prm[:, PI["omka"], :], rr.prm[:, PI["k_a"], :], -1.0, 1.0,
                                          ALU.mult, ALU.add), reads=[rr.prm], writes=[rr.prm])
    p.dma('pool' if USE_F32R else 'sp', RV(rr.lw1[0:64, :]), prm_d["w_lora"], writes=[rr.lw1])
    p.dma('pool' if USE_F32R else 'sp', RV(rr.lw1[64:128, :]), prm_d["a_lora"], writes=[rr.lw1])
    p.dma('pool' if USE_F32R else 'sp', RV(rr.lw2[:, :]), prm_d["g_lora"][0:128, :], writes=[rr.lw2])
    p.dma('pool' if USE_F32R else 'sp', RV(rr.lw3[0:32, :]), prm_d["g_lora"][128:160, :], writes=[rr.lw3])
    p.dma('pool' if USE_F32R else 'sp', RV(rr.lw3[32:64, :]), prm_d["v_lora"], writes=[rr.lw3])
    mul = rr.mul
    p.dma('sp', mul[:, 0:1], prm_d["mu"][3072:3200].rearrange("(p o) -> p o", o=1), writes=[mul])
    p.dma('sp', mul[:, 1:2], prm_d["mu"][3200:3328].rearrange("(p o) -> p o", o=1), writes=[mul])
    p.dma('sp', mul[0:32, 2:3], prm_d["mu"][3328:3360].rearrange("(p o) -> p o", o=1), writes=[mul])
    p.dma('sp', mul[32:64, 2:3], prm_d["mu_v"].rearrange("(p o) -> p o", o=1), writes=[mul])

    T1, T2, T3 = S_(3), S_(4), S_(18)

    def shift_load(dst, r0, nrow, mu_ap):
        H, Pv = T1, T2
        p.dma('sp', H[0:nrow, :], projT[r0:r0 + nrow, :], reads=[proj_t], writes=[H])
        p.dma('act', Pv[0:nrow, 1:S], projT[r0:r0 + nrow, 0:S - 1], reads=[proj_t], writes=[Pv])
        p.op('pool', I('memset', Pv[0:nrow, 0:1], 0.0), writes=[Pv])
        p.op('pool', I('tensor_tensor', Pv[0:nrow, :], Pv[0:nrow, :], H[0:nrow, :], ALU.subtract),
             reads=[Pv, H], writes=[Pv])
        p.op('dve', I('scalar_tensor_tensor', dst[0:nrow, :], Pv[0:nrow, :], mu_ap, H[0:nrow, :],
                                                     ALU.mult, ALU.add), reads=[Pv, H, mul, rr.prm], writes=[dst])

    XI1, XI2, XI3 = S_(14), S_(15), S_(16)
    shift_load(XI1, ROW_LORA, 128, mul[:, 0:1])
    p.op('act', I('activation', XI1[0:64, :], XI1[0:64, :], AF.Tanh), reads=[XI1], writes=[XI1])
    shift_load(XI2, ROW_LORA + 128, 128, mul[:, 1:2])
    p.op('act', I('activation', XI2[:, :], XI2[:, :], AF.Sigmoid), reads=[XI2], writes=[XI2])
    shift_load(XI3, ROW_LORA + 256, 64, mul[0:64, 2:3])
    p.op('act', I('activation', XI3[0:32, :], XI3[0:32, :], AF.Sigmoid), reads=[XI3], writes=[XI3])

    R, K, V, A, KKN, KM, CUM, U = S_(0), S_(1), S_(2), S_(7), S_(8), S_(9), S_(10), S_(11)
    PINV, PPREV, DE, G, BON, YB = S_(4), S_(5), S_(6), S_(12), S_(13), S_(17)
    Pt = S_(3)
    bk = [0]

    def lora_mm(dst_fn, parts, g):
        for ch in range(4):
            ps = pb[2 + bk[0] % 6]
            bk[0] += 1
            cs = slice(ch * 512, (ch + 1) * 512)
            for i, (lw, xi, r0, r1) in enumerate(parts):
                p.op('pe', mm(ps[:, :], lw[r0:r1, g * 128:(g + 1) * 128], xi[r0:r1, cs], i == 0, i == len(parts) - 1, r=True),
                     reads=[lw, xi], writes=[ps])
            dst_fn(ps, cs)

    for g in gs:
        if hook is not None:
            hook()
        fr = ROW_RWKV + g * 128
        shift_load(R, fr, 128, prm("mu_r", g))
        shift_load(K, fr + 1024, 128, prm("mu_k", g))
        shift_load(V, fr + 2048, 128, prm("mu_v", g))
        if not has_vres:
            p.dma('sp', vfT[g * 128:(g + 1) * 128, :], V[:, :], reads=[V], writes=[vf_t])
        lora_mm(lambda ps, cs: p.op('act', I('activation', U[:, cs], ps[:, :], AF.Identity,
                                                                  bias=prm("negw0", g), scale=-1.0),
                                    reads=[ps, rr.prm], writes=[U]), [(rr.lw1, XI1, 0, 64)], g)
        p.op('act', I('activation', T1[:, :], U[:, :], AF.Abs), reads=[U], writes=[T1])
        p.op('act', I('activation', T1[:, :], T1[:, :], AF.Exp, scale=-1.0), reads=[T1], writes=[T1])
        p.op('act', I('activation', T1[:, :], T1[:, :], AF.Ln, bias=rr.cst[:, 2:3]), reads=[T1, rr.cst], writes=[T1])
        p.op('dve', I('scalar_tensor_tensor', T1[:, :], U[:, :], 0.0, T1[:, :], ALU.max, ALU.add),
             reads=[U, T1], writes=[T1])
        p.op('act', I('activation', T1[:, :], T1[:, :], AF.Exp, bias=rr.cst[:, 0:1], scale=-1.0),
             reads=[T1, rr.cst], writes=[T1])
        p.op('dve', I('tensor_scalar_mul', U[:, :], T1[:, :], -1.0), reads=[T1], writes=[U])
        src = U
        bufs = [CUM, T1]
        bi = 0
        for sft in (1, 2, 4, 8, 16):
            dst = bufs[bi % 2]
            bi += 1
            s3 = src[:, :].rearrange("p (c k) -> p c k", k=32)
            d3 = dst[:, :].rearrange("p (c k) -> p c k", k=32)
            p.op('dve', I('tensor_tensor', d3[:, :, sft:], s3[:, :, sft:], s3[:, :, :32 - sft], ALU.add),
                 reads=[src], writes=[dst])
            p.op('pool', I('tensor_copy', d3[:, :, :sft], s3[:, :, :sft]),
                 reads=[src], writes=[dst])
            src = dst
        c3 = CUM[:, :].rearrange("p (c k) -> p c k", k=32)
        p.op('dve', I('tensor_copy', rr.cumc[:, :], c3[:, :, 31]), reads=[CUM], writes=[rr.cumc])
        p.op('act', I('activation', rr.pc[:, :], rr.cumc[:, :], AF.Exp), reads=[rr.cumc], writes=[rr.pc])
        p.op('act', I('activation', Pt[:, :], CUM[:, :], AF.Exp), reads=[CUM], writes=[Pt])
        p.op('act', I('activation', PINV[:, :], CUM[:, :], AF.Exp, scale=-1.0), reads=[CUM], writes=[PINV])
        p.op('dve', I('tensor_tensor', PPREV[:, :], CUM[:, :], U[:, :], ALU.subtract), reads=[CUM, U], writes=[PPREV])
        p.op('act', I('activation', PPREV[:, :], PPREV[:, :], AF.Exp), reads=[PPREV], writes=[PPREV])
        d3 = DE[:, :].rearrange("p (c k) -> p c k", k=32)
        p.op('pool', I('tensor_tensor', d3, rr.cumc[:, :].unsqueeze(2).to_broadcast([128, 64, 32]), c3, ALU.subtract),
             reads=[rr.cumc, CUM], writes=[DE])
        p.op('act', I('activation', DE[:, :], DE[:, :], AF.Exp), reads=[DE], writes=[DE])
        lora_mm(lambda ps, cs: p.op('act', I('activation', A[:, cs], ps[:, :], AF.Sigmoid, bias=prm("a0", g)),
                                    reads=[ps, rr.prm], writes=[A]), [(rr.lw1, XI1, 64, 128)], g)
        lora_mm(lambda ps, cs: p.op('act', I('copy', G[:, cs], ps[:, :]), reads=[ps], writes=[G]),
                [(rr.lw2, XI2, 0, 128), (rr.lw3, XI3, 0, 32)], g)
        if has_vres:
            lora_mm(lambda ps, cs: p.op('act', I('activation', T3[:, cs], ps[:, :], AF.Sigmoid, bias=prm("v0", g)),
                                        reads=[ps, rr.prm], writes=[T3]), [(rr.lw3, XI3, 32, 64)], g)
            p.dma('sp', U[:, :], vfT[g * 128:(g + 1) * 128, :], reads=[vf_t], writes=[U])
            p.op('dve', I('tensor_tensor', U[:, :], U[:, :], V[:, :], ALU.subtract), reads=[U, V], writes=[U])
            p.op('pool', I('tensor_tensor', U[:, :], U[:, :], T3[:, :], ALU.mult), reads=[U, T3], writes=[U])
            p.op('dve', I('tensor_tensor', V[:, :], V[:, :], U[:, :], ALU.add), reads=[U, V], writes=[V])
        p.op('dve', I('tensor_scalar_mul', KKN[:, :], K[:, :], prm("k_k", g)), reads=[K, rr.prm], writes=[KKN])
        p.op('act', I('activation', T3[:, :], KKN[:, :], AF.Square), reads=[KKN], writes=[T3])
        for ch in range(4):
            ps = pb[2 + bk[0] % 6]
            bk[0] += 1
            cs = slice(ch * 512, (ch + 1) * 512)
            p.op('pe', mm(ps[:, :], rr.blk[:, :], T3[:, cs], True, True, r=True), reads=[rr.blk, T3], writes=[ps])
            p.op('act', I('activation', U[:, cs], ps[:, :], AF.Sqrt), reads=[ps], writes=[U])
        p.op('dve', I('tensor_scalar_max', U[:, :], U[:, :], 1e-12), reads=[U], writes=[U])
        p.op('dve', I('reciprocal', U[:, :], U[:, :]), reads=[U], writes=[U])
        p.op('pool', I('tensor_tensor', KKN[:, :], KKN[:, :], U[:, :], ALU.mult), reads=[KKN, U], writes=[KKN])
        p.op('dve', I('tensor_scalar', T3[:, :], A[:, :], prm("k_a", g), prm("omka", g), ALU.mult, ALU.add),
             reads=[A, rr.prm], writes=[T3])
        p.op('pool', I('tensor_tensor', KM[:, :], K[:, :], T3[:, :], ALU.mult), reads=[K, T3], writes=[KM])
        p.op('dve', I('scalar_tensor_tensor', T3[:, :], R[:, :], prm("r_k", g), KM[:, :], ALU.mult, ALU.mult),
             reads=[R, KM, rr.prm], writes=[T3])
        for ch in range(4):
            ps = pb[2 + bk[0] % 6]
            bk[0] += 1
            cs = slice(ch * 512, (ch + 1) * 512)
            p.op('pe', mm(ps[:, :], rr.blk[:, :], T3[:, cs], True, True, r=True), reads=[rr.blk, T3], writes=[ps])
            p.op('dve', I('tensor_tensor', BON[:, cs], ps[:, :], V[:, cs], ALU.mult),
                 reads=[ps, V], writes=[BON])
        RB, AB, KT, BT, BHT, KHT = R, PPREV, U, PINV, DE, CUM
        p.op('pool', I('tensor_tensor', RB[:, :], R[:, :], Pt[:, :], ALU.mult), reads=[R, Pt], writes=[RB])
        p.op('dve', I('scalar_tensor_tensor', AB[:, :], KKN[:, :], -1.0, PPREV[:, :], ALU.mult, ALU.mult),
             reads=[KKN, PPREV], writes=[AB])
        p.op('pool', I('tensor_tensor', KT[:, :], KM[:, :], PINV[:, :], ALU.mult), reads=[KM, PINV], writes=[KT])
        p.op('dve', I('tensor_tensor', T3[:, :], KKN[:, :], A[:, :], ALU.mult), reads=[KKN, A], writes=[T3])
        p.op('pool', I('tensor_tensor', BT[:, :], T3[:, :], PINV[:, :], ALU.mult), reads=[T3, PINV], writes=[BT])
        p.op('dve', I('tensor_tensor', KHT[:, :], KM[:, :], DE[:, :], ALU.mult), reads=[KM, DE], writes=[KHT])
        p.op('pool', I('tensor_tensor', BHT[:, :], T3[:, :], DE[:, :], ALU.mult), reads=[T3, DE], writes=[BHT])
        for nm_, vw_ in (("RB", RB), ("AB", AB), ("KT", KT), ("BT", BT), ("BHT", BHT), ("KHT", KHT), ("V", V), ("G", G), ("BON", BON)):
            dump(nm_, vw_, (slice(None), slice(None)))
        dump("PC", rr.pc, (slice(None), slice(None)))
        for hh in range(2):
            p.op('pool', I('memset', rr.z[hh][:, 0, :], 0.0), writes=[rr.z[hh]])

        for tb in range(16):
            ts_ = slice(tb * 128, (tb + 1) * 128)
            tm = rr.tm[tb % 2]
            dg = rr.dg[tb % 2]
            for i, X in enumerate((V, AB, BHT, KHT, RB)):
                bank, c0 = (pb[0], i * 128) if i < 4 else (pb[1], 0)
                p.op('pe', I('transpose', bank[:, c0:c0 + 128], X[:, ts_], cx.ident[:, :]),
                     reads=[X, cx.ident], writes=[bank])
            p.op('act', I('copy', tm[:, 0:4, :], pb[0][:, :].rearrange("p (a b) -> p a b", b=128)),
                 reads=[pb[0]], writes=[tm])
            p.op('dve', I('tensor_copy', tm[:, 4, :], pb[1][:, 0:128]), reads=[pb[1]], writes=[tm])
            p.op('pool', I('tensor_tensor',
                dg[:, :, :], cx.ident[:, :].unsqueeze(1).to_broadcast([128, 4, 128]),
                rr.pc[:, tb * 4:(tb + 1) * 4].unsqueeze(2).to_broadcast([128, 4, 128]), ALU.mult),
                reads=[cx.ident, rr.pc], writes=[dg])
            H = []
            for hh in range(2):
                b = 64 * hh
                d = rr.hd[hh]
                H.append(dict(d=d, b=b, fs=slice(b, b + 64), hb=[pb[2 + 3 * hh], pb[3 + 3 * hh], pb[4 + 3 * hh]]))

            def fm(X, h):
                return X[h["fs"], ts_]

            for h in H:
                hb0, hb1 = h["hb"][0], h["hb"][1]
                pairs = ((BT, AB), (AB, BT), (KT, AB), (BT, RB), (KT, RB))
                for i, (L, Rr) in enumerate(pairs):
                    dstp = hb0[:, i * 128:(i + 1) * 128] if i < 4 else hb1[:, 0:128]
                    p.op('pe', mm(dstp, fm(L, h), fm(Rr, h), True, True, r=True), reads=[L, Rr], writes=[hb0 if i < 4 else hb1])
                am = h["d"]["am"]
                p.op('dve', I('tensor_tensor',
                    am[:, 0:4, :], hb0[:, :].rearrange("p (a b) -> p a b", b=128),
                    rr.mask5[:, 0:512].rearrange("p (a b) -> p a b", b=128), ALU.mult),
                    reads=[hb0, rr.mask5], writes=[am])
                p.op('dve', I('tensor_tensor', am[:, 4, :], hb1[:, 0:128], rr.mask5[:, 512:640], ALU.mult),
                     reads=[hb1, rr.mask5], writes=[am])
                p.op('dve', I('tensor_tensor', h["d"]["pt"][:, :], am[:, 0, :], cx.ident[:, :], ALU.add),
                     reads=[am, cx.ident], writes=[h["d"]["pt"]])
                h["xt"], h["x"] = am[:, 0, :], am[:, 1, :]
                h["xt_t"] = h["x_t"] = am
                h["pt"], h["pt_o"] = h["d"]["pt"], h["d"]["pt2"]
            for lvl in range(1, 5):
                for h in H:
                    hb1 = h["hb"][1]
                    xn_t = h["d"]["xa"] if lvl % 2 == 1 else h["d"]["xb"]
                    p.op('pe', mm(hb1[:, 128:256], h["xt"], h["x"], True, True, r=True), reads=[h["x_t"]], writes=[hb1])
                    if lvl < 4:
                        p.op('pe', mm(hb1[:, 256:384], h["x"], h["xt"], True, True, r=True), reads=[h["x_t"]], writes=[hb1])
                        p.op('act', I('copy',
                            xn_t[:, :, :], hb1[:, 128:384].rearrange("p (a b) -> p a b", b=128)), reads=[hb1], writes=[xn_t])
                    else:
                        p.op('act', I('copy', xn_t[:, 0, :], hb1[:, 128:256]),
                             reads=[hb1], writes=[xn_t])
                    h["x"], h["xt"], h["x_t"] = xn_t[:, 0, :], xn_t[:, 1, :], xn_t
                for h in H:
                    hb1 = h["hb"][1]
                    p.op('pe', mm(hb1[:, 384:512], h["x"], h["pt"][:, :], True, True, r=True), reads=[h["x_t"], h["pt"]], writes=[hb1])
                    p.op('dve', I('tensor_tensor', h["pt_o"][:, :], hb1[:, 384:512], h["pt"][:, :], ALU.add),
                         reads=[hb1, h["pt"]], writes=[h["pt_o"]])
                    h["pt"], h["pt_o"] = h["pt_o"], h["pt"]
            for h in H:
                d, hb2, fsl = h["d"], h["hb"][2], slice(h["b"], h["b"] + 64)
                p.op('pe', mm(hb2[:, 0:64], d["am"][:, 2, :], tm[:, 0, fsl], True, True, r=True), reads=[d["am"], tm], writes=[hb2])
                p.op('act', I('copy', d["w0"][:, :], hb2[:, 0:64]), reads=[hb2], writes=[d["w0"]])
            for h in H:
                d, hb2, fsl = h["d"], h["hb"][2], slice(h["b"], h["b"] + 64)
                p.op('pe', mm(hb2[:, 64:128], h["pt"][:, :], d["w0"][:, :], True, True, r=True), reads=[h["pt"], d["w0"]], writes=[hb2])
                p.op('pe', mm(hb2[:, 128:192], h["pt"][:, :], tm[:, 1, fsl], True, True, r=True), reads=[h["pt"], tm], writes=[hb2])
                p.op('act', I('copy', d["ut"][:, :], hb2[:, 64:192]), reads=[hb2], writes=[d["ut"]])
                p.op('dve', I('tensor_tensor',
                    d["utm"][:, :, :], d["ut"][:, :].unsqueeze(1).to_broadcast([128, 4, 128]),
                    rr.cmask[:, :].unsqueeze(2).to_broadcast([128, 4, 128]), ALU.mult),
                    reads=[d["ut"], rr.cmask], writes=[d["utm"]])
                p.op('pool', I('tensor_tensor',
                    d["vm"][:, :, :], tm[:, 0, fsl].unsqueeze(1).to_broadcast([128, 4, 64]),
                    rr.cmask[:, :].unsqueeze(2).to_broadcast([128, 4, 64]), ALU.mult),
                    reads=[tm, rr.cmask], writes=[d["vm"]])
            for h in H:
                d, hb1, hb2, fsl = h["d"], h["hb"][1], h["hb"][2], slice(h["b"], h["b"] + 64)
                for c in range(4):
                    p.op('pe', mm(hb1[0:64, 128 + c * 64:192 + c * 64], tm[:, 2, fsl], d["utm"][:, c, 0:64], True, False, r=True),
                         reads=[tm, d["utm"]], writes=[hb1])
                    p.op('pe', mm(hb1[0:64, 128 + c * 64:192 + c * 64], tm[:, 3, fsl], d["vm"][:, c, :], False, True, r=True),
                         reads=[tm, d["vm"]], writes=[hb1])
                    p.op('pe', mm(hb2[0:64, 192 + c * 64:256 + c * 64], d["utm"][:, c, 64:128], tm[:, 2, fsl], True, False, r=True),
                         reads=[tm, d["utm"]], writes=[hb2])
                    p.op('pe', mm(hb2[0:64, 192 + c * 64:256 + c * 64], rr.ident_r[:, fsl], dg[:, c, fsl], False, True, r=True),
                         reads=[rr.ident_r, dg], writes=[hb2])
                p.op('act', I('copy', d["gc"][:, :, :], hb1[0:64, 128:384].rearrange("p (a b) -> p a b", b=64)),
                     reads=[hb1], writes=[d["gc"]])
                p.op('dve', I('tensor_copy', d["mt"][:, :, :], hb2[0:64, 192:448].rearrange("p (a b) -> p a b", b=64)),
                     reads=[hb2], writes=[d["mt"]])
            ybank = pb[1]
            for h in H:
                d, hb1, fsl = h["d"], h["hb"][1], slice(h["b"], h["b"] + 64)
                p.op('pe', mm(hb1[0:64, 384:512], tm[:, 4, fsl], rr.ident_bf[:, :], True, False), reads=[tm, rr.ident_bf], writes=[hb1])
                p.op('pe', mm(hb1[0:64, 384:512], d["ut"][:, 64:128], d["am"][:, 3, :], False, True, r=True), reads=[d["ut"], d["am"]], writes=[hb1])
                p.op('dve', I('tensor_tensor',
                    d["qtm"][:, :, :], hb1[0:64, 384:512].unsqueeze(1).to_broadcast([64, 4, 128]),
                    rr.tmask[:, :].rearrange("p (a b) -> p a b", b=128), ALU.mult), reads=[hb1, rr.tmask], writes=[d["qtm"]])
            yacc = pb[0]
            for hi, h in enumerate(H):
                d, fsl = h["d"], slice(h["b"], h["b"] + 64)
                yp = yacc[:, hi * 64:(hi + 1) * 64]
                p.op('pe', mm(yp, d["am"][:, 3, :], d["ut"][:, 0:64], hi == 0, False, r=True), reads=[d["am"], d["ut"]], writes=[yacc])
                p.op('pe', mm(yp, d["am"][:, 4, :], tm[:, 0, fsl], False, False, r=True), reads=[d["am"], tm], writes=[yacc])
            if tb == 0:
                d0 = H[0]["d"]
                dump("tm", tm, (slice(None), slice(None), slice(None)))
                dump("am", d0["am"], (slice(None), slice(None), slice(None)))
                dump("pt", H[0]["pt"], (slice(None), slice(None)))
                dump("ut", d0["ut"], (slice(None), slice(None)))
                dump("gc", d0["gc"], (slice(None), slice(None), slice(None)))
                dump("mt", d0["mt"], (slice(None), slice(None), slice(None)))
                dump("qtm", d0["qtm"], (slice(None), slice(None), slice(None)))
            for c in range(4):
                sidx = tb * 4 + c
                for hi, h in enumerate(H):
                    d = h["d"]
                    z = rr.z[hi]
                    zp = ybank[0:64, 256 + hi * 64:320 + hi * 64]
                    p.op('pe', mm(zp, d["mt"][:, c, :], z[:, sidx % 8, :], True, True, r=True), reads=[d["mt"], z], writes=[ybank])
                    p.op('dve', I('tensor_tensor',
                        z[:, (sidx + 1) % 8, :], zp, d["gc"][:, c, :], ALU.add), reads=[ybank, d["gc"]], writes=[z])
            for hi, h in enumerate(H):
                d = h["d"]
                z = rr.z[hi]
                yp = yacc[:, hi * 64:(hi + 1) * 64]
                for c in range(4):
                    sidx = tb * 4 + c
                    p.op('pe', mm(yp, d["qtm"][:, c, :], z[:, sidx % 8, :], False, c == 3, r=True), reads=[d["qtm"], z], writes=[yacc])
            y3 = rr.ysb[:, :].rearrange("p (a b) -> p a b", b=64)
            q3 = rr.ysq[:, :].rearrange("p (a b) -> p a b", b=64)
            st = rr.st
            p.op('act', I('copy', rr.ysb[:, :], yacc[:, 0:128]), reads=[yacc], writes=[rr.ysb])
            p.op('act', I('activation', rr.ysq[:, :], rr.ysb[:, :], AF.Square), reads=[rr.ysb], writes=[rr.ysq])
            if tb == 0:
                dump("ysb", rr.ysb, (slice(None), slice(None)))
            p.op('dve', I('reduce_sum', st["s1"][:, :], y3, AX.X), reads=[rr.ysb], writes=[st["s1"]])
            p.op('dve', I('reduce_sum', st["s2"][:, :], q3, AX.X), reads=[rr.ysq], writes=[st["s2"]])
            p.op('dve', I('tensor_scalar_mul', st["m"][:, :], st["s1"][:, :], 1.0 / 64), reads=[st["s1"]], writes=[st["m"]])
            p.op('dve', I('tensor_tensor', st["msq"][:, :], st["m"][:, :], st["m"][:, :], ALU.mult), reads=[st["m"]], writes=[st["msq"]])
            p.op('dve', I('scalar_tensor_tensor', st["var"][:, :], st["s2"][:, :], 1.0 / 64, st["msq"][:, :], ALU.mult, ALU.subtract),
                 reads=[st["s2"], st["msq"]], writes=[st["var"]])
            p.op('act', I('activation', st["sd"][:, :], st["var"][:, :], AF.Sqrt, bias=rr.cst[:, 1:2]),
                 reads=[st["var"], rr.cst], writes=[st["sd"]])
            p.op('dve', I('reciprocal', st["rstd"][:, :], st["sd"][:, :]), reads=[st["sd"]], writes=[st["rstd"]])
            n3 = rr.yn[:, :].rearrange("p (a b) -> p a b", b=64)
            p.op('dve', I('tensor_tensor', n3, y3, st["m"][:, :].unsqueeze(2).to_broadcast([128, 2, 64]), ALU.subtract),
                 reads=[rr.ysb, st["m"]], writes=[rr.yn])
            p.op('dve', I('tensor_tensor', n3, n3, st["rstd"][:, :].unsqueeze(2).to_broadcast([128, 2, 64]), ALU.mult),
                 reads=[rr.yn, st["rstd"]], writes=[rr.yn])
            p.op('pe', I('transpose', ybank[:, 384:512], rr.yn[:, :], cx.ident[:, :]), reads=[rr.yn, cx.ident], writes=[ybank])
            p.op('act', I('activation', rr.yo[:, :], ybank[:, 384:512], AF.Identity, bias=prm("gn_b", g), scale=prm("gn_g", g)),
                 reads=[ybank, rr.prm], writes=[rr.yo])
            p.op('dve', I('tensor_tensor', rr.yo[:, :], rr.yo[:, :], BON[:, ts_], ALU.add), reads=[rr.yo, BON], writes=[rr.yo])
            p.op('dve', I('tensor_tensor', YB[:, ts_], rr.yo[:, :], G[:, ts_], ALU.mult), reads=[rr.yo, G], writes=[YB])
        p.dma('sp', ysT[row0 + g * 128:row0 + (g + 1) * 128, :], YB[:, :], reads=[YB], writes=[ys_t],
              is_output=(ys_t.name == "EXT"))
    ROUND_F32R[0] = False


def wview(w_ap):
    return w_ap.rearrange("(c p) n -> p c n", p=128)


def bfv(sl, i0, nslab, a, b_):
    per = 4096 // b_
    out = []
    for j in range(nslab):
        vw = sl.bf(i0 + j, cols=per * b_)
        out.append(vw.derive(vw.ap.rearrange("p (a b) -> p a b", b=b_)))
    return out, per


class KView:
    def __init__(self, sl, i0, KC, T):
        self.views, self.per = bfv(sl, i0, (KC * T + 4095) // 4096, KC, T)
        self.KC = KC

    def k(self, k):
        v = self.views[k // self.per]
        return v, v[:, k % self.per, :]

    def tiles(self):
        return self.views


def load_kview(p, kv, src_T, c0, T, src_t, q='pool'):
    sv = src_T.rearrange("(c p) t -> p c t", p=128)
    for j, v in enumerate(kv.views):
        k0 = j * kv.per
        k1 = min(kv.KC, k0 + kv.per)
        p.dma(q, v[:, 0:k1 - k0, :], sv[:, k0:k1, c0:c0 + T], reads=[src_t], writes=[v])


def proj_stage(p, cx, sl, xT, xT_t, w, projT, proj_t, v_tm, vtm_t):
    pb = cx.pb
    KC = 16
    wv = wview(w)
    pi = 0
    for tt in range(2):
        t0 = tt * 1024
        xs = KView(sl, 0, KC, 1024)
        load_kview(p, xs, xT, t0, 1024, xT_t)
        wb = []
        for i in range(3):
            a, b_ = sl.bf(4 + 2 * i), sl.bf(5 + 2 * i)
            wb.append([a.derive(a.ap.rearrange("p (c n) -> p c n", n=512)),
                       b_.derive(b_.ap.rearrange("p (c n) -> p c n", n=512))])
        ob = [sl.f32(10, cols=1024, off=0), sl.f32(10, cols=1024, off=1024), sl.f32(11, cols=1024, off=0)]
        oi = 0
        for nb in range(NCAT // 512):
            wt = wb[nb % 3]
            for hf in range(2):
                p.dma('pool', wt[hf][:, :, :], wv[:, hf * 8:(hf + 1) * 8, nb * 512:(nb + 1) * 512], writes=[wt[hf]])
            if nb in (4, 5):
                for tb in range(8):
                    ps = pb[pi % 6]
                    pi += 1
                    for k in range(KC):
                        xv, xa = xs.k(k)
                        p.op('pe', mm(ps[:, :], xa[:, tb * 128:(tb + 1) * 128], wt[k // 8][:, k % 8, :], k == 0, k == KC - 1),
                             reads=[xv, wt[k // 8]], writes=[ps])
                    o = ob[oi % 3]
                    oi += 1
                    p.op('act', I('copy', o[:, 0:512], ps[:, :]), reads=[ps], writes=[o])
                    p.dma('sp', v_tm[t0 + tb * 128:t0 + (tb + 1) * 128, (nb - 4) * 512:(nb - 3) * 512], o[:, 0:512],
                          reads=[o], writes=[vtm_t])
                continue
            for j in range(4):
                n = nb * 4 + j
                o = ob[oi % 3]
                oi += 1
                for th in range(2):
                    ps = pb[pi % 6]
                    pi += 1
                    for k in range(KC):
                        xv, xa = xs.k(k)
                        p.op('pe', mm(ps[:, :], wt[k // 8][:, k % 8, j * 128:(j + 1) * 128], xa[:, th * 512:(th + 1) * 512],
                                      k == 0, k == KC - 1), reads=[xv, wt[k // 8]], writes=[ps])
                    osl = o[:, th * 512:(th + 1) * 512]
                    if n >= GATE_CHUNK0:
                        p.op('act', I('activation', osl, ps[:, :], AF.Sigmoid), reads=[ps], writes=[o])
                    elif th == 0:
                        p.op('act', I('copy', osl, ps[:, :]), reads=[ps], writes=[o])
                    else:
                        p.op('dve', I('tensor_copy', osl, ps[:, :]), reads=[ps], writes=[o])
                p.dma('sp', projT[n * 128:(n + 1) * 128, t0:t0 + 1024], o[:, :], reads=[o], writes=[proj_t])


class LnRes:
    def __init__(self, p):
        self.st = {n: p.sb([128, 1], F32, "ln_" + n) for n in ("s1", "s2", "m", "msq", "var", "sd", "rstd")}
        self.eps = p.sb([128, 1], F32, "ln_eps")
        p.op('pool', I('memset', self.eps[:, :], LN_EPS), writes=[self.eps])


def resid_ln(p, cx, sl, lr, f_tm, f_t, x_tm, x_t, g_d, b_d, out_tm, out_t, outT, outT_t, final=False):
    pb = cx.pb
    G, Bv = sl.f32(0), sl.f32(1)
    p.dma('sp', G[:, :], g_d.partition_broadcast(128), writes=[G])
    p.dma('sp', Bv[:, :], b_d.partition_broadcast(128), writes=[Bv])
    st = lr.st
    for tb in range(S // 128):
        rs_ = slice(tb * 128, (tb + 1) * 128)
        X, F, Y, Q = sl.f32(2 + (tb % 2)), sl.f32(4 + (tb % 2)), sl.f32(6 + (tb % 2)), sl.f32(8)
        p.dma('sp', X[:, :], x_tm[rs_, :], reads=[x_t], writes=[X])
        p.dma('act', F[:, :], f_tm[rs_, :], reads=[f_t], writes=[F])
        p.op('dve', I('scalar_tensor_tensor', Y[:, :], X[:, :], ALPHA, F[:, :], ALU.mult, ALU.add), reads=[X, F], writes=[Y])
        p.op('dve', I('reduce_sum', st["s1"][:, :], Y[:, :], AX.X), reads=[Y], writes=[st["s1"]])
        p.op('act', I('activation', Q[:, :], Y[:, :], AF.Square), reads=[Y], writes=[Q])
        p.op('dve', I('reduce_sum', st["s2"][:, :], Q[:, :], AX.X), reads=[Q], writes=[st["s2"]])
        p.op('dve', I('tensor_scalar_mul', st["m"][:, :], st["s1"][:, :], 1.0 / D), reads=[st["s1"]], writes=[st["m"]])
        p.op('dve', I('tensor_tensor', st["msq"][:, :], st["m"][:, :], st["m"][:, :], ALU.mult), reads=[st["m"]], writes=[st["msq"]])
        p.op('dve', I('scalar_tensor_tensor', st["var"][:, :], st["s2"][:, :], 1.0 / D, st["msq"][:, :], ALU.mult, ALU.subtract),
             reads=[st["s2"], st["msq"]], writes=[st["var"]])
        p.op('act', I('activation', st["sd"][:, :], st["var"][:, :], AF.Sqrt, bias=lr.eps[:, 0:1]), reads=[st["var"], lr.eps], writes=[st["sd"]])
        p.op('dve', I('reciprocal', st["rstd"][:, :], st["sd"][:, :]), reads=[st["sd"]], writes=[st["rstd"]])
        p.op('dve', I('tensor_scalar', Y[:, :], Y[:, :], st["m"][:, 0:1], st["rstd"][:, 0:1], ALU.subtract, ALU.mult),
             reads=[Y, st["m"], st["rstd"]], writes=[Y])
        p.op('pool', I('tensor_tensor', Y[:, :], Y[:, :], G[:, :], ALU.mult), reads=[Y, G], writes=[Y])
        p.op('dve', I('tensor_tensor', Y[:, :], Y[:, :], Bv[:, :], ALU.add), reads=[Y, Bv], writes=[Y])
        p.dma('sp', out_tm[rs_, :], Y[:, :], reads=[Y], writes=[out_t], is_output=final)
        if outT is not None:
            OT = sl.f32(9 + (tb % 2))
            o3 = OT[:, :].rearrange("p (c t) -> p c t", t=128)
            for c4 in range(4):
                ps = pb[c4 % 4]
                for j in range(4):
                    c = c4 * 4 + j
                    p.op('pe', I('transpose', ps[:, j * 128:(j + 1) * 128], Y[:, c * 128:(c + 1) * 128], cx.ident[:, :]),
                         reads=[Y, cx.ident], writes=[ps])
                p.op('act' if c4 % 2 == 0 else 'dve', I('copy' if c4 % 2 == 0 else 'tensor_copy', o3[:, c4 * 4:(c4 + 1) * 4, :],
                                                        ps[:, :].rearrange("p (c t) -> p c t", t=128)), reads=[ps], writes=[OT])
            p.dma('act', outT.rearrange("(c p) t -> p c t", p=128)[:, :, rs_], o3, reads=[OT], writes=[outT_t])


def mix_stage(p, cx, sl, ysT, ys_t, projT, proj_t, bo, wo, f_tm, f_t):
    pb = cx.pb
    pi = 0
    for tt in range(2):
        t0 = tt * 1024
        ys = KView(sl, 0, 24, 1024)
        load_kview(p, ys, ysT, t0, 1024, ys_t)
        acc = KView(sl, 6, 16, 1024)
        for db in range(4):
            wts = []
            for n in range(3):
                vw = sl.bf(10 + (db % 2) * 3 + n)
                wt = vw.derive(vw.ap.rearrange("p (c n) -> p c n", n=512))
                p.dma('pool', wt[:, :, :], wview(bo[n])[:, :, db * 512:(db + 1) * 512], writes=[wt])
                wts.append(wt)
            for j in range(4):
                dc = db * 4 + j
                av, aa = acc.k(dc)
                for th in range(2):
                    cs = slice(th * 512, (th + 1) * 512)
                    gsl = sl.f32(16 + (pi % 2))
                    g3 = gsl.derive(gsl.ap[:, 0:1536].rearrange("p (n t) -> p n t", t=512))
                    tmp = gsl.derive(gsl.ap[:, 1536:2048])
                    for n in range(3):
                        r0 = ROW_GATE + n * 2048 + dc * 128
                        p.dma('sp', g3[:, n, :], projT[r0:r0 + 128, t0 + th * 512:t0 + (th + 1) * 512], reads=[proj_t], writes=[g3])
                    pss = []
                    for n in range(3):
                        ps = pb[pi % 8]
                        pi += 1
                        for k in range(8):
                            yv, ya = ys.k(n * 8 + k)
                            p.op('pe', mm(ps[:, :], wts[n][:, k, j * 128:(j + 1) * 128], ya[:, cs], k == 0, k == 7),
                                 reads=[wts[n], yv], writes=[ps])
                        pss.append(ps)
                    p.op('dve', I('tensor_tensor', tmp[:, :], pss[0][:, :], g3[:, 0, :], ALU.mult), reads=[pss[0], g3], writes=[tmp])
                    p.op('dve', I('tensor_tensor', g3[:, 1, :], pss[1][:, :], g3[:, 1, :], ALU.mult), reads=[pss[1], g3], writes=[g3])
                    p.op('dve', I('tensor_tensor', g3[:, 2, :], pss[2][:, :], g3[:, 2, :], ALU.mult), reads=[pss[2], g3], writes=[g3])
                    p.op('pool', I('tensor_tensor', tmp[:, :], tmp[:, :], g3[:, 1, :], ALU.add), reads=[tmp, g3], writes=[tmp])
                    p.op('dve', I('tensor_tensor', aa[:, cs], tmp[:, :], g3[:, 2, :], ALU.add), reads=[tmp, g3], writes=[av])
        wos = KView(sl, 10, 16, 2048)
        wov = wview(wo)
        for jx, v in enumerate(wos.views):
            k0 = jx * wos.per
            p.dma('pool', v[:, 0:wos.per, :], wov[:, k0:k0 + wos.per, :], writes=[v])
        for tb in range(8):
            o = sl.f32(18 + (tb % 2))
            for c4 in range(4):
                ps = pb[pi % 8]
                pi += 1
                for k in range(16):
                    av, aa = acc.k(k)
                    wv_, wa = wos.k(k)
                    p.op('pe', mm(ps[:, :], aa[:, tb * 128:(tb + 1) * 128], wa[:, c4 * 512:(c4 + 1) * 512], k == 0, k == 15),
                         reads=[av, wv_], writes=[ps])
                p.op('act' if c4 % 2 == 0 else 'dve', I('copy' if c4 % 2 == 0 else 'tensor_copy', o[:, c4 * 512:(c4 + 1) * 512], ps[:, :]),
                     reads=[ps], writes=[o])
            p.dma('sp', f_tm[t0 + tb * 128:t0 + (tb + 1) * 128, :], o[:, :], reads=[o], writes=[f_t])


class MoeRes:
    def __init__(self, p):
        self.lg = p.sb([128, 8], F32, "mo_lg")
        self.mx = p.sb([128, 8], F32, "mo_mx")
        self.sel = p.sb([128, 8], F32, "mo_sel")
        self.e = p.sb([128, 8], F32, "mo_e")
        self.d = p.sb([128, 1], F32, "mo_d")
        self.comb = p.sb([128, 4, 8], F32, "mo_comb")
        self.rt = p.sb([128, 16, 8], F32, "mo_rt")
        self.combT = p.sb([8, 512], F32, "mo_combT")
        self.selE = None


def ffn_stage(p, cx, sl, mo, xT, xT_t, experts, router, f_tm, f_t, bf_src=None):
    pb = cx.pb
    pi = 0
    moe = router is not None
    if moe:
        p.dma('sp', mo.rt[:, :, :], router.rearrange("(c p) e -> p c e", p=128), writes=[mo.rt])
        if mo.selE is None:
            mo.selE = cx.load_const("selE", [8, 1024], F32, tag="_mo")
    for tt in range(4):
        t0 = tt * 512
        xs = KView(sl, 0, 16, 512)
        load_kview(p, xs, xT, t0, 512, xT_t)
        hT = KView(sl, 2, 44, 512)
        accT = [sl.f32(16 + c // 4, cols=512, off=(c % 4) * 512) for c in range(16)]
        CB = sl.f32(20, cols=512, off=1024)
        STMP = [sl.f32(20, cols=512, off=1536), sl.f32(14, cols=512, off=0)]
        if moe:
            for tb in range(4):
                xf = sl.f32(19)
                x3 = xf.derive(xf.ap.rearrange("p (c t) -> p c t", t=128))
                p.dma('sp', x3[:, :, :], xT.rearrange("(c p) t -> p c t", p=128)[:, :, t0 + tb * 128:t0 + (tb + 1) * 128],
                      reads=[xT_t], writes=[x3])
                ps = pb[pi % 8]
                pi += 1
                for k in range(16):
                    p.op('pe', mm(ps[:, 0:8], x3[:, k, :], mo.rt[:, k, :], k == 0, k == 15), reads=[x3, mo.rt], writes=[ps])
                p.op('dve', I('tensor_copy', mo.lg[:, :], ps[:, 0:8]), reads=[ps], writes=[mo.lg])
                p.op('dve', I('max', mo.mx[:, :], mo.lg[:, :]), reads=[mo.lg], writes=[mo.mx])
                p.op('dve', I('tensor_scalar', mo.sel[:, :], mo.lg[:, :], mo.mx[:, 1:2], None, ALU.is_ge), reads=[mo.lg, mo.mx], writes=[mo.sel])
                p.op('dve', I('tensor_scalar', mo.e[:, :], mo.lg[:, :], mo.mx[:, 0:1], None, ALU.subtract), reads=[mo.lg, mo.mx], writes=[mo.e])
                p.op('act', I('activation', mo.e[:, :], mo.e[:, :], AF.Exp), reads=[mo.e], writes=[mo.e])
                p.op('dve', I('tensor_tensor', mo.d[:, :], mo.mx[:, 1:2], mo.mx[:, 0:1], ALU.subtract), reads=[mo.mx], writes=[mo.d])
                p.op('act', I('activation', mo.d[:, :], mo.d[:, :], AF.Exp), reads=[mo.d], writes=[mo.d])
                p.op('dve', I('tensor_scalar_add', mo.d[:, :], mo.d[:, :], 1.0), reads=[mo.d], writes=[mo.d])
                p.op('dve', I('reciprocal', mo.d[:, :], mo.d[:, :]), reads=[mo.d], writes=[mo.d])
                p.op('dve', I('tensor_tensor', mo.e[:, :], mo.e[:, :], mo.sel[:, :], ALU.mult), reads=[mo.e, mo.sel], writes=[mo.e])
                p.op('dve', I('tensor_scalar_mul', mo.comb[:, tb, :], mo.e[:, :], mo.d[:, 0:1]), reads=[mo.e, mo.d], writes=[mo.comb])
            pst = pb[pi % 8]
            pi += 1
            for tb in range(4):
                p.op('pe', I('transpose', pst[0:8, tb * 128:(tb + 1) * 128], mo.comb[:, tb, :], cx.ident[:, :]),
                     reads=[mo.comb, cx.ident], writes=[pst])
            p.op('dve', I('tensor_copy', mo.combT[0:8, :], pst[0:8, :]), reads=[pst], writes=[mo.combT])
        for ei, (wg, wu, wd) in enumerate(experts):
            wgv, wuv, wdv = wview(wg), wview(wu), wview(wd)
            for fb in range(DFF // 256):
                wts = []
                for wi, wvv in enumerate((wgv, wuv)):
                    vw = sl.bf(8 + (fb % 2) * 2 + wi)
                    wt = vw.derive(vw.ap.rearrange("p (c n) -> p c n", n=256))
                    if bf_src is None:
                        p.dma('pool', wt[:, :, :], wvv[:, :, fb * 256:(fb + 1) * 256], writes=[wt])
                    else:
                        p.dma('sp', wt[:, :, :], wvv[:, :, fb * 256:(fb + 1) * 256], reads=[bf_src], writes=[wt])
                    wts.append(wt)
                for j in range(2):
                    fc = fb * 2 + j
                    psg, psu = pb[pi % 8], pb[(pi + 1) % 8]
                    pi += 2
                    for wi, ps in enumerate((psg, psu)):
                        for k in range(16):
                            xv, xa = xs.k(k)
                            p.op('pe', mm(ps[:, :], wts[wi][:, k, j * 128:(j + 1) * 128], xa[:, :], k == 0, k == 15),
                                 reads=[wts[wi], xv], writes=[ps])
                    tmpv = sl.f32(20, cols=512, off=(fc % 2) * 512)
                    hv, ha = hT.k(fc)
                    p.op('act', I('activation', tmpv[:, :], psg[:, :], AF.Silu), reads=[psg], writes=[tmpv])
                    p.op('dve', I('tensor_tensor', ha[:, :], tmpv[:, :], psu[:, :], ALU.mult), reads=[tmpv, psu], writes=[hv])
            if moe:
                psb = pb[pi % 8]
                pi += 1
                p.op('pe', mm(psb[:, :], mo.selE[0:8, ei * 128:(ei + 1) * 128], mo.combT[0:8, :], True, True),
                     reads=[mo.selE, mo.combT], writes=[psb])
                p.op('act', I('copy', CB[:, :], psb[:, :]), reads=[psb], writes=[CB])
            for half in range(2):
                for k in range(44):
                    rb = sl.bf(12 + (k % 8) // 4, cols=1024, off=(k % 4) * 1024)
                    if bf_src is None:
                        p.dma('pool', rb[:, :], wd[k * 128:(k + 1) * 128, half * 1024:(half + 1) * 1024], writes=[rb])
                    else:
                        p.dma('sp', rb[:, :], wd[k * 128:(k + 1) * 128, half * 1024:(half + 1) * 1024],
                              reads=[bf_src], writes=[rb])
                    hv, ha = hT.k(k)
                    for j in range(8):
                        p.op('pe', mm(pb[j][:, :], rb[:, j * 128:(j + 1) * 128], ha[:, :], k == 0, k == 43),
                             reads=[rb, hv], writes=[pb[j]])
                for j in range(8):
                    ps = pb[j]
                    av = accT[half * 8 + j]
                    if not moe:
                        p.op('act' if j % 2 == 0 else 'dve', I('copy' if j % 2 == 0 else 'tensor_copy', av[:, :], ps[:, :]),
                             reads=[ps], writes=[av])
                    elif ei == 0:
                        p.op('dve', I('tensor_tensor', av[:, :], ps[:, :], CB[:, :], ALU.mult), reads=[ps, CB], writes=[av])
                    else:
                        st_ = STMP[j % 2]
                        p.op('dve', I('tensor_tensor', st_[:, :], ps[:, :], CB[:, :], ALU.mult), reads=[ps, CB], writes=[st_])
                        p.op('pool', I('tensor_tensor', av[:, :], av[:, :], st_[:, :], ALU.add), reads=[av, st_], writes=[av])
        for tb in range(4):
            o = sl.f32(8 + (tb % 2))
            for c4 in range(4):
                ps = pb[pi % 8]
                pi += 1
                for j in range(4):
                    c = c4 * 4 + j
                    p.op('pe', I('transpose', ps[:, j * 128:(j + 1) * 128], accT[c][:, tb * 128:(tb + 1) * 128], cx.ident[:, :]),
                         reads=[accT[c], cx.ident], writes=[ps])
                p.op('act' if c4 % 2 == 0 else 'dve', I('copy' if c4 % 2 == 0 else 'tensor_copy', o[:, c4 * 512:(c4 + 1) * 512], ps[:, :]),
                     reads=[ps], writes=[o])
            p.dma('sp', f_tm[t0 + tb * 128:t0 + (tb + 1) * 128, :], o[:, :], reads=[o], writes=[f_t])


N_ACTIVE = 4


def build_full():
    p = Prog()
    ct = const_tables()
    ct.update(rwkv_tables())
    cd = {n: p.dram("c_" + n, a.shape, F32, "ExternalInput") for n, a in ct.items()}
    dr = lambda n, sh, kind="ExternalInput": p.dram(n, sh, F32, kind)
    sc = lambda n, sh: p.nc.dram_tensor(n, list(sh), F32).ap()
    x_tm = dr("x_tm", [S, D])
    xT = dr("xT", [D, S])
    out = dr("out", [S, D], "ExternalOutput")
    L = []
    for l in range(2):
        d = {}
        d["wcat"] = dr(f"wcat{l}", [D, NCAT])
        for n, sh in dict(mu=[3360], w0=[1024], a0=[1024], k_k=[1024], k_a=[1024], r_k=[1024], gn_g=[1024], gn_b=[1024],
                          w_lora=[64, 1024], a_lora=[64, 1024], g_lora=[160, 1024]).items():
            d[n] = dr(f"rw{l}_{n}", sh)
        d["mu_v"] = dr(f"rw{l}_mu_v", [32])
        d["v0"] = dr(f"rw{l}_v0", [1024])
        d["v_lora"] = dr(f"rw{l}_v_lora", [32, 1024])
        d["qn"] = dr(f"qn{l}", [512])
        d["kvn"] = dr(f"kvn{l}", [512])
        d["w_uq"] = dr(f"w_uq{l}", [512, 1536])
        d["w_ukv"] = dr(f"w_ukv{l}", [512, 2048])
        d["bo"] = [dr(f"bo{l}_{n}", [1024, D]) for n in range(3)]
        d["wo"] = dr(f"wo{l}", [D, D])
        for n in ("ln1_g", "ln1_b", "ln2_g", "ln2_b"):
            d[n] = dr(f"{n}{l}", [D])
        L.append(d)
    ffn = (dr("ffn_wg", [D, DFF]), dr("ffn_wu", [D, DFF]), dr("ffn_wd", [DFF, D]))
    router = dr("router", [D, NE])
    moe = [(dr(f"moe_wg{e}", [D, DFF]), dr(f"moe_wu{e}", [D, DFF]), dr(f"moe_wd{e}", [DFF, D])) for e in range(NE)]
    projT, v_tm, ysT, vfT = sc("projT", [NCAT, S]), sc("v_tm", [S, 1024]), sc("ysT", [3072, S]), sc("vfT", [1024, S])
    f_tm = sc("f_tm", [S, D])
    scb = lambda n, sh: p.nc.dram_tensor(n, list(sh), BF16).ap()
    moe_bf = [(scb(f"moe_wg_bf{e}", [D, DFF]), scb(f"moe_wu_bf{e}", [D, DFF]), scb(f"moe_wd_bf{e}", [DFF, D])) for e in range(NE)]
    t_cv = Tile(None, "cv")
    conv = []
    for e in range(NE):
        for j in range(3):
            src, dst = moe[e][j], moe_bf[e][j]
            R = src.shape[0]
            for r0 in range(0, R, 256):
                conv.append((dst[r0:r0 + 256, :], src[r0:r0 + 256, :]))
    npoints = 48
    per = (len(conv) + npoints - 1) // npoints

    def conv_some():
        for _ in range(per):
            if conv:
                dst, src = conv.pop(0)
                p.dma('pool', dst, src, writes=[t_cv], ring="cv")
    x1, x1T, x2, x2T = sc("x1", [S, D]), sc("x1T", [D, S]), sc("x2", [S, D]), sc("x2T", [D, S])
    T = lambda n: Tile(None, n)
    t_in, t_proj, t_v, t_ys, t_vf, t_f = T("in"), T("proj"), T("vtm"), T("ys"), T("vf"), T("f")
    t_x1, t_x1T, t_x2, t_x2T, t_out = T("x1"), T("x1T"), T("x2"), T("x2T"), T("EXT")

    cx = Ctx(p, cd)
    sl = Slabs(p)
    ar = AttnRes(p, cx, sl)
    mr = MlaRes(p, sl)
    rr = RwkvRes(p, cx, sl)
    lr = LnRes(p)
    mo = MoeRes(p)
    print("sbuf remaining", p.nc.sbuf_bytes_remaining)

    cur_tm, cur_t, curT, curT_t = x_tm, t_in, xT, t_in
    for l in range(2):
        d = L[l]
        proj_stage(p, cx, sl, curT, curT_t, d["wcat"], projT, t_proj, v_tm, t_v)
        p.dma('sp', ar.cos[:, :], cd["cosA"], writes=[ar.cos])
        p.dma('sp', ar.sin[:, :], cd["sinA"], writes=[ar.sin])
        for h in range(8):
            conv_some()
            moba_head(p, cx, ar, ar.cos, ar.sin, projT[h * 128:(h + 1) * 128, :], projT[1024 + h * 128:1024 + (h + 1) * 128, :],
                      v_tm[:, h * 128:(h + 1) * 128], t_proj, ysT[h * 128:(h + 1) * 128, :], t_ys, v_tile=t_v)
        rwkv_stage(p, cx, rr, projT, t_proj, d, ysT, t_ys, 1024, vfT, t_vf, l > 0, hook=conv_some)
        mla_stage(p, cx, ar, mr, projT, t_proj, d["w_uq"], d["w_ukv"], d["qn"], d["kvn"], cd["cosM"], cd["sinM"], ysT, t_ys, 2048, hook=conv_some)
        mix_stage(p, cx, sl, ysT, t_ys, projT, t_proj, d["bo"], d["wo"], f_tm, t_f)
        resid_ln(p, cx, sl, lr, f_tm, t_f, cur_tm, cur_t, d["ln1_g"], d["ln1_b"], x1, t_x1, x1T, t_x1T)
        if l == 0:
            ffn_stage(p, cx, sl, mo, x1T, t_x1T, [ffn], None, f_tm, t_f)
            resid_ln(p, cx, sl, lr, f_tm, t_f, x1, t_x1, d["ln2_g"], d["ln2_b"], x2, t_x2, x2T, t_x2T)
            cur_tm, cur_t, curT, curT_t = x2, t_x2, x2T, t_x2T
        else:
            while conv:
                conv_some()
            ffn_stage(p, cx, sl, mo, x1T, t_x1T, moe_bf, router, f_tm, t_f, bf_src=t_cv)
            resid_ln(p, cx, sl, lr, f_tm, t_f, x1, t_x1, d["ln2_g"], d["ln2_b"], out, t_out, None, None, final=True)
    print("instr", {k: len(v) for k, v in p.ops.items()})
    return p.build(), ct


def kernel(**inp):
    f = lambda a: np.ascontiguousarray(np.asarray(a, dtype=np.float32))
    nc, ct = _get_full()
    shared = {"c_" + n: a for n, a in ct.items()}
    for l in range(2):
        shared[f"wcat{l}"] = make_wcat(f(inp["w_in"][l]), f(inp["w_in_vres"][l - 1]) if l > 0 else None)
        for n in ("w0", "a0", "k_k", "k_a", "gn_g", "gn_b", "w_lora", "a_lora", "g_lora"):
            shared[f"rw{l}_{n}"] = f(inp["rwkv_" + n][l])
        shared[f"rw{l}_mu"] = f(inp["rwkv_mu"][l])
        shared[f"rw{l}_r_k"] = f(inp["rwkv_r_k"][l]).reshape(1024)
        if l > 0:
            shared[f"rw{l}_mu_v"] = f(inp["rwkv_mu_vres"][l - 1])
            shared[f"rw{l}_v0"] = f(inp["rwkv_v0"][l - 1])
            shared[f"rw{l}_v_lora"] = f(inp["rwkv_v_lora"][l - 1])
        else:
            shared[f"rw{l}_mu_v"] = np.zeros(32, np.float32)
            shared[f"rw{l}_v0"] = np.zeros(1024, np.float32)
            shared[f"rw{l}_v_lora"] = np.zeros((32, 1024), np.float32)
        shared[f"qn{l}"] = f(inp["mla_q_norm"][l])
        shared[f"kvn{l}"] = f(inp["mla_kv_norm"][l])
        shared[f"w_uq{l}"] = f(inp["mla_w_uq"][l])
        shared[f"w_ukv{l}"] = f(inp["mla_w_ukv"][l])
        for n in range(3):
            shared[f"bo{l}_{n}"] = f(inp["branch_out"][l][n])
        shared[f"wo{l}"] = f(inp["w_out"][l])
        for n in ("ln1_g", "ln1_b", "ln2_g", "ln2_b"):
            shared[f"{n}{l}"] = f(inp[n][l])
    shared["ffn_wg"], shared["ffn_wu"], shared["ffn_wd"] = f(inp["ffn_wg"][0]), f(inp["ffn_wu"][0]), f(inp["ffn_wd"][0])
    shared["router"] = f(inp["moe_router"][0])
    for e in range(NE):
        shared[f"moe_wg{e}"] = f(inp["moe_wg"][0][e])
        shared[f"moe_wu{e}"] = f(inp["moe_wu"][0][e])
        shared[f"moe_wd{e}"] = f(inp["moe_wd"][0][e])
    x = f(inp["x"])
    in_maps = []
    for b in range(N_ACTIVE):
        m = dict(shared)
        m["x_tm"] = x[b]
        m["xT"] = np.ascontiguousarray(x[b].T)
        in_maps.append(m)
    res = run_bass_kernel_spmd(nc, in_maps, core_ids=list(range(N_ACTIVE)))
    return np.stack([res.results[b]["out"] for b in range(N_ACTIVE)], 0).astype(np.float32)


_FULL = []


def _get_full():
    if not _FULL:
        _FULL.append(build_full())
    return _FULL[0]
```

```python
import contextlib
import numpy as np
import concourse.bass as bass
import concourse.mybir as mybir
from concourse.bass_utils import run_bass_kernel_spmd

F32 = mybir.dt.float32
BF16 = mybir.dt.bfloat16
F32R = mybir.dt.float32r
AF = mybir.ActivationFunctionType
ALU = mybir.AluOpType
AX = mybir.AxisListType

COMPUTE = ('pe', 'act', 'dve', 'pool')
DMA_RING = 8


class Tile:
    def __init__(self, h, name):
        self.h = h
        self.name = name
        self.w = {}
        self.r = {}

    def __getitem__(self, k):
        return self.h[k]

    def all_w(self):
        return self.w.items()

    def all_r(self):
        return self.r.items()


class Prog:
    def __init__(self, name="k"):
        self.nc = bass.Bass("TRN2", target_bir_lowering=False)
        self.es = contextlib.ExitStack()
        self.ops = {e: [] for e in COMPUTE + ('sp',)}
        self.ncomp = {e: 0 for e in COMPUTE}
        self.waited = {e: {} for e in COMPUTE + ('sp',)}
        self.sems = {}
        self.dma_n = {}
        self.ntile = 0
        self.out_tokens = []
        for e in COMPUTE:
            self.sems[e] = self.es.enter_context(self.nc.semaphore("s_" + e))

    def dram(self, name, shape, dt, kind):
        return self.nc.dram_tensor(name, list(shape), dt, kind=kind).ap()

    def sb(self, shape, dt, name=None):
        self.ntile += 1
        name = name or f"t{self.ntile}"
        h = self.es.enter_context(self.nc.sbuf_tensor(name, list(shape), dt))
        return Tile(h, name)

    def ps(self, shape, dt=F32, name=None):
        self.ntile += 1
        name = name or f"p{self.ntile}"
        h = self.es.enter_context(self.nc.psum_tensor(name, list(shape), dt))
        return Tile(h, name)

    def _deps(self, eng, reads, writes, is_dma=False):
        deps = []
        for t in reads:
            for tok in t.all_w():
                deps.append((tok, True))
        for t in writes:
            for tok in t.all_w():
                if is_dma and tok[0].startswith("d_"):
                    continue
                deps.append((tok, False))
            for tok in t.all_r():
                deps.append((tok, False))
        need = {}
        for (sem, val), raw in deps:
            if sem == eng and not raw:
                continue
            if self.waited[eng].get(sem, 0) >= val:
                continue
            need[sem] = max(need.get(sem, 0), val)
        for sem, val in need.items():
            self.waited[eng][sem] = val
        return list(need.items())

    def _record(self, tok, reads, writes):
        sem, val = tok
        for t in reads:
            if t.r.get(sem, 0) < val:
                t.r[sem] = val
        for t in writes:
            if t.w.get(sem, 0) < val:
                t.w[sem] = val

    def op(self, eng, fn, reads=(), writes=()):
        waits = self._deps(eng, reads, writes)
        self.ncomp[eng] += 1
        tok = (eng, self.ncomp[eng])
        self.ops[eng].append((waits, fn, (eng, 1)))
        self._record(tok, reads, writes)
        return tok

    def dma(self, q, out, in_, reads=(), writes=(), is_output=False, **kw):
        n = self.dma_n.get(q, 0)
        self.dma_n[q] = n + 1
        slot = n % DMA_RING
        key = f"d_{q}{slot}"
        if key not in self.sems:
            self.sems[key] = self.es.enter_context(self.nc.semaphore(key))
        prev = 16 * (n // DMA_RING)
        waits = self._deps(q, reads, writes, is_dma=True)
        if prev > 0 and self.waited[q].get(key, 0) < prev:
            waits.append((key, prev))
            self.waited[q][key] = prev
        tok = (key, prev + 16)
        self.ops[q].append((waits, I('dma_start', out=out, in_=in_, **kw), (key, 16)))
        self._record(tok, reads, writes)
        if is_output:
            self.out_tokens.append(tok)
        return tok

    def build(self):
        nc = self.nc
        fin = {}
        for sem, val in self.out_tokens:
            fin[sem] = max(fin.get(sem, 0), val)
        ops = self.ops
        sems = self.sems

        def emit(engname):
            def f(e):
                for waits, fn, inc in ops[engname]:
                    for sem, val in waits:
                        e.wait_ge(sems[sem], val)
                    ins = fn(e)
                    if inc is not None:
                        ins.then_inc(sems[inc[0]], inc[1])
                if engname == 'sp':
                    for sem, val in fin.items():
                        e.wait_ge(sems[sem], val)
            return f

        with nc.allow_low_precision("fp32r (11-bit mantissa) rounding of single-pass matmul operands"), nc.Block() as block:
            block.sync(emit('sp'))
            block.tensor(emit('pe'))
            block.scalar(emit('act'))
            block.vector(emit('dve'))
            block.gpsimd(emit('pool'))
        self.es.close()
        return nc


D = 2048
S = 2048
B = 4
NTOK = 1024
BW = 1024
DFF = 5632
NE = 8
NCAT = 13824
GATE_CHUNK0 = 60
ALPHA = 4 ** 0.25
LN_EPS = 1e-5


ROUND_F32R = [False]


def I(method, *args, **kw):
    if ROUND_F32R[0] and method in ("activation", "copy", "tensor_copy", "tensor_tensor", "tensor_scalar", "tensor_scalar_mul", "scalar_tensor_tensor", "tensor_scalar_max", "reciprocal", "memset") and args:
        out = args[0]
        if out.dtype == F32 and type(out.tensor).__name__.startswith("SB"):
            args = (out.bitcast(F32R),) + tuple(args[1:])
    return lambda e: getattr(e, method)(*args, **kw)


def mm(ps_ap, lhsT, rhs, start, stop, r=False):
    if r and USE_F32R:
        lhsT = lhsT.bitcast(F32R)
        rhs = rhs.bitcast(F32R)
    return I('matmul', ps_ap, lhsT, rhs, start=start, stop=stop)


class Gemm:
    def __init__(self, p, KC, wcols=512, nbuf=3, q='pool'):
        self.p = p
        self.KC = KC
        self.wcols = wcols
        self.wb = [p.sb([128, KC, wcols], BF16) for _ in range(nbuf)]
        self.i = 0
        self.q = q

    def load(self, w_ap, c0, ncols):
        wt = self.wb[self.i % len(self.wb)]
        self.i += 1
        wv = w_ap.rearrange("(c p) n -> p c n", p=128)
        self.p.dma(self.q, wt[:, :, 0:ncols], wv[:, :, c0:c0 + ncols], writes=[wt])
        return wt


def build_proj():
    p = Prog()
    KC = D // 128
    xT = p.dram("xT", [D, NTOK], F32, "ExternalInput")
    w = p.dram("w", [D, NCAT], F32, "ExternalInput")
    oT = p.dram("oT", [NCAT, NTOK], F32, "ExternalOutput")
    xs = p.sb([128, KC, NTOK], BF16, "xs")
    xv = xT.rearrange("(c p) t -> p c t", p=128)
    for i in range(4):
        p.dma('pool', xs[:, i * 4:(i + 1) * 4, :], xv[:, i * 4:(i + 1) * 4, :], writes=[xs])
    g = Gemm(p, KC)
    pss = [p.ps([128, 512], F32) for _ in range(6)]
    ob = [p.sb([128, NTOK], F32) for _ in range(3)]
    pi = 0
    for nb in range(NCAT // 512):
        wt = g.load(w, nb * 512, 512)
        for j in range(4):
            n = nb * 4 + j
            o = ob[n % 3]
            for th in range(NTOK // 512):
                ps = pss[pi % 6]
                pi += 1
                for k in range(KC):
                    p.op('pe', mm(ps[:, :], wt[:, k, j * 128:(j + 1) * 128], xs[:, k, th * 512:(th + 1) * 512],
                                  k == 0, k == KC - 1), reads=[wt, xs], writes=[ps])
                osl = o[:, th * 512:(th + 1) * 512]
                if n >= GATE_CHUNK0:
                    p.op('act', I('activation', osl, ps[:, :], AF.Sigmoid),
                         reads=[ps], writes=[o])
                elif th == 0:
                    p.op('act', I('copy', osl, ps[:, :]), reads=[ps], writes=[o])
                else:
                    p.op('dve', I('tensor_copy', osl, ps[:, :]), reads=[ps], writes=[o])
            p.dma('sp', oT[n * 128:(n + 1) * 128, :], o[:, :], reads=[o], is_output=True)
    return p.build()


_W_IN_COLS = 13664


def make_wcat(w_in_l, w_vres_l):
    z = lambda n: np.zeros((D, n), np.float32)
    segs = [
        w_in_l[:, 0:3072],
        w_in_l[:, 3072:6144],
        w_in_l[:, 6144:6432],
        w_vres_l if w_vres_l is not None else z(32),
        z(64),
        w_in_l[:, 6432:7520],
        z(64),
        w_in_l[:, 7520:13664],
    ]
    return np.ascontiguousarray(np.concatenate(segs, axis=1))


_NC_CACHE = {}


def get_nc(name, builder):
    if name not in _NC_CACHE:
        _NC_CACHE[name] = builder()
    return _NC_CACHE[name]


def launch(name, builder, in_maps):
    nc = get_nc(name, builder)
    res = run_bass_kernel_spmd(nc, in_maps, core_ids=list(range(len(in_maps))))
    return res.results


BIG = 30000.0


def const_tables():
    c = {}
    c["ident"] = np.eye(128, dtype=np.float32)
    k = np.arange(128)[:, None]
    q = np.arange(128)[None, :]
    c["tri"] = np.where(k <= q, 0.0, -BIG).astype(np.float32)
    sel = np.zeros((8, 8, 128), np.float32)
    for n in range(8):
        sel[n, n, :] = 1.0
    c["selE"] = sel.reshape(8, 8 * 128)
    pos = np.arange(S, dtype=np.float32)

    def tabs(dim):
        inv = (1.0 / (10000.0 ** (np.arange(0, dim, 2, dtype=np.float32) / dim))).astype(np.float32)
        ang = (pos[:, None] * inv[None, :]).astype(np.float32)
        co, si = np.cos(ang).astype(np.float32).T, np.sin(ang).astype(np.float32).T
        return (np.ascontiguousarray(np.concatenate([co, co], 0)),
                np.ascontiguousarray(np.concatenate([-si, si], 0)))
    c["cosA"], c["sinA"] = tabs(128)
    c["cosM"], c["sinM"] = tabs(64)
    gpen = np.zeros((16, 8), np.float32)
    own = np.zeros((16, 8), np.float32)
    for i in range(16):
        gpen[i, i // 2:] = -1e30
        own[i, i // 2] = 1.0
    c["gpen"] = np.ascontiguousarray(np.broadcast_to(gpen.reshape(1, 128), (128, 128)))
    c["ownm"] = np.ascontiguousarray(np.broadcast_to(own.reshape(1, 128), (128, 128)))
    return c


class Ctx:
    def __init__(self, p, cd):
        self.p = p
        self.cd = cd
        self.pb = [p.ps([128, 512], F32, f"pb{i}") for i in range(8)]
        self.ident = self.load_const("ident", [128, 128])
        self.ones = p.sb([128, 128], F32, "ones")
        self.ones_bf = p.sb([128, 128], BF16, "ones_bf")
        p.op('pool', I('memset', self.ones[:, :], 1.0), writes=[self.ones])
        p.op('pool', I('memset', self.ones_bf[:, :], 1.0), writes=[self.ones_bf])

    def load_const(self, name, shape, dt=F32, tag=""):
        t = self.p.sb(shape, dt, "k_" + name + tag + ("_bf" if dt == BF16 else ("_r" if dt == F32R else "")))
        src = self.cd[name]
        if dt == F32:
            self.p.dma('sp', t[:, :], src, writes=[t])
        else:
            self.p.dma('pool', t[:, :], src, writes=[t])
        return t


NSLAB = 21


class View:
    def __init__(self, tile, ap, rng, state=None):
        self.tile = tile
        self.ap = ap
        self.name = tile.name
        self.rng = rng
        if state is None:
            self.w, self.r = {}, {}
            tile.views.append(self)
        else:
            self.w, self.r = state

    def derive(self, ap):
        return View(self.tile, ap, self.rng, (self.w, self.r))

    def __getitem__(self, k):
        return self.ap[k]

    def _overlapping(self):
        r0, r1, b0, b1 = self.rng
        for v in self.tile.views:
            q0, q1, c0, c1 = v.rng
            if q0 < r1 and r0 < q1 and c0 < b1 and b0 < c1:
                yield v

    def all_w(self):
        for v in self._overlapping():
            yield from v.w.items()

    def all_r(self):
        for v in self._overlapping():
            yield from v.r.items()


class Slabs:
    def __init__(self, p):
        self.t = [p.sb([128, 2048], F32, f"slab{i}") for i in range(NSLAB)]
        for t in self.t:
            t.views = []
        self.cache = {}

    def _get(self, key, mk):
        if key not in self.cache:
            self.cache[key] = mk()
        return self.cache[key]

    def f32(self, i, rows=128, cols=2048, off=0):
        return self._get((i, "f", rows, cols, off),
                         lambda: View(self.t[i], self.t[i].h[0:rows, off:off + cols], (0, rows, off * 4, (off + cols) * 4)))

    def bf(self, i, rows=128, cols=4096, off=0):
        return self._get((i, "b", rows, cols, off),
                         lambda: View(self.t[i], self.t[i].h.bitcast(BF16)[0:rows, off:off + cols], (0, rows, off * 2, (off + cols) * 2)))


class AttnRes:
    def __init__(self, p, cx, sl):
        self.tri = cx.load_const("tri", [128, 128], BF16)
        self.selE = cx.load_const("selE", [8, 1024], BF16)
        self.ident_bf = cx.load_const("ident", [128, 128], BF16)
        self.RTb = p.sb([8, S], BF16, "a_RTb")
        self.gpen = cx.load_const("gpen", [128, 128])
        self.ownm = cx.load_const("ownm", [128, 128])
        self.raw = [sl.f32(0), sl.f32(1)]
        self.rot = [sl.f32(2), sl.f32(3)]
        self.t1 = sl.f32(4)
        self.t2 = sl.f32(5)
        self.qf = sl.f32(6)
        self.kf = sl.f32(7)
        self.qb = [sl.bf(8, cols=2048), sl.bf(8, cols=2048, off=2048)]
        self.kb = [sl.bf(9, cols=2048), sl.bf(9, cols=2048, off=2048)]
        vv = sl.bf(10, cols=2048)
        self.v = vv.derive(vv.ap.rearrange("p (b d) -> p b d", d=128))
        self.pt = [sl.bf(10, cols=512, off=2048 + 512 * i) for i in range(3)]
        self.krow = sl.f32(11, rows=1)
        self.RT = sl.f32(12, rows=8)
        self.rs = [sl.f32(13, cols=512, off=0), sl.f32(13, cols=512, off=512)]
        self.o = [sl.f32(13, cols=512, off=1024), sl.f32(13, cols=512, off=1536)]
        self.cos = sl.f32(14)
        self.sin = sl.f32(15)
        self.small = {n: p.sb([128, 128], F32, "a_s_" + n) for n in
                      ("qq", "nb", "gate", "mx", "sel", "R", "kmT", "km2", "kmbc", "qq2")}


def rope_fm(p, out_f, raw, rot, cos, sin, t1, t2, nrow=128):
    p.op('dve', I('tensor_tensor', t1[0:nrow, :], raw[0:nrow, :], cos[0:nrow, :], ALU.mult),
         reads=[raw, cos], writes=[t1])
    p.op('pool', I('tensor_tensor', t2[0:nrow, :], rot[0:nrow, :], sin[0:nrow, :], ALU.mult),
         reads=[rot, sin], writes=[t2])
    p.op('dve', I('tensor_tensor', out_f[0:nrow, :], t1[0:nrow, :], t2[0:nrow, :], ALU.add),
         reads=[t1, t2], writes=[out_f])


def attn_core(p, cx, ar, qparts, kparts, scale, out_dram, out_tile, gating):
    pb = cx.pb
    sm = ar.small
    sq = ar.t1
    first = True
    for (_, nr, qf) in qparts:
        p.op('act', I('activation', sq[0:nr, :], qf[0:nr, :], AF.Square),
             reads=[qf], writes=[sq])
        for i in range(16):
            p.op('pe', mm(pb[0][:, i:i + 1], sq[0:nr, i * 128:(i + 1) * 128], cx.ones[0:nr, 0:1],
                          True, True), reads=[sq, cx.ones], writes=[pb[0]])
        if first:
            p.op('dve', I('tensor_copy', sm["qq"][:, 0:16], pb[0][:, 0:16]), reads=[pb[0]], writes=[sm["qq"]])
        else:
            p.op('dve', I('tensor_tensor', sm["qq"][:, 0:16], sm["qq"][:, 0:16], pb[0][:, 0:16], ALU.add),
                 reads=[pb[0], sm["qq"]], writes=[sm["qq"]])
        first = False
    sk = ar.t2
    for pi_, (_, nr, kf) in enumerate(kparts):
        p.op('act', I('activation', sk[0:nr, :], kf[0:nr, :], AF.Square),
             reads=[kf], writes=[sk])
        for c in range(4):
            p.op('pe', mm(pb[1 + c][0:1, :], cx.ones[0:nr, 0:1], sk[0:nr, c * 512:(c + 1) * 512],
                          True, True), reads=[sk, cx.ones], writes=[pb[1 + c]])
        for c in range(4):
            if pi_ == 0:
                p.op('dve', I('tensor_copy', ar.krow[0:1, c * 512:(c + 1) * 512], pb[1 + c][0:1, :]),
                     reads=[pb[1 + c]], writes=[ar.krow])
            else:
                p.op('dve', I('tensor_tensor', ar.krow[0:1, c * 512:(c + 1) * 512],
                                                          ar.krow[0:1, c * 512:(c + 1) * 512], pb[1 + c][0:1, :], ALU.add),
                     reads=[pb[1 + c], ar.krow], writes=[ar.krow])
    p.op('dve', I('reduce_max', sm["km2"][0:1, 0:1], ar.krow[0:1, :], AX.X), reads=[ar.krow], writes=[sm["km2"]])
    p.op('pe', mm(pb[0][:, 32:33], cx.ones[0:1, :], sm["km2"][0:1, 0:1], True, True),
         reads=[cx.ones, sm["km2"]], writes=[pb[0]])
    p.op('dve', I('tensor_copy', sm["kmbc"][:, 0:1], pb[0][:, 32:33]), reads=[pb[0]], writes=[sm["kmbc"]])
    p.op('dve', I('tensor_scalar_mul', sm["qq2"][:, 0:16], sm["qq"][:, 0:16], sm["kmbc"][:, 0:1]),
         reads=[sm["qq"], sm["kmbc"]], writes=[sm["qq2"]])
    p.op('act', I('activation', sm["nb"][:, 0:16], sm["qq2"][:, 0:16], AF.Sqrt),
         reads=[sm["qq2"]], writes=[sm["nb"]])
    R3 = sm["R"][:, :].rearrange("p (i n) -> p i n", n=8)
    nb3 = sm["nb"][:, 0:16].unsqueeze(2).to_broadcast([128, 16, 8])
    if gating:
        kf = kparts[0][2]
        qf = qparts[0][2]
        p.op('dve', I('reduce_sum', sm["kmT"][:, 0:8], kf[:, :].rearrange("p (n k) -> p n k", k=256), AX.X),
             reads=[kf], writes=[sm["kmT"]])
        p.op('dve', I('tensor_scalar_mul', sm["kmT"][:, 0:8], sm["kmT"][:, 0:8], 1.0 / 256.0),
             reads=[sm["kmT"]], writes=[sm["kmT"]])
        for i in range(16):
            p.op('pe', mm(pb[5][:, i * 8:(i + 1) * 8], qf[:, i * 128:(i + 1) * 128], sm["kmT"][:, 0:8], True, True),
                 reads=[qf, sm["kmT"]], writes=[pb[5]])
        p.op('dve', I('tensor_tensor', sm["gate"][:, :], pb[5][:, 0:128], ar.gpen[:, :], ALU.add),
             reads=[pb[5], ar.gpen], writes=[sm["gate"]])
        for i in range(16):
            p.op('dve', I('max', sm["mx"][:, i * 8:(i + 1) * 8], sm["gate"][:, i * 8:(i + 1) * 8]),
                 reads=[sm["gate"]], writes=[sm["mx"]])
        g3 = sm["gate"][:, :].rearrange("p (i n) -> p i n", n=8)
        thr3 = sm["mx"][:, :].rearrange("p (i n) -> p i n", n=8)[:, :, 2:3].to_broadcast([128, 16, 8])
        s3 = sm["sel"][:, :].rearrange("p (i n) -> p i n", n=8)
        p.op('dve', I('tensor_tensor', s3, g3, thr3, ALU.is_ge), reads=[sm["gate"], sm["mx"]], writes=[sm["sel"]])
        p.op('dve', I('tensor_tensor', sm["sel"][:, :], sm["sel"][:, :], ar.ownm[:, :], ALU.max),
             reads=[sm["sel"], ar.ownm], writes=[sm["sel"]])
        p.op('dve', I('tensor_scalar', sm["R"][:, :], sm["sel"][:, :], -1.0, BIG, ALU.add, ALU.mult),
             reads=[sm["sel"]], writes=[sm["R"]])
        p.op('dve', I('tensor_tensor', R3, R3, nb3, ALU.subtract), reads=[sm["R"], sm["nb"]], writes=[sm["R"]])
    else:
        p.op('dve', I('memset', sm["R"][:, :], 0.0), writes=[sm["R"]])
        p.op('dve', I('tensor_tensor', R3, R3, nb3, ALU.subtract), reads=[sm["R"], sm["nb"]], writes=[sm["R"]])
    for i in range(16):
        bank = pb[1 + i // 4]
        p.op('pe', I('transpose', bank[0:8, (i % 4) * 128:(i % 4 + 1) * 128],
                                                          sm["R"][:, i * 8:(i + 1) * 8], cx.ident[:, :]),
             reads=[sm["R"], cx.ident], writes=[bank])
    for c in range(4):
        p.op('dve', I('tensor_copy', ar.RTb[0:8, c * 512:(c + 1) * 512], pb[1 + c][0:8, :]),
             reads=[pb[1 + c]], writes=[ar.RTb])
    steps = []
    sti = 0
    for c in range(4):
        nj = 4 * c + 4
        for j in range(nj):
            steps.append((c, j, nj, sti))
            sti += 1

    def stage1(c, j, nj, si):
        q0 = max(0, j - 4 * c) * 128
        cols = slice(q0, 512)
        gcols = slice(c * 512 + q0, (c + 1) * 512)
        st = pb[si % 4]
        pt = ar.pt[si % 3]
        for pi_, ((qb, nr, _), (kb, _, _)) in enumerate(zip(qparts, kparts)):
            p.op('pe', mm(st[:, cols], kb[0:nr, j * 128:(j + 1) * 128], qb[0:nr, gcols], pi_ == 0, False),
                 reads=[kb, qb], writes=[st])
        n = j // 2
        diag = j >= 4 * c
        p.op('pe', mm(st[:, cols], ar.selE[0:8, n * 128:(n + 1) * 128], ar.RTb[0:8, gcols], False, not diag),
             reads=[ar.selE, ar.RTb], writes=[st])
        if diag:
            p.op('pe', mm(st[:, q0:q0 + 128], ar.ident_bf[:, :], ar.tri[:, :], False, True),
                 reads=[ar.ident_bf, ar.tri], writes=[st])
        p.op('act', I('activation', pt[:, cols], st[:, cols], AF.Exp, scale=scale), reads=[st], writes=[pt])

    def stage2(c, j, nj, si):
        q0 = max(0, j - 4 * c) * 128
        cols = slice(q0, 512)
        OT = pb[4 + (c % 2)]
        SM = pb[6 + (c % 2)]
        pt = ar.pt[si % 3]
        p.op('pe', mm(OT[:, cols], ar.v[:, j, :], pt[:, cols], j == 0, j == nj - 1), reads=[ar.v, pt], writes=[OT])
        p.op('pe', mm(SM[:, cols], cx.ones_bf[:, :], pt[:, cols], j == 0, j == nj - 1),
             reads=[cx.ones_bf, pt], writes=[SM])
        if j == nj - 1:
            rs = ar.rs[c % 2]
            o = ar.o[c % 2]
            p.op('dve', I('reciprocal', rs[:, :], SM[:, :]), reads=[SM], writes=[rs])
            p.op('dve', I('tensor_tensor', o[:, :], OT[:, :], rs[:, :], ALU.mult), reads=[OT, rs], writes=[o])
            p.dma('sp', out_dram[:, c * 512:(c + 1) * 512], o[:, :], reads=[o], writes=[out_tile],
                  is_output=(out_tile.name == "EXT"))

    LOOK = 2
    for i in range(len(steps) + LOOK):
        if i < len(steps):
            stage1(*steps[i])
        if i >= LOOK:
            stage2(*steps[i - LOOK])


def load_rot(p, q, dst, src, half):
    p.dma(q, dst[0:half, :], src[half:2 * half, :], writes=[dst])
    p.dma(q, dst[half:2 * half, :], src[0:half, :], writes=[dst])


def moba_head(p, cx, ar, cosA, sinA, q_src, k_src, v_src, src_tile, out_dram, out_tile, v_tile=None):
    for (src, ff, bb, ri) in ((q_src, ar.qf, ar.qb[0], 0), (k_src, ar.kf, ar.kb[0], 1)):
        raw, rot = ar.raw[ri], ar.rot[ri]
        p.dma('sp', raw[:, :], src, reads=[src_tile], writes=[raw])
        p.dma('act', rot[0:64, :], src[64:128, :], reads=[src_tile], writes=[rot])
        p.dma('act', rot[64:128, :], src[0:64, :], reads=[src_tile], writes=[rot])
        rope_fm(p, ff, raw, rot, cosA, sinA, ar.t1, ar.t2)
        p.op('act', I('copy', bb[:, :], ff[:, :]), reads=[ff], writes=[bb])
    p.dma('pool', ar.v[:, :, :], v_src.rearrange("(b p) d -> p b d", p=128), reads=[v_tile or src_tile], writes=[ar.v])
    attn_core(p, cx, ar, [(ar.qb[0], 128, ar.qf)], [(ar.kb[0], 128, ar.kf)], 128 ** -0.5, out_dram, out_tile, True)


RMS_EPS = 1e-6
ROW_MLA = 6528
ROW_GATE = 7680


class MlaRes:
    def __init__(self, p, sl):
        self.xnq = [sl.bf(16 + c // 2, cols=2048, off=(c % 2) * 2048) for c in range(4)]
        self.xnkv = [sl.bf(18 + c // 2, cols=2048, off=(c % 2) * 2048) for c in range(4)]
        self.kpe_f = sl.f32(20, rows=64)
        self.kpe_b = p.sb([64, S], BF16, "kpe_b")
        self.wq = p.sb([128, 4, 256], BF16, "m_wq")
        self.wkv = p.sb([128, 4, 256], BF16, "m_wkv")
        self.gq = p.sb([128, 4], F32, "m_gq")
        self.gkv = p.sb([128, 4], F32, "m_gkv")
        self.row = p.sb([1, 512], F32, "m_row")
        self.eps = p.sb([1, 1], F32, "m_eps")
        p.op('pool', I('memset', self.eps[:, :], RMS_EPS), writes=[self.eps])


def mla_stage(p, cx, ar, mr, projT, proj_t, w_uq, w_ukv, qn, kvn, cosM, sinM, ysT, ys_t, row0, heads=range(8)):
    pb = cx.pb
    p.dma('sp', ar.cos[0:64, :], cosM, writes=[ar.cos])
    p.dma('sp', ar.sin[0:64, :], sinM, writes=[ar.sin])
    p.dma('sp', mr.gq[:, :], qn.rearrange("(c p) -> p c", p=128), writes=[mr.gq], allow_slow_non_contiguous=True)
    p.dma('sp', mr.gkv[:, :], kvn.rearrange("(c p) -> p c", p=128), writes=[mr.gkv], allow_slow_non_contiguous=True)
    xt = ar.raw[0].derive(ar.raw[0].ap.rearrange("p (c t) -> p c t", c=4))
    sq = ar.raw[1].derive(ar.raw[1].ap.rearrange("p (c t) -> p c t", c=4))
    for (r0, g, xn) in ((ROW_MLA, mr.gq, mr.xnq), (ROW_MLA + 512, mr.gkv, mr.xnkv)):
        for ch in range(4):
            cs = slice(ch * 512, (ch + 1) * 512)
            p.dma('sp', xt[:, :, :], projT[r0:r0 + 512, cs].rearrange("(c p) t -> p c t", p=128),
                  reads=[proj_t], writes=[xt])
            p.op('act', I('activation', sq[:, :, :], xt[:, :, :], AF.Square), reads=[xt], writes=[sq])
            for c in range(4):
                p.op('pe', mm(pb[0][0:1, :], cx.ones[:, 0:1], sq[:, c, :], c == 0, c == 3),
                     reads=[cx.ones, sq], writes=[pb[0]])
            p.op('act', I('activation', mr.row[0:1, :], pb[0][0:1, :], AF.Sqrt, bias=mr.eps[0:1, 0:1],
                                               scale=1.0 / 512.0), reads=[pb[0], mr.eps], writes=[mr.row])
            p.op('dve', I('reciprocal', mr.row[0:1, :], mr.row[0:1, :]), reads=[mr.row], writes=[mr.row])
            p.op('pe', mm(pb[1][:, :], cx.ones[0:1, :], mr.row[0:1, :], True, True),
                 reads=[cx.ones, mr.row], writes=[pb[1]])
            for c in range(4):
                p.op('dve', I('scalar_tensor_tensor',
                    xn[c][:, cs], xt[:, c, :], g[:, c:c + 1], pb[1][:, :], ALU.mult, ALU.mult),
                    reads=[xt, g, pb[1]], writes=[xn[c]])
    rk = ROW_MLA + 1024
    p.dma('sp', ar.raw[0][0:64, :], projT[rk:rk + 64, :], reads=[proj_t], writes=[ar.raw[0]])
    p.dma('act', ar.rot[0][0:32, :], projT[rk + 32:rk + 64, :], reads=[proj_t], writes=[ar.rot[0]])
    p.dma('act', ar.rot[0][32:64, :], projT[rk:rk + 32, :], reads=[proj_t], writes=[ar.rot[0]])
    rope_fm(p, mr.kpe_f, ar.raw[0], ar.rot[0], ar.cos, ar.sin, ar.t1, ar.t2, nrow=64)
    p.op('act', I('copy', mr.kpe_b[:, :], mr.kpe_f[0:64, :]), reads=[mr.kpe_f], writes=[mr.kpe_b])
    uqv = w_uq.rearrange("(c p) n -> p c n", p=128)
    ukvv = w_ukv.rearrange("(c p) n -> p c n", p=128)
    bi = 0
    for h in heads:
        b0 = h * 192
        p.dma('pool', mr.wq[:, :, 0:192], uqv[:, :, b0:b0 + 192], writes=[mr.wq])
        p.dma('pool', mr.wq[:, :, 192:224], uqv[:, :, b0 + 160:b0 + 192], writes=[mr.wq])
        p.dma('pool', mr.wq[:, :, 224:256], uqv[:, :, b0 + 128:b0 + 160], writes=[mr.wq])
        p.dma('pool', mr.wkv[:, :, :], ukvv[:, :, h * 256:(h + 1) * 256], writes=[mr.wkv])

        def proj_fm(wt, c0, ncol, xn, dst, nrow):
            nonlocal bi
            for ch in range(4):
                ps = pb[bi % 4]
                bi += 1
                cs = slice(ch * 512, (ch + 1) * 512)
                for c in range(4):
                    p.op('pe', mm(ps[0:nrow, :], wt[:, c, c0:c0 + ncol], xn[c][:, cs], c == 0, c == 3),
                         reads=[wt, xn[c]], writes=[ps])
                p.op('act', I('copy', dst[0:nrow, cs], ps[0:nrow, :]), reads=[ps], writes=[dst])

        proj_fm(mr.wq, 0, 128, mr.xnq, ar.qf, 128)
        p.op('dve', I('tensor_copy', ar.qb[0][:, :], ar.qf[:, :]), reads=[ar.qf], writes=[ar.qb[0]])
        proj_fm(mr.wq, 128, 64, mr.xnq, ar.raw[0], 64)
        proj_fm(mr.wq, 192, 64, mr.xnq, ar.rot[0], 64)
        rope_fm(p, ar.raw[1], ar.raw[0], ar.rot[0], ar.cos, ar.sin, ar.t1, ar.t2, nrow=64)
        p.op('dve', I('tensor_copy', ar.qb[1][0:64, :], ar.raw[1][0:64, :]), reads=[ar.raw[1]], writes=[ar.qb[1]])
        proj_fm(mr.wkv, 0, 128, mr.xnkv, ar.kf, 128)
        p.op('dve', I('tensor_copy', ar.kb[0][:, :], ar.kf[:, :]), reads=[ar.kf], writes=[ar.kb[0]])
        for g4 in range(4):
            ps = pb[4 + g4 % 2]
            for t4 in range(4):
                tb = g4 * 4 + t4
                for c in range(4):
                    p.op('pe', mm(ps[:, t4 * 128:(t4 + 1) * 128], mr.xnkv[c][:, tb * 128:(tb + 1) * 128],
                                  mr.wkv[:, c, 128:256], c == 0, c == 3), reads=[mr.xnkv[c], mr.wkv], writes=[ps])
            p.op('act', I('copy',
                ar.v[:, g4 * 4:(g4 + 1) * 4, :], ps[:, :].rearrange("p (b d) -> p b d", d=128)),
                reads=[ps], writes=[ar.v])
        attn_core(p, cx, ar, [(ar.qb[0], 128, ar.qf), (ar.qb[1], 64, ar.raw[1])],
                  [(ar.kb[0], 128, ar.kf), (mr.kpe_b, 64, mr.kpe_f)], 192 ** -0.5,
                  ysT[row0 + h * 128:row0 + (h + 1) * 128, :], ys_t, False)


ROW_RWKV = 3072
ROW_LORA = 6144
GN_EPS = 64e-5
USE_F32R = False


def RV(ap):
    return ap.bitcast(F32R) if USE_F32R else ap


def rwkv_tables():
    c = {}
    t = np.arange(128)
    same = (t[:, None] // 32) == (t[None, :] // 32)
    strict = (same & (t[:, None] < t[None, :])).astype(np.float32)
    incl = (same & (t[:, None] <= t[None, :])).astype(np.float32)
    c["mask5"] = np.ascontiguousarray(np.stack([strict, strict.T, strict, incl, incl], 1).reshape(128, 640))
    cm = (t[:, None] // 32 == np.arange(4)[None, :]).astype(np.float32)
    c["cmask"] = np.ascontiguousarray(cm)
    c["tmask"] = np.ascontiguousarray(np.broadcast_to(cm.T.reshape(1, 512), (64, 512)))
    blk = np.zeros((128, 128), np.float32)
    blk[:64, :64] = 1
    blk[64:, 64:] = 1
    c["blk"] = blk
    return c


class RwkvRes:
    def __init__(self, p, cx, sl):
        self.mask5 = cx.load_const("mask5", [128, 640])
        self.cmask = cx.load_const("cmask", [128, 4])
        self.tmask = cx.load_const("tmask", [64, 512])
        self.blk = cx.load_const("blk", [128, 128], F32R if USE_F32R else F32)
        self.ident_r = cx.load_const("ident", [128, 128], F32R) if USE_F32R else cx.ident
        self.sl = sl
        s = lambda n, shape: p.sb(shape, F32, "r_" + n)

        def v3(i, rows, off, a, b_):
            vw = sl.f32(i, rows=rows, cols=a * b_, off=off)
            return vw.derive(vw.ap.rearrange("p (a b) -> p a b", b=b_))

        self.prm = s("prm", [128, 16, 8])
        self.lw1 = sl.f32(19, cols=1024, off=0)
        self.lw2 = sl.f32(19, cols=1024, off=1024)
        self.lw3 = sl.f32(20, rows=64, cols=1024, off=0)
        self.cst = s("cst", [128, 4])
        for i, v in enumerate((-0.5, GN_EPS, 1.0)):
            p.op('pool', I('memset', self.cst[:, i:i + 1], v), writes=[self.cst])
        self.pc = s("pc", [128, 64])
        self.mul = s("mul", [128, 3])
        self.cumc = s("cumc", [128, 64])
        def b3(i, rows, off, a, b_):
            vw = sl.bf(i, rows=rows, cols=a * b_, off=off)
            return vw.derive(vw.ap.rearrange("p (a b) -> p a b", b=b_))

        self.ident_bf = cx.load_const("ident", [128, 128], BF16, tag="_rw")
        self.tm = [b3(9, 128, 640 * i, 5, 128) for i in range(2)]
        self.dg = [v3(18, 128, 512 * i, 4, 128) for i in range(2)]
        self.z = [v3(18, 64, 1024 + 512 * i, 8, 64) for i in range(2)]
        self.hd = []
        for hh in range(2):
            sa, sb_ = ((1, 3), (7, 8))[hh]
            d = {}
            d["am"] = b3(sa, 128, 0, 5, 128)
            d["utm"] = b3(sa, 128, 640, 4, 128)
            d["pt"] = sl.bf(sa, cols=128, off=1152)
            d["pt2"] = sl.bf(sa, cols=128, off=1280)
            d["ut"] = sl.bf(sa, cols=128, off=1408)
            d["xa"] = b3(sa, 128, 1536, 2, 128)
            d["xb"] = b3(sa, 128, 1792, 2, 128)
            d["w0"] = sl.bf(sa, cols=64, off=2048)
            d["vm"] = b3(sa, 128, 2112, 4, 64)
            d["qtm"] = v3(sb_, 64, 0, 4, 128)
            d["gc"] = v3(sb_, 64, 512, 4, 64)
            d["mt"] = v3(sb_, 64, 768, 4, 64)
            self.hd.append(d)
        self.ysb = s("ysb", [128, 128])
        self.ysq = s("ysq", [128, 128])
        self.st = {n: s("st_" + n, [128, 2]) for n in ("s1", "s2", "m", "msq", "var", "sd", "rstd")}
        self.yn = s("yn", [128, 128])
        self.yo = s("yo", [128, 128])


PRM_NAMES = ["mu_r", "mu_k", "mu_v", "w0", "a0", "v0", "k_k", "k_a", "r_k", "gn_g", "gn_b", "negw0", "omka"]


def rwkv_stage(p, cx, rr, projT, proj_t, prm_d, ysT, ys_t, row0, vfT, vf_t, has_vres, gs=range(8), dbg=None):
    sl = rr.sl
    pb = cx.pb
    S_ = sl.f32
    ext_t = Tile(None, "EXT")
    ROUND_F32R[0] = USE_F32R

    def dump(name, view, sl_):
        if dbg is not None and name in dbg:
            p.dma('sp', dbg.pop(name), view[sl_], reads=[view], writes=[ext_t], is_output=True)
    PI = {n: i for i, n in enumerate(PRM_NAMES)}

    def prm(n, g):
        return rr.prm[:, PI[n], g:g + 1]

    nsl = lambda ap: ap.rearrange("(g p) -> p g", p=128)
    for n, src in (("mu_r", prm_d["mu"][0:1024]), ("mu_k", prm_d["mu"][1024:2048]), ("mu_v", prm_d["mu"][2048:3072]),
                   ("w0", prm_d["w0"]), ("a0", prm_d["a0"]), ("v0", prm_d["v0"]), ("k_k", prm_d["k_k"]),
                   ("k_a", prm_d["k_a"]), ("r_k", prm_d["r_k"]), ("gn_g", prm_d["gn_g"]), ("gn_b", prm_d["gn_b"])):
        p.dma('sp', rr.prm[:, PI[n], :], nsl(src), writes=[rr.prm], allow_slow_non_contiguous=True)
    p.op('dve', I('tensor_scalar_mul', rr.prm[:, PI["negw0"], :], rr.prm[:, PI["w0"], :], -1.0),
         reads=[rr.prm], writes=[rr.prm])
    p.op('dve', I('tensor_scalar', rr.prm[:, PI["omka"], :], rr.prm[:, PI["k_a"], :], -1.0, 1.0,
                                          ALU.mult, ALU.add), reads=[rr.prm], writes=[rr.prm])
    p.dma('pool' if USE_F32R else 'sp', RV(rr.lw1[0:64, :]), prm_d["w_lora"], writes=[rr.lw1])
    p.dma('pool' if USE_F32R else 'sp', RV(rr.lw1[64:128, :]), prm_d["a_lora"], writes=[rr.lw1])
    p.dma('pool' if USE_F32R else 'sp', RV(rr.lw2[:, :]), prm_d["g_lora"][0:128, :], writes=[rr.lw2])
    p.dma('pool' if USE_F32R else 'sp', RV(rr.lw3[0:32, :]), prm_d["g_lora"][128:160, :], writes=[rr.lw3])
    p.dma('pool' if USE_F32R else 'sp', RV(rr.lw3[32:64, :]), prm_d["v_lora"], writes=[rr.lw3])
    mul = rr.mul
    p.dma('sp', mul[:, 0:1], prm_d["mu"][3072:3200].rearrange("(p o) -> p o", o=1), writes=[mul])
    p.dma('sp', mul[:, 1:2], prm_d["mu"][3200:3328].rearrange("(p o) -> p o", o=1), writes=[mul])
    p.dma('sp', mul[0:32, 2:3], prm_d["mu"][3328:3360].rearrange("(p o) -> p o", o=1), writes=[mul])
    p.dma('sp', mul[32:64, 2:3], prm_d["mu_v"].rearrange("(p o) -> p o", o=1), writes=[mul])

    T1, T2, T3 = S_(3), S_(4), S_(18)

    def shift_load(dst, r0, nrow, mu_ap):
        H, Pv = T1, T2
        p.dma('sp', H[0:nrow, :], projT[r0:r0 + nrow, :], reads=[proj_t], writes=[H])
        p.dma('act', Pv[0:nrow, 1:S], projT[r0:r0 + nrow, 0:S - 1], reads=[proj_t], writes=[Pv])
        p.op('pool', I('memset', Pv[0:nrow, 0:1], 0.0), writes=[Pv])
        p.op('pool', I('tensor_tensor', Pv[0:nrow, :], Pv[0:nrow, :], H[0:nrow, :], ALU.subtract),
             reads=[Pv, H], writes=[Pv])
        p.op('dve', I('scalar_tensor_tensor', dst[0:nrow, :], Pv[0:nrow, :], mu_ap, H[0:nrow, :],
                                                     ALU.mult, ALU.add), reads=[Pv, H, mul, rr.prm], writes=[dst])

    XI1, XI2, XI3 = S_(14), S_(15), S_(16)
    shift_load(XI1, ROW_LORA, 128, mul[:, 0:1])
    p.op('act', I('activation', XI1[0:64, :], XI1[0:64, :], AF.Tanh), reads=[XI1], writes=[XI1])
    shift_load(XI2, ROW_LORA + 128, 128, mul[:, 1:2])
    p.op('act', I('activation', XI2[:, :], XI2[:, :], AF.Sigmoid), reads=[XI2], writes=[XI2])
    shift_load(XI3, ROW_LORA + 256, 64, mul[0:64, 2:3])
    p.op('act', I('activation', XI3[0:32, :], XI3[0:32, :], AF.Sigmoid), reads=[XI3], writes=[XI3])

    R, K, V, A, KKN, KM, CUM, U = S_(0), S_(1), S_(2), S_(7), S_(8), S_(9), S_(10), S_(11)
    PINV, PPREV, DE, G, BON, YB = S_(4), S_(5), S_(6), S_(12), S_(13), S_(17)
    Pt = S_(3)
    bk = [0]

    def lora_mm(dst_fn, parts, g):
        for ch in range(4):
            ps = pb[2 + bk[0] % 6]
            bk[0] += 1
            cs = slice(ch * 512, (ch + 1) * 512)
            for i, (lw, xi, r0, r1) in enumerate(parts):
                p.op('pe', mm(ps[:, :], lw[r0:r1, g * 128:(g + 1) * 128], xi[r0:r1, cs], i == 0, i == len(parts) - 1, r=True),
                     reads=[lw, xi], writes=[ps])
            dst_fn(ps, cs)

    for g in gs:
        fr = ROW_RWKV + g * 128
        shift_load(R, fr, 128, prm("mu_r", g))
        shift_load(K, fr + 1024, 128, prm("mu_k", g))
        shift_load(V, fr + 2048, 128, prm("mu_v", g))
        if not has_vres:
            p.dma('sp', vfT[g * 128:(g + 1) * 128, :], V[:, :], reads=[V], writes=[vf_t])
        lora_mm(lambda ps, cs: p.op('act', I('activation', U[:, cs], ps[:, :], AF.Identity,
                                                                  bias=prm("negw0", g), scale=-1.0),
                                    reads=[ps, rr.prm], writes=[U]), [(rr.lw1, XI1, 0, 64)], g)
        p.op('act', I('activation', T1[:, :], U[:, :], AF.Abs), reads=[U], writes=[T1])
        p.op('act', I('activation', T1[:, :], T1[:, :], AF.Exp, scale=-1.0), reads=[T1], writes=[T1])
        p.op('act', I('activation', T1[:, :], T1[:, :], AF.Ln, bias=rr.cst[:, 2:3]), reads=[T1, rr.cst], writes=[T1])
        p.op('dve', I('scalar_tensor_tensor', T1[:, :], U[:, :], 0.0, T1[:, :], ALU.max, ALU.add),
             reads=[U, T1], writes=[T1])
        p.op('act', I('activation', T1[:, :], T1[:, :], AF.Exp, bias=rr.cst[:, 0:1], scale=-1.0),
             reads=[T1, rr.cst], writes=[T1])
        p.op('dve', I('tensor_scalar_mul', U[:, :], T1[:, :], -1.0), reads=[T1], writes=[U])
        src = U
        bufs = [CUM, T1]
        bi = 0
        for sft in (1, 2, 4, 8, 16):
            dst = bufs[bi % 2]
            bi += 1
            s3 = src[:, :].rearrange("p (c k) -> p c k", k=32)
            d3 = dst[:, :].rearrange("p (c k) -> p c k", k=32)
            p.op('dve', I('tensor_tensor', d3[:, :, sft:], s3[:, :, sft:], s3[:, :, :32 - sft], ALU.add),
                 reads=[src], writes=[dst])
            p.op('pool', I('tensor_copy', d3[:, :, :sft], s3[:, :, :sft]),
                 reads=[src], writes=[dst])
            src = dst
        c3 = CUM[:, :].rearrange("p (c k) -> p c k", k=32)
        p.op('dve', I('tensor_copy', rr.cumc[:, :], c3[:, :, 31]), reads=[CUM], writes=[rr.cumc])
        p.op('act', I('activation', rr.pc[:, :], rr.cumc[:, :], AF.Exp), reads=[rr.cumc], writes=[rr.pc])
        p.op('act', I('activation', Pt[:, :], CUM[:, :], AF.Exp), reads=[CUM], writes=[Pt])
        p.op('act', I('activation', PINV[:, :], CUM[:, :], AF.Exp, scale=-1.0), reads=[CUM], writes=[PINV])
        p.op('dve', I('tensor_tensor', PPREV[:, :], CUM[:, :], U[:, :], ALU.subtract), reads=[CUM, U], writes=[PPREV])
        p.op('act', I('activation', PPREV[:, :], PPREV[:, :], AF.Exp), reads=[PPREV], writes=[PPREV])
        d3 = DE[:, :].rearrange("p (c k) -> p c k", k=32)
        p.op('pool', I('tensor_tensor', d3, rr.cumc[:, :].unsqueeze(2).to_broadcast([128, 64, 32]), c3, ALU.subtract),
             reads=[rr.cumc, CUM], writes=[DE])
        p.op('act', I('activation', DE[:, :], DE[:, :], AF.Exp), reads=[DE], writes=[DE])
        lora_mm(lambda ps, cs: p.op('act', I('activation', A[:, cs], ps[:, :], AF.Sigmoid, bias=prm("a0", g)),
                                    reads=[ps, rr.prm], writes=[A]), [(rr.lw1, XI1, 64, 128)], g)
        lora_mm(lambda ps, cs: p.op('act', I('copy', G[:, cs], ps[:, :]), reads=[ps], writes=[G]),
                [(rr.lw2, XI2, 0, 128), (rr.lw3, XI3, 0, 32)], g)
        if has_vres:
            lora_mm(lambda ps, cs: p.op('act', I('activation', T3[:, cs], ps[:, :], AF.Sigmoid, bias=prm("v0", g)),
                                        reads=[ps, rr.prm], writes=[T3]), [(rr.lw3, XI3, 32, 64)], g)
            p.dma('sp', U[:, :], vfT[g * 128:(g + 1) * 128, :], reads=[vf_t], writes=[U])
            p.op('dve', I('tensor_tensor', U[:, :], U[:, :], V[:, :], ALU.subtract), reads=[U, V], writes=[U])
            p.op('pool', I('tensor_tensor', U[:, :], U[:, :], T3[:, :], ALU.mult), reads=[U, T3], writes=[U])
            p.op('dve', I('tensor_tensor', V[:, :], V[:, :], U[:, :], ALU.add), reads=[U, V], writes=[V])
        p.op('dve', I('tensor_scalar_mul', KKN[:, :], K[:, :], prm("k_k", g)), reads=[K, rr.prm], writes=[KKN])
        p.op('act', I('activation', T3[:, :], KKN[:, :], AF.Square), reads=[KKN], writes=[T3])
        for ch in range(4):
            ps = pb[2 + bk[0] % 6]
            bk[0] += 1
            cs = slice(ch * 512, (ch + 1) * 512)
            p.op('pe', mm(ps[:, :], rr.blk[:, :], T3[:, cs], True, True, r=True), reads=[rr.blk, T3], writes=[ps])
            p.op('act', I('activation', U[:, cs], ps[:, :], AF.Sqrt), reads=[ps], writes=[U])
        p.op('dve', I('tensor_scalar_max', U[:, :], U[:, :], 1e-12), reads=[U], writes=[U])
        p.op('dve', I('reciprocal', U[:, :], U[:, :]), reads=[U], writes=[U])
        p.op('pool', I('tensor_tensor', KKN[:, :], KKN[:, :], U[:, :], ALU.mult), reads=[KKN, U], writes=[KKN])
        p.op('dve', I('tensor_scalar', T3[:, :], A[:, :], prm("k_a", g), prm("omka", g), ALU.mult, ALU.add),
             reads=[A, rr.prm], writes=[T3])
        p.op('pool', I('tensor_tensor', KM[:, :], K[:, :], T3[:, :], ALU.mult), reads=[K, T3], writes=[KM])
        p.op('dve', I('scalar_tensor_tensor', T3[:, :], R[:, :], prm("r_k", g), KM[:, :], ALU.mult, ALU.mult),
             reads=[R, KM, rr.prm], writes=[T3])
        for ch in range(4):
            ps = pb[2 + bk[0] % 6]
            bk[0] += 1
            cs = slice(ch * 512, (ch + 1) * 512)
            p.op('pe', mm(ps[:, :], rr.blk[:, :], T3[:, cs], True, True, r=True), reads=[rr.blk, T3], writes=[ps])
            p.op('dve', I('tensor_tensor', BON[:, cs], ps[:, :], V[:, cs], ALU.mult),
                 reads=[ps, V], writes=[BON])
        RB, AB, KT, BT, BHT, KHT = R, PPREV, U, PINV, DE, CUM
        p.op('pool', I('tensor_tensor', RB[:, :], R[:, :], Pt[:, :], ALU.mult), reads=[R, Pt], writes=[RB])
        p.op('dve', I('scalar_tensor_tensor', AB[:, :], KKN[:, :], -1.0, PPREV[:, :], ALU.mult, ALU.mult),
             reads=[KKN, PPREV], writes=[AB])
        p.op('pool', I('tensor_tensor', KT[:, :], KM[:, :], PINV[:, :], ALU.mult), reads=[KM, PINV], writes=[KT])
        p.op('dve', I('tensor_tensor', T3[:, :], KKN[:, :], A[:, :], ALU.mult), reads=[KKN, A], writes=[T3])
        p.op('pool', I('tensor_tensor', BT[:, :], T3[:, :], PINV[:, :], ALU.mult), reads=[T3, PINV], writes=[BT])
        p.op('dve', I('tensor_tensor', KHT[:, :], KM[:, :], DE[:, :], ALU.mult), reads=[KM, DE], writes=[KHT])
        p.op('pool', I('tensor_tensor', BHT[:, :], T3[:, :], DE[:, :], ALU.mult), reads=[T3, DE], writes=[BHT])
        for nm_, vw_ in (("RB", RB), ("AB", AB), ("KT", KT), ("BT", BT), ("BHT", BHT), ("KHT", KHT), ("V", V), ("G", G), ("BON", BON)):
            dump(nm_, vw_, (slice(None), slice(None)))
        dump("PC", rr.pc, (slice(None), slice(None)))
        for hh in range(2):
            p.op('pool', I('memset', rr.z[hh][:, 0, :], 0.0), writes=[rr.z[hh]])

        for tb in range(16):
            ts_ = slice(tb * 128, (tb + 1) * 128)
            tm = rr.tm[tb % 2]
            dg = rr.dg[tb % 2]
            for i, X in enumerate((V, AB, BHT, KHT, RB)):
                bank, c0 = (pb[0], i * 128) if i < 4 else (pb[1], 0)
                p.op('pe', I('transpose', bank[:, c0:c0 + 128], X[:, ts_], cx.ident[:, :]),
                     reads=[X, cx.ident], writes=[bank])
            p.op('act', I('copy', tm[:, 0:4, :], pb[0][:, :].rearrange("p (a b) -> p a b", b=128)),
                 reads=[pb[0]], writes=[tm])
            p.op('dve', I('tensor_copy', tm[:, 4, :], pb[1][:, 0:128]), reads=[pb[1]], writes=[tm])
            p.op('pool', I('tensor_tensor',
                dg[:, :, :], cx.ident[:, :].unsqueeze(1).to_broadcast([128, 4, 128]),
                rr.pc[:, tb * 4:(tb + 1) * 4].unsqueeze(2).to_broadcast([128, 4, 128]), ALU.mult),
                reads=[cx.ident, rr.pc], writes=[dg])
            H = []
            for hh in range(2):
                b = 64 * hh
                d = rr.hd[hh]
                H.append(dict(d=d, b=b, fs=slice(b, b + 64), hb=[pb[2 + 3 * hh], pb[3 + 3 * hh], pb[4 + 3 * hh]]))

            def fm(X, h):
                return X[h["fs"], ts_]

            for h in H:
                hb0, hb1 = h["hb"][0], h["hb"][1]
                pairs = ((BT, AB), (AB, BT), (KT, AB), (BT, RB), (KT, RB))
                for i, (L, Rr) in enumerate(pairs):
                    dstp = hb0[:, i * 128:(i + 1) * 128] if i < 4 else hb1[:, 0:128]
                    p.op('pe', mm(dstp, fm(L, h), fm(Rr, h), True, True, r=True), reads=[L, Rr], writes=[hb0 if i < 4 else hb1])
                am = h["d"]["am"]
                p.op('dve', I('tensor_tensor',
                    am[:, 0:4, :], hb0[:, :].rearrange("p (a b) -> p a b", b=128),
                    rr.mask5[:, 0:512].rearrange("p (a b) -> p a b", b=128), ALU.mult),
                    reads=[hb0, rr.mask5], writes=[am])
                p.op('dve', I('tensor_tensor', am[:, 4, :], hb1[:, 0:128], rr.mask5[:, 512:640], ALU.mult),
                     reads=[hb1, rr.mask5], writes=[am])
                p.op('dve', I('tensor_tensor', h["d"]["pt"][:, :], am[:, 0, :], cx.ident[:, :], ALU.add),
                     reads=[am, cx.ident], writes=[h["d"]["pt"]])
                h["xt"], h["x"] = am[:, 0, :], am[:, 1, :]
                h["xt_t"] = h["x_t"] = am
                h["pt"], h["pt_o"] = h["d"]["pt"], h["d"]["pt2"]
            for lvl in range(1, 5):
                for h in H:
                    hb1 = h["hb"][1]
                    xn_t = h["d"]["xa"] if lvl % 2 == 1 else h["d"]["xb"]
                    p.op('pe', mm(hb1[:, 128:256], h["xt"], h["x"], True, True, r=True), reads=[h["x_t"]], writes=[hb1])
                    if lvl < 4:
                        p.op('pe', mm(hb1[:, 256:384], h["x"], h["xt"], True, True, r=True), reads=[h["x_t"]], writes=[hb1])
                        p.op('act', I('copy',
                            xn_t[:, :, :], hb1[:, 128:384].rearrange("p (a b) -> p a b", b=128)), reads=[hb1], writes=[xn_t])
                    else:
                        p.op('act', I('copy', xn_t[:, 0, :], hb1[:, 128:256]),
                             reads=[hb1], writes=[xn_t])
                    h["x"], h["xt"], h["x_t"] = xn_t[:, 0, :], xn_t[:, 1, :], xn_t
                for h in H:
                    hb1 = h["hb"][1]
                    p.op('pe', mm(hb1[:, 384:512], h["x"], h["pt"][:, :], True, True, r=True), reads=[h["x_t"], h["pt"]], writes=[hb1])
                    p.op('dve', I('tensor_tensor', h["pt_o"][:, :], hb1[:, 384:512], h["pt"][:, :], ALU.add),
                         reads=[hb1, h["pt"]], writes=[h["pt_o"]])
                    h["pt"], h["pt_o"] = h["pt_o"], h["pt"]
            for h in H:
                d, hb2, fsl = h["d"], h["hb"][2], slice(h["b"], h["b"] + 64)
                p.op('pe', mm(hb2[:, 0:64], d["am"][:, 2, :], tm[:, 0, fsl], True, True, r=True), reads=[d["am"], tm], writes=[hb2])
                p.op('act', I('copy', d["w0"][:, :], hb2[:, 0:64]), reads=[hb2], writes=[d["w0"]])
            for h in H:
                d, hb2, fsl = h["d"], h["hb"][2], slice(h["b"], h["b"] + 64)
                p.op('pe', mm(hb2[:, 64:128], h["pt"][:, :], d["w0"][:, :], True, True, r=True), reads=[h["pt"], d["w0"]], writes=[hb2])
                p.op('pe', mm(hb2[:, 128:192], h["pt"][:, :], tm[:, 1, fsl], True, True, r=True), reads=[h["pt"], tm], writes=[hb2])
                p.op('act', I('copy', d["ut"][:, :], hb2[:, 64:192]), reads=[hb2], writes=[d["ut"]])
                p.op('dve', I('tensor_tensor',
                    d["utm"][:, :, :], d["ut"][:, :].unsqueeze(1).to_broadcast([128, 4, 128]),
                    rr.cmask[:, :].unsqueeze(2).to_broadcast([128, 4, 128]), ALU.mult),
                    reads=[d["ut"], rr.cmask], writes=[d["utm"]])
                p.op('pool', I('tensor_tensor',
                    d["vm"][:, :, :], tm[:, 0, fsl].unsqueeze(1).to_broadcast([128, 4, 64]),
                    rr.cmask[:, :].unsqueeze(2).to_broadcast([128, 4, 64]), ALU.mult),
                    reads=[tm, rr.cmask], writes=[d["vm"]])
            for h in H:
                d, hb1, hb2, fsl = h["d"], h["hb"][1], h["hb"][2], slice(h["b"], h["b"] + 64)
                for c in range(4):
                    p.op('pe', mm(hb1[0:64, 128 + c * 64:192 + c * 64], tm[:, 2, fsl], d["utm"][:, c, 0:64], True, False, r=True),
                         reads=[tm, d["utm"]], writes=[hb1])
                    p.op('pe', mm(hb1[0:64, 128 + c * 64:192 + c * 64], tm[:, 3, fsl], d["vm"][:, c, :], False, True, r=True),
                         reads=[tm, d["vm"]], writes=[hb1])
                    p.op('pe', mm(hb2[0:64, 192 + c * 64:256 + c * 64], d["utm"][:, c, 64:128], tm[:, 2, fsl], True, False, r=True),
                         reads=[tm, d["utm"]], writes=[hb2])
                    p.op('pe', mm(hb2[0:64, 192 + c * 64:256 + c * 64], rr.ident_r[:, fsl], dg[:, c, fsl], False, True, r=True),
                         reads=[rr.ident_r, dg], writes=[hb2])
                p.op('act', I('copy', d["gc"][:, :, :], hb1[0:64, 128:384].rearrange("p (a b) -> p a b", b=64)),
                     reads=[hb1], writes=[d["gc"]])
                p.op('dve', I('tensor_copy', d["mt"][:, :, :], hb2[0:64, 192:448].rearrange("p (a b) -> p a b", b=64)),
                     reads=[hb2], writes=[d["mt"]])
            ybank = pb[1]
            for h in H:
                d, hb1, fsl = h["d"], h["hb"][1], slice(h["b"], h["b"] + 64)
                p.op('pe', mm(hb1[0:64, 384:512], tm[:, 4, fsl], rr.ident_bf[:, :], True, False), reads=[tm, rr.ident_bf], writes=[hb1])
                p.op('pe', mm(hb1[0:64, 384:512], d["ut"][:, 64:128], d["am"][:, 3, :], False, True, r=True), reads=[d["ut"], d["am"]], writes=[hb1])
                p.op('dve', I('tensor_tensor',
                    d["qtm"][:, :, :], hb1[0:64, 384:512].unsqueeze(1).to_broadcast([64, 4, 128]),
                    rr.tmask[:, :].rearrange("p (a b) -> p a b", b=128), ALU.mult), reads=[hb1, rr.tmask], writes=[d["qtm"]])
            yacc = pb[0]
            for hi, h in enumerate(H):
                d, fsl = h["d"], slice(h["b"], h["b"] + 64)
                yp = yacc[:, hi * 64:(hi + 1) * 64]
                p.op('pe', mm(yp, d["am"][:, 3, :], d["ut"][:, 0:64], hi == 0, False, r=True), reads=[d["am"], d["ut"]], writes=[yacc])
                p.op('pe', mm(yp, d["am"][:, 4, :], tm[:, 0, fsl], False, False, r=True), reads=[d["am"], tm], writes=[yacc])
            if tb == 0:
                d0 = H[0]["d"]
                dump("tm", tm, (slice(None), slice(None), slice(None)))
                dump("am", d0["am"], (slice(None), slice(None), slice(None)))
                dump("pt", H[0]["pt"], (slice(None), slice(None)))
                dump("ut", d0["ut"], (slice(None), slice(None)))
                dump("gc", d0["gc"], (slice(None), slice(None), slice(None)))
                dump("mt", d0["mt"], (slice(None), slice(None), slice(None)))
                dump("qtm", d0["qtm"], (slice(None), slice(None), slice(None)))
            for c in range(4):
                sidx = tb * 4 + c
                for hi, h in enumerate(H):
                    d = h["d"]
                    z = rr.z[hi]
                    zp = ybank[0:64, 256 + hi * 64:320 + hi * 64]
                    p.op('pe', mm(zp, d["mt"][:, c, :], z[:, sidx % 8, :], True, True, r=True), reads=[d["mt"], z], writes=[ybank])
                    p.op('dve', I('tensor_tensor',
                        z[:, (sidx + 1) % 8, :], zp, d["gc"][:, c, :], ALU.add), reads=[ybank, d["gc"]], writes=[z])
            for hi, h in enumerate(H):
                d = h["d"]
                z = rr.z[hi]
                yp = yacc[:, hi * 64:(hi + 1) * 64]
                for c in range(4):
                    sidx = tb * 4 + c
                    p.op('pe', mm(yp, d["qtm"][:, c, :], z[:, sidx % 8, :], False, c == 3, r=True), reads=[d["qtm"], z], writes=[yacc])
            y3 = rr.ysb[:, :].rearrange("p (a b) -> p a b", b=64)
            q3 = rr.ysq[:, :].rearrange("p (a b) -> p a b", b=64)
            st = rr.st
            p.op('act', I('copy', rr.ysb[:, :], yacc[:, 0:128]), reads=[yacc], writes=[rr.ysb])
            p.op('act', I('activation', rr.ysq[:, :], rr.ysb[:, :], AF.Square), reads=[rr.ysb], writes=[rr.ysq])
            if tb == 0:
                dump("ysb", rr.ysb, (slice(None), slice(None)))
            p.op('dve', I('reduce_sum', st["s1"][:, :], y3, AX.X), reads=[rr.ysb], writes=[st["s1"]])
            p.op('dve', I('reduce_sum', st["s2"][:, :], q3, AX.X), reads=[rr.ysq], writes=[st["s2"]])
            p.op('dve', I('tensor_scalar_mul', st["m"][:, :], st["s1"][:, :], 1.0 / 64), reads=[st["s1"]], writes=[st["m"]])
            p.op('dve', I('tensor_tensor', st["msq"][:, :], st["m"][:, :], st["m"][:, :], ALU.mult), reads=[st["m"]], writes=[st["msq"]])
            p.op('dve', I('scalar_tensor_tensor', st["var"][:, :], st["s2"][:, :], 1.0 / 64, st["msq"][:, :], ALU.mult, ALU.subtract),
                 reads=[st["s2"], st["msq"]], writes=[st["var"]])
            p.op('act', I('activation', st["sd"][:, :], st["var"][:, :], AF.Sqrt, bias=rr.cst[:, 1:2]),
                 reads=[st["var"], rr.cst], writes=[st["sd"]])
            p.op('dve', I('reciprocal', st["rstd"][:, :], st["sd"][:, :]), reads=[st["sd"]], writes=[st["rstd"]])
            n3 = rr.yn[:, :].rearrange("p (a b) -> p a b", b=64)
            p.op('dve', I('tensor_tensor', n3, y3, st["m"][:, :].unsqueeze(2).to_broadcast([128, 2, 64]), ALU.subtract),
                 reads=[rr.ysb, st["m"]], writes=[rr.yn])
            p.op('dve', I('tensor_tensor', n3, n3, st["rstd"][:, :].unsqueeze(2).to_broadcast([128, 2, 64]), ALU.mult),
                 reads=[rr.yn, st["rstd"]], writes=[rr.yn])
            p.op('pe', I('transpose', ybank[:, 384:512], rr.yn[:, :], cx.ident[:, :]), reads=[rr.yn, cx.ident], writes=[ybank])
            p.op('act', I('activation', rr.yo[:, :], ybank[:, 384:512], AF.Identity, bias=prm("gn_b", g), scale=prm("gn_g", g)),
                 reads=[ybank, rr.prm], writes=[rr.yo])
            p.op('dve', I('tensor_tensor', rr.yo[:, :], rr.yo[:, :], BON[:, ts_], ALU.add), reads=[rr.yo, BON], writes=[rr.yo])
            p.op('dve', I('tensor_tensor', YB[:, ts_], rr.yo[:, :], G[:, ts_], ALU.mult), reads=[rr.yo, G], writes=[YB])
        p.dma('sp', ysT[row0 + g * 128:row0 + (g + 1) * 128, :], YB[:, :], reads=[YB], writes=[ys_t],
              is_output=(ys_t.name == "EXT"))
    ROUND_F32R[0] = False


def wview(w_ap):
    return w_ap.rearrange("(c p) n -> p c n", p=128)


def bfv(sl, i0, nslab, a, b_):
    per = 4096 // b_
    out = []
    for j in range(nslab):
        vw = sl.bf(i0 + j, cols=per * b_)
        out.append(vw.derive(vw.ap.rearrange("p (a b) -> p a b", b=b_)))
    return out, per


class KView:
    def __init__(self, sl, i0, KC, T):
        self.views, self.per = bfv(sl, i0, (KC * T + 4095) // 4096, KC, T)
        self.KC = KC

    def k(self, k):
        v = self.views[k // self.per]
        return v, v[:, k % self.per, :]

    def tiles(self):
        return self.views


def load_kview(p, kv, src_T, c0, T, src_t, q='pool'):
    sv = src_T.rearrange("(c p) t -> p c t", p=128)
    for j, v in enumerate(kv.views):
        k0 = j * kv.per
        k1 = min(kv.KC, k0 + kv.per)
        p.dma(q, v[:, 0:k1 - k0, :], sv[:, k0:k1, c0:c0 + T], reads=[src_t], writes=[v])


def proj_stage(p, cx, sl, xT, xT_t, w, projT, proj_t, v_tm, vtm_t):
    pb = cx.pb
    KC = 16
    wv = wview(w)
    pi = 0
    for tt in range(2):
        t0 = tt * 1024
        xs = KView(sl, 0, KC, 1024)
        load_kview(p, xs, xT, t0, 1024, xT_t)
        wb = []
        for i in range(3):
            a, b_ = sl.bf(4 + 2 * i), sl.bf(5 + 2 * i)
            wb.append([a.derive(a.ap.rearrange("p (c n) -> p c n", n=512)),
                       b_.derive(b_.ap.rearrange("p (c n) -> p c n", n=512))])
        ob = [sl.f32(10, cols=1024, off=0), sl.f32(10, cols=1024, off=1024), sl.f32(11, cols=1024, off=0)]
        oi = 0
        for nb in range(NCAT // 512):
            wt = wb[nb % 3]
            for hf in range(2):
                p.dma('pool', wt[hf][:, :, :], wv[:, hf * 8:(hf + 1) * 8, nb * 512:(nb + 1) * 512], writes=[wt[hf]])
            if nb in (4, 5):
                for tb in range(8):
                    ps = pb[pi % 6]
                    pi += 1
                    for k in range(KC):
                        xv, xa = xs.k(k)
                        p.op('pe', mm(ps[:, :], xa[:, tb * 128:(tb + 1) * 128], wt[k // 8][:, k % 8, :], k == 0, k == KC - 1),
                             reads=[xv, wt[k // 8]], writes=[ps])
                    o = ob[oi % 3]
                    oi += 1
                    p.op('act', I('copy', o[:, 0:512], ps[:, :]), reads=[ps], writes=[o])
                    p.dma('sp', v_tm[t0 + tb * 128:t0 + (tb + 1) * 128, (nb - 4) * 512:(nb - 3) * 512], o[:, 0:512],
                          reads=[o], writes=[vtm_t])
                continue
            for j in range(4):
                n = nb * 4 + j
                o = ob[oi % 3]
                oi += 1
                for th in range(2):
                    ps = pb[pi % 6]
                    pi += 1
                    for k in range(KC):
                        xv, xa = xs.k(k)
                        p.op('pe', mm(ps[:, :], wt[k // 8][:, k % 8, j * 128:(j + 1) * 128], xa[:, th * 512:(th + 1) * 512],
                                      k == 0, k == KC - 1), reads=[xv, wt[k // 8]], writes=[ps])
                    osl = o[:, th * 512:(th + 1) * 512]
                    if n >= GATE_CHUNK0:
                        p.op('act', I('activation', osl, ps[:, :], AF.Sigmoid), reads=[ps], writes=[o])
                    elif th == 0:
                        p.op('act', I('copy', osl, ps[:, :]), reads=[ps], writes=[o])
                    else:
                        p.op('dve', I('tensor_copy', osl, ps[:, :]), reads=[ps], writes=[o])
                p.dma('sp', projT[n * 128:(n + 1) * 128, t0:t0 + 1024], o[:, :], reads=[o], writes=[proj_t])


class LnRes:
    def __init__(self, p):
        self.st = {n: p.sb([128, 1], F32, "ln_" + n) for n in ("s1", "s2", "m", "msq", "var", "sd", "rstd")}
        self.eps = p.sb([128, 1], F32, "ln_eps")
        p.op('pool', I('memset', self.eps[:, :], LN_EPS), writes=[self.eps])


def resid_ln(p, cx, sl, lr, f_tm, f_t, x_tm, x_t, g_d, b_d, out_tm, out_t, outT, outT_t, final=False):
    pb = cx.pb
    G, Bv = sl.f32(0), sl.f32(1)
    p.dma('sp', G[:, :], g_d.partition_broadcast(128), writes=[G])
    p.dma('sp', Bv[:, :], b_d.partition_broadcast(128), writes=[Bv])
    st = lr.st
    for tb in range(S // 128):
        rs_ = slice(tb * 128, (tb + 1) * 128)
        X, F, Y, Q = sl.f32(2 + (tb % 2)), sl.f32(4 + (tb % 2)), sl.f32(6 + (tb % 2)), sl.f32(8)
        p.dma('sp', X[:, :], x_tm[rs_, :], reads=[x_t], writes=[X])
        p.dma('act', F[:, :], f_tm[rs_, :], reads=[f_t], writes=[F])
        p.op('dve', I('scalar_tensor_tensor', Y[:, :], X[:, :], ALPHA, F[:, :], ALU.mult, ALU.add), reads=[X, F], writes=[Y])
        p.op('dve', I('reduce_sum', st["s1"][:, :], Y[:, :], AX.X), reads=[Y], writes=[st["s1"]])
        p.op('act', I('activation', Q[:, :], Y[:, :], AF.Square), reads=[Y], writes=[Q])
        p.op('dve', I('reduce_sum', st["s2"][:, :], Q[:, :], AX.X), reads=[Q], writes=[st["s2"]])
        p.op('dve', I('tensor_scalar_mul', st["m"][:, :], st["s1"][:, :], 1.0 / D), reads=[st["s1"]], writes=[st["m"]])
        p.op('dve', I('tensor_tensor', st["msq"][:, :], st["m"][:, :], st["m"][:, :], ALU.mult), reads=[st["m"]], writes=[st["msq"]])
        p.op('dve', I('scalar_tensor_tensor', st["var"][:, :], st["s2"][:, :], 1.0 / D, st["msq"][:, :], ALU.mult, ALU.subtract),
             reads=[st["s2"], st["msq"]], writes=[st["var"]])
        p.op('act', I('activation', st["sd"][:, :], st["var"][:, :], AF.Sqrt, bias=lr.eps[:, 0:1]), reads=[st["var"], lr.eps], writes=[st["sd"]])
        p.op('dve', I('reciprocal', st["rstd"][:, :], st["sd"][:, :]), reads=[st["sd"]], writes=[st["rstd"]])
        p.op('dve', I('tensor_scalar', Y[:, :], Y[:, :], st["m"][:, 0:1], st["rstd"][:, 0:1], ALU.subtract, ALU.mult),
             reads=[Y, st["m"], st["rstd"]], writes=[Y])
        p.op('pool', I('tensor_tensor', Y[:, :], Y[:, :], G[:, :], ALU.mult), reads=[Y, G], writes=[Y])
        p.op('dve', I('tensor_tensor', Y[:, :], Y[:, :], Bv[:, :], ALU.add), reads=[Y, Bv], writes=[Y])
        p.dma('sp', out_tm[rs_, :], Y[:, :], reads=[Y], writes=[out_t], is_output=final)
        if outT is not None:
            OT = sl.f32(9 + (tb % 2))
            o3 = OT[:, :].rearrange("p (c t) -> p c t", t=128)
            for c4 in range(4):
                ps = pb[c4 % 4]
                for j in range(4):
                    c = c4 * 4 + j
                    p.op('pe', I('transpose', ps[:, j * 128:(j + 1) * 128], Y[:, c * 128:(c + 1) * 128], cx.ident[:, :]),
                         reads=[Y, cx.ident], writes=[ps])
                p.op('act' if c4 % 2 == 0 else 'dve', I('copy' if c4 % 2 == 0 else 'tensor_copy', o3[:, c4 * 4:(c4 + 1) * 4, :],
                                                        ps[:, :].rearrange("p (c t) -> p c t", t=128)), reads=[ps], writes=[OT])
            p.dma('act', outT.rearrange("(c p) t -> p c t", p=128)[:, :, rs_], o3, reads=[OT], writes=[outT_t])


def mix_stage(p, cx, sl, ysT, ys_t, projT, proj_t, bo, wo, f_tm, f_t):
    pb = cx.pb
    pi = 0
    for tt in range(2):
        t0 = tt * 1024
        ys = KView(sl, 0, 24, 1024)
        load_kview(p, ys, ysT, t0, 1024, ys_t)
        acc = KView(sl, 6, 16, 1024)
        for db in range(4):
            wts = []
            for n in range(3):
                vw = sl.bf(10 + (db % 2) * 3 + n)
                wt = vw.derive(vw.ap.rearrange("p (c n) -> p c n", n=512))
                p.dma('pool', wt[:, :, :], wview(bo[n])[:, :, db * 512:(db + 1) * 512], writes=[wt])
                wts.append(wt)
            for j in range(4):
                dc = db * 4 + j
                av, aa = acc.k(dc)
                for th in range(2):
                    cs = slice(th * 512, (th + 1) * 512)
                    gsl = sl.f32(16 + (pi % 2))
                    g3 = gsl.derive(gsl.ap[:, 0:1536].rearrange("p (n t) -> p n t", t=512))
                    tmp = gsl.derive(gsl.ap[:, 1536:2048])
                    for n in range(3):
                        r0 = ROW_GATE + n * 2048 + dc * 128
                        p.dma('sp', g3[:, n, :], projT[r0:r0 + 128, t0 + th * 512:t0 + (th + 1) * 512], reads=[proj_t], writes=[g3])
                    pss = []
                    for n in range(3):
                        ps = pb[pi % 8]
                        pi += 1
                        for k in range(8):
                            yv, ya = ys.k(n * 8 + k)
                            p.op('pe', mm(ps[:, :], wts[n][:, k, j * 128:(j + 1) * 128], ya[:, cs], k == 0, k == 7),
                                 reads=[wts[n], yv], writes=[ps])
                        pss.append(ps)
                    p.op('dve', I('tensor_tensor', tmp[:, :], pss[0][:, :], g3[:, 0, :], ALU.mult), reads=[pss[0], g3], writes=[tmp])
                    p.op('dve', I('tensor_tensor', g3[:, 1, :], pss[1][:, :], g3[:, 1, :], ALU.mult), reads=[pss[1], g3], writes=[g3])
                    p.op('dve', I('tensor_tensor', g3[:, 2, :], pss[2][:, :], g3[:, 2, :], ALU.mult), reads=[pss[2], g3], writes=[g3])
                    p.op('pool', I('tensor_tensor', tmp[:, :], tmp[:, :], g3[:, 1, :], ALU.add), reads=[tmp, g3], writes=[tmp])
                    p.op('dve', I('tensor_tensor', aa[:, cs], tmp[:, :], g3[:, 2, :], ALU.add), reads=[tmp, g3], writes=[av])
        wos = KView(sl, 10, 16, 2048)
        wov = wview(wo)
        for jx, v in enumerate(wos.views):
            k0 = jx * wos.per
            p.dma('pool', v[:, 0:wos.per, :], wov[:, k0:k0 + wos.per, :], writes=[v])
        for tb in range(8):
            o = sl.f32(18 + (tb % 2))
            for c4 in range(4):
                ps = pb[pi % 8]
                pi += 1
                for k in range(16):
                    av, aa = acc.k(k)
                    wv_, wa = wos.k(k)
                    p.op('pe', mm(ps[:, :], aa[:, tb * 128:(tb + 1) * 128], wa[:, c4 * 512:(c4 + 1) * 512], k == 0, k == 15),
                         reads=[av, wv_], writes=[ps])
                p.op('act' if c4 % 2 == 0 else 'dve', I('copy' if c4 % 2 == 0 else 'tensor_copy', o[:, c4 * 512:(c4 + 1) * 512], ps[:, :]),
                     reads=[ps], writes=[o])
            p.dma('sp', f_tm[t0 + tb * 128:t0 + (tb + 1) * 128, :], o[:, :], reads=[o], writes=[f_t])


class MoeRes:
    def __init__(self, p):
        self.lg = p.sb([128, 8], F32, "mo_lg")
        self.mx = p.sb([128, 8], F32, "mo_mx")
        self.sel = p.sb([128, 8], F32, "mo_sel")
        self.e = p.sb([128, 8], F32, "mo_e")
        self.d = p.sb([128, 1], F32, "mo_d")
        self.comb = p.sb([128, 4, 8], F32, "mo_comb")
        self.rt = p.sb([128, 16, 8], F32, "mo_rt")
        self.combT = p.sb([8, 512], F32, "mo_combT")
        self.selE = None


def ffn_stage(p, cx, sl, mo, xT, xT_t, experts, router, f_tm, f_t):
    pb = cx.pb
    pi = 0
    moe = router is not None
    if moe:
        p.dma('sp', mo.rt[:, :, :], router.rearrange("(c p) e -> p c e", p=128), writes=[mo.rt])
        if mo.selE is None:
            mo.selE = cx.load_const("selE", [8, 1024], F32, tag="_mo")
    for tt in range(4):
        t0 = tt * 512
        xs = KView(sl, 0, 16, 512)
        load_kview(p, xs, xT, t0, 512, xT_t)
        hT = KView(sl, 2, 44, 512)
        accT = [sl.f32(16 + c // 4, cols=512, off=(c % 4) * 512) for c in range(16)]
        CB = sl.f32(20, cols=512, off=1024)
        STMP = [sl.f32(20, cols=512, off=1536), sl.f32(14, cols=512, off=0)]
        if moe:
            for tb in range(4):
                xf = sl.f32(19)
                x3 = xf.derive(xf.ap.rearrange("p (c t) -> p c t", t=128))
                p.dma('sp', x3[:, :, :], xT.rearrange("(c p) t -> p c t", p=128)[:, :, t0 + tb * 128:t0 + (tb + 1) * 128],
                      reads=[xT_t], writes=[x3])
                ps = pb[pi % 8]
                pi += 1
                for k in range(16):
                    p.op('pe', mm(ps[:, 0:8], x3[:, k, :], mo.rt[:, k, :], k == 0, k == 15), reads=[x3, mo.rt], writes=[ps])
                p.op('dve', I('tensor_copy', mo.lg[:, :], ps[:, 0:8]), reads=[ps], writes=[mo.lg])
                p.op('dve', I('max', mo.mx[:, :], mo.lg[:, :]), reads=[mo.lg], writes=[mo.mx])
                p.op('dve', I('tensor_scalar', mo.sel[:, :], mo.lg[:, :], mo.mx[:, 1:2], None, ALU.is_ge), reads=[mo.lg, mo.mx], writes=[mo.sel])
                p.op('dve', I('tensor_scalar', mo.e[:, :], mo.lg[:, :], mo.mx[:, 0:1], None, ALU.subtract), reads=[mo.lg, mo.mx], writes=[mo.e])
                p.op('act', I('activation', mo.e[:, :], mo.e[:, :], AF.Exp), reads=[mo.e], writes=[mo.e])
                p.op('dve', I('tensor_tensor', mo.d[:, :], mo.mx[:, 1:2], mo.mx[:, 0:1], ALU.subtract), reads=[mo.mx], writes=[mo.d])
                p.op('act', I('activation', mo.d[:, :], mo.d[:, :], AF.Exp), reads=[mo.d], writes=[mo.d])
                p.op('dve', I('tensor_scalar_add', mo.d[:, :], mo.d[:, :], 1.0), reads=[mo.d], writes=[mo.d])
                p.op('dve', I('reciprocal', mo.d[:, :], mo.d[:, :]), reads=[mo.d], writes=[mo.d])
                p.op('dve', I('tensor_tensor', mo.e[:, :], mo.e[:, :], mo.sel[:, :], ALU.mult), reads=[mo.e, mo.sel], writes=[mo.e])
                p.op('dve', I('tensor_scalar_mul', mo.comb[:, tb, :], mo.e[:, :], mo.d[:, 0:1]), reads=[mo.e, mo.d], writes=[mo.comb])
            pst = pb[pi % 8]
            pi += 1
            for tb in range(4):
                p.op('pe', I('transpose', pst[0:8, tb * 128:(tb + 1) * 128], mo.comb[:, tb, :], cx.ident[:, :]),
                     reads=[mo.comb, cx.ident], writes=[pst])
            p.op('dve', I('tensor_copy', mo.combT[0:8, :], pst[0:8, :]), reads=[pst], writes=[mo.combT])
        for ei, (wg, wu, wd) in enumerate(experts):
            wgv, wuv, wdv = wview(wg), wview(wu), wview(wd)
            for fb in range(DFF // 256):
                wts = []
                for wi, wvv in enumerate((wgv, wuv)):
                    vw = sl.bf(8 + (fb % 2) * 2 + wi)
                    wt = vw.derive(vw.ap.rearrange("p (c n) -> p c n", n=256))
                    p.dma('pool', wt[:, :, :], wvv[:, :, fb * 256:(fb + 1) * 256], writes=[wt])
                    wts.append(wt)
                for j in range(2):
                    fc = fb * 2 + j
                    psg, psu = pb[pi % 8], pb[(pi + 1) % 8]
                    pi += 2
                    for wi, ps in enumerate((psg, psu)):
                        for k in range(16):
                            xv, xa = xs.k(k)
                            p.op('pe', mm(ps[:, :], wts[wi][:, k, j * 128:(j + 1) * 128], xa[:, :], k == 0, k == 15),
                                 reads=[wts[wi], xv], writes=[ps])
                    tmpv = sl.f32(20, cols=512, off=(fc % 2) * 512)
                    hv, ha = hT.k(fc)
                    p.op('act', I('activation', tmpv[:, :], psg[:, :], AF.Silu), reads=[psg], writes=[tmpv])
                    p.op('dve', I('tensor_tensor', ha[:, :], tmpv[:, :], psu[:, :], ALU.mult), reads=[tmpv, psu], writes=[hv])
            if moe:
                psb = pb[pi % 8]
                pi += 1
                p.op('pe', mm(psb[:, :], mo.selE[0:8, ei * 128:(ei + 1) * 128], mo.combT[0:8, :], True, True),
                     reads=[mo.selE, mo.combT], writes=[psb])
                p.op('act', I('copy', CB[:, :], psb[:, :]), reads=[psb], writes=[CB])
            for half in range(2):
                for k in range(44):
                    rb = sl.bf(12 + (k % 8) // 4, cols=1024, off=(k % 4) * 1024)
                    p.dma('pool', rb[:, :], wd[k * 128:(k + 1) * 128, half * 1024:(half + 1) * 1024], writes=[rb])
                    hv, ha = hT.k(k)
                    for j in range(8):
                        p.op('pe', mm(pb[j][:, :], rb[:, j * 128:(j + 1) * 128], ha[:, :], k == 0, k == 43),
                             reads=[rb, hv], writes=[pb[j]])
                for j in range(8):
                    ps = pb[j]
                    av = accT[half * 8 + j]
                    if not moe:
                        p.op('act' if j % 2 == 0 else 'dve', I('copy' if j % 2 == 0 else 'tensor_copy', av[:, :], ps[:, :]),
                             reads=[ps], writes=[av])
                    elif ei == 0:
                        p.op('dve', I('tensor_tensor', av[:, :], ps[:, :], CB[:, :], ALU.mult), reads=[ps, CB], writes=[av])
                    else:
                        st_ = STMP[j % 2]
                        p.op('dve', I('tensor_tensor', st_[:, :], ps[:, :], CB[:, :], ALU.mult), reads=[ps, CB], writes=[st_])
                        p.op('pool', I('tensor_tensor', av[:, :], av[:, :], st_[:, :], ALU.add), reads=[av, st_], writes=[av])
        for tb in range(4):
            o = sl.f32(8 + (tb % 2))
            for c4 in range(4):
                ps = pb[pi % 8]
                pi += 1
                for j in range(4):
                    c = c4 * 4 + j
                    p.op('pe', I('transpose', ps[:, j * 128:(j + 1) * 128], accT[c][:, tb * 128:(tb + 1) * 128], cx.ident[:, :]),
                         reads=[accT[c], cx.ident], writes=[ps])
                p.op('act' if c4 % 2 == 0 else 'dve', I('copy' if c4 % 2 == 0 else 'tensor_copy', o[:, c4 * 512:(c4 + 1) * 512], ps[:, :]),
                     reads=[ps], writes=[o])
            p.dma('sp', f_tm[t0 + tb * 128:t0 + (tb + 1) * 128, :], o[:, :], reads=[o], writes=[f_t])


N_ACTIVE = 4


def build_full():
    p = Prog()
    ct = const_tables()
    ct.update(rwkv_tables())
    cd = {n: p.dram("c_" + n, a.shape, F32, "ExternalInput") for n, a in ct.items()}
    dr = lambda n, sh, kind="ExternalInput": p.dram(n, sh, F32, kind)
    sc = lambda n, sh: p.nc.dram_tensor(n, list(sh), F32).ap()
    x_tm = dr("x_tm", [S, D])
    xT = dr("xT", [D, S])
    out = dr("out", [S, D], "ExternalOutput")
    L = []
    for l in range(2):
        d = {}
        d["wcat"] = dr(f"wcat{l}", [D, NCAT])
        for n, sh in dict(mu=[3360], w0=[1024], a0=[1024], k_k=[1024], k_a=[1024], r_k=[1024], gn_g=[1024], gn_b=[1024],
                          w_lora=[64, 1024], a_lora=[64, 1024], g_lora=[160, 1024]).items():
            d[n] = dr(f"rw{l}_{n}", sh)
        d["mu_v"] = dr(f"rw{l}_mu_v", [32])
        d["v0"] = dr(f"rw{l}_v0", [1024])
        d["v_lora"] = dr(f"rw{l}_v_lora", [32, 1024])
        d["qn"] = dr(f"qn{l}", [512])
        d["kvn"] = dr(f"kvn{l}", [512])
        d["w_uq"] = dr(f"w_uq{l}", [512, 1536])
        d["w_ukv"] = dr(f"w_ukv{l}", [512, 2048])
        d["bo"] = [dr(f"bo{l}_{n}", [1024, D]) for n in range(3)]
        d["wo"] = dr(f"wo{l}", [D, D])
        for n in ("ln1_g", "ln1_b", "ln2_g", "ln2_b"):
            d[n] = dr(f"{n}{l}", [D])
        L.append(d)
    ffn = (dr("ffn_wg", [D, DFF]), dr("ffn_wu", [D, DFF]), dr("ffn_wd", [DFF, D]))
    router = dr("router", [D, NE])
    moe = [(dr(f"moe_wg{e}", [D, DFF]), dr(f"moe_wu{e}", [D, DFF]), dr(f"moe_wd{e}", [DFF, D])) for e in range(NE)]
    projT, v_tm, ysT, vfT = sc("projT", [NCAT, S]), sc("v_tm", [S, 1024]), sc("ysT", [3072, S]), sc("vfT", [1024, S])
    f_tm = sc("f_tm", [S, D])
    x1, x1T, x2, x2T = sc("x1", [S, D]), sc("x1T", [D, S]), sc("x2", [S, D]), sc("x2T", [D, S])
    T = lambda n: Tile(None, n)
    t_in, t_proj, t_v, t_ys, t_vf, t_f = T("in"), T("proj"), T("vtm"), T("ys"), T("vf"), T("f")
    t_x1, t_x1T, t_x2, t_x2T, t_out = T("x1"), T("x1T"), T("x2"), T("x2T"), T("EXT")

    cx = Ctx(p, cd)
    sl = Slabs(p)
    ar = AttnRes(p, cx, sl)
    mr = MlaRes(p, sl)
    rr = RwkvRes(p, cx, sl)
    lr = LnRes(p)
    mo = MoeRes(p)
    print("sbuf remaining", p.nc.sbuf_bytes_remaining)

    cur_tm, cur_t, curT, curT_t = x_tm, t_in, xT, t_in
    for l in range(2):
        d = L[l]
        proj_stage(p, cx, sl, curT, curT_t, d["wcat"], projT, t_proj, v_tm, t_v)
        p.dma('sp', ar.cos[:, :], cd["cosA"], writes=[ar.cos])
        p.dma('sp', ar.sin[:, :], cd["sinA"], writes=[ar.sin])
        for h in range(8):
            moba_head(p, cx, ar, ar.cos, ar.sin, projT[h * 128:(h + 1) * 128, :], projT[1024 + h * 128:1024 + (h + 1) * 128, :],
                      v_tm[:, h * 128:(h + 1) * 128], t_proj, ysT[h * 128:(h + 1) * 128, :], t_ys, v_tile=t_v)
        rwkv_stage(p, cx, rr, projT, t_proj, d, ysT, t_ys, 1024, vfT, t_vf, l > 0)
        mla_stage(p, cx, ar, mr, projT, t_proj, d["w_uq"], d["w_ukv"], d["qn"], d["kvn"], cd["cosM"], cd["sinM"], ysT, t_ys, 2048)
        mix_stage(p, cx, sl, ysT, t_ys, projT, t_proj, d["bo"], d["wo"], f_tm, t_f)
        resid_ln(p, cx, sl, lr, f_tm, t_f, cur_tm, cur_t, d["ln1_g"], d["ln1_b"], x1, t_x1, x1T, t_x1T)
        if l == 0:
            ffn_stage(p, cx, sl, mo, x1T, t_x1T, [ffn], None, f_tm, t_f)
            resid_ln(p, cx, sl, lr, f_tm, t_f, x1, t_x1, d["ln2_g"], d["ln2_b"], x2, t_x2, x2T, t_x2T)
            cur_tm, cur_t, curT, curT_t = x2, t_x2, x2T, t_x2T
        else:
            ffn_stage(p, cx, sl, mo, x1T, t_x1T, moe, router, f_tm, t_f)
            resid_ln(p, cx, sl, lr, f_tm, t_f, x1, t_x1, d["ln2_g"], d["ln2_b"], out, t_out, None, None, final=True)
    print("instr", {k: len(v) for k, v in p.ops.items()})
    return p.build(), ct


def kernel(**inp):
    f = lambda a: np.ascontiguousarray(np.asarray(a, dtype=np.float32))
    nc, ct = _get_full()
    shared = {"c_" + n: a for n, a in ct.items()}
    for l in range(2):
        shared[f"wcat{l}"] = make_wcat(f(inp["w_in"][l]), f(inp["w_in_vres"][l - 1]) if l > 0 else None)
        for n in ("w0", "a0", "k_k", "k_a", "gn_g", "gn_b", "w_lora", "a_lora", "g_lora"):
            shared[f"rw{l}_{n}"] = f(inp["rwkv_" + n][l])
        shared[f"rw{l}_mu"] = f(inp["rwkv_mu"][l])
        shared[f"rw{l}_r_k"] = f(inp["rwkv_r_k"][l]).reshape(1024)
        if l > 0:
            shared[f"rw{l}_mu_v"] = f(inp["rwkv_mu_vres"][l - 1])
            shared[f"rw{l}_v0"] = f(inp["rwkv_v0"][l - 1])
            shared[f"rw{l}_v_lora"] = f(inp["rwkv_v_lora"][l - 1])
        else:
            shared[f"rw{l}_mu_v"] = np.zeros(32, np.float32)
            shared[f"rw{l}_v0"] = np.zeros(1024, np.float32)
            shared[f"rw{l}_v_lora"] = np.zeros((32, 1024), np.float32)
        shared[f"qn{l}"] = f(inp["mla_q_norm"][l])
        shared[f"kvn{l}"] = f(inp["mla_kv_norm"][l])
        shared[f"w_uq{l}"] = f(inp["mla_w_uq"][l])
        shared[f"w_ukv{l}"] = f(inp["mla_w_ukv"][l])
        for n in range(3):
            shared[f"bo{l}_{n}"] = f(inp["branch_out"][l][n])
        shared[f"wo{l}"] = f(inp["w_out"][l])
        for n in ("ln1_g", "ln1_b", "ln2_g", "ln2_b"):
            shared[f"{n}{l}"] = f(inp[n][l])
    shared["ffn_wg"], shared["ffn_wu"], shared["ffn_wd"] = f(inp["ffn_wg"][0]), f(inp["ffn_wu"][0]), f(inp["ffn_wd"][0])
    shared["router"] = f(inp["moe_router"][0])
    for e in range(NE):
        shared[f"moe_wg{e}"] = f(inp["moe_wg"][0][e])
        shared[f"moe_wu{e}"] = f(inp["moe_wu"][0][e])
        shared[f"moe_wd{e}"] = f(inp["moe_wd"][0][e])
    x = f(inp["x"])
    in_maps = []
    for b in range(N_ACTIVE):
        m = dict(shared)
        m["x_tm"] = x[b]
        m["xT"] = np.ascontiguousarray(x[b].T)
        in_maps.append(m)
    res = run_bass_kernel_spmd(nc, in_maps, core_ids=list(range(N_ACTIVE)))
    return np.stack([res.results[b]["out"] for b in range(N_ACTIVE)], 0).astype(np.float32)


_FULL = []


def _get_full():
    if not _FULL:
        _FULL.append(build_full())
    return _FULL[0]
```

```python
import contextlib
import numpy as np
import concourse.bass as bass
import concourse.mybir as mybir
from concourse.bass_utils import run_bass_kernel_spmd

F32 = mybir.dt.float32
BF16 = mybir.dt.bfloat16
F32R = mybir.dt.float32r
AF = mybir.ActivationFunctionType
ALU = mybir.AluOpType
AX = mybir.AxisListType

COMPUTE = ('pe', 'act', 'dve', 'pool')
DMA_RING = 6


class Tile:
    def __init__(self, h, name):
        self.h = h
        self.name = name
        self.w = {}
        self.r = {}

    def __getitem__(self, k):
        return self.h[k]

    def all_w(self):
        return self.w.items()

    def all_r(self):
        return self.r.items()


class Prog:
    def __init__(self, name="k"):
        self.nc = bass.Bass("TRN2", target_bir_lowering=False)
        self.es = contextlib.ExitStack()
        self.ops = {e: [] for e in COMPUTE + ('sp',)}
        self.ncomp = {e: 0 for e in COMPUTE}
        self.waited = {e: {} for e in COMPUTE + ('sp',)}
        self.sems = {}
        self.dma_n = {}
        self.ntile = 0
        self.out_tokens = []
        for e in COMPUTE:
            self.sems[e] = self.es.enter_context(self.nc.semaphore("s_" + e))

    def dram(self, name, shape, dt, kind):
        return self.nc.dram_tensor(name, list(shape), dt, kind=kind).ap()

    def sb(self, shape, dt, name=None):
        self.ntile += 1
        name = name or f"t{self.ntile}"
        h = self.es.enter_context(self.nc.sbuf_tensor(name, list(shape), dt))
        return Tile(h, name)

    def ps(self, shape, dt=F32, name=None):
        self.ntile += 1
        name = name or f"p{self.ntile}"
        h = self.es.enter_context(self.nc.psum_tensor(name, list(shape), dt))
        return Tile(h, name)

    def _deps(self, eng, reads, writes, is_dma=False):
        deps = []
        for t in reads:
            for tok in t.all_w():
                deps.append((tok, True))
        for t in writes:
            for tok in t.all_w():
                if is_dma and tok[0].startswith("d_"):
                    continue
                deps.append((tok, False))
            for tok in t.all_r():
                deps.append((tok, False))
        need = {}
        for (sem, val), raw in deps:
            if sem == eng and not raw:
                continue
            if self.waited[eng].get(sem, 0) >= val:
                continue
            need[sem] = max(need.get(sem, 0), val)
        for sem, val in need.items():
            self.waited[eng][sem] = val
        return list(need.items())

    def _record(self, tok, reads, writes):
        sem, val = tok
        for t in reads:
            if t.r.get(sem, 0) < val:
                t.r[sem] = val
        for t in writes:
            if t.w.get(sem, 0) < val:
                t.w[sem] = val

    def op(self, eng, fn, reads=(), writes=()):
        waits = self._deps(eng, reads, writes)
        self.ncomp[eng] += 1
        tok = (eng, self.ncomp[eng])
        self.ops[eng].append((waits, fn, (eng, 1)))
        self._record(tok, reads, writes)
        return tok

    def dma(self, q, out, in_, reads=(), writes=(), is_output=False, **kw):
        n = self.dma_n.get(q, 0)
        self.dma_n[q] = n + 1
        slot = n % DMA_RING
        key = f"d_{q}{slot}"
        if key not in self.sems:
            self.sems[key] = self.es.enter_context(self.nc.semaphore(key))
        prev = 16 * (n // DMA_RING)
        waits = self._deps(q, reads, writes, is_dma=True)
        if prev > 0 and self.waited[q].get(key, 0) < prev:
            waits.append((key, prev))
            self.waited[q][key] = prev
        tok = (key, prev + 16)
        self.ops[q].append((waits, I('dma_start', out=out, in_=in_, **kw), (key, 16)))
        self._record(tok, reads, writes)
        if is_output:
            self.out_tokens.append(tok)
        return tok

    def build(self):
        nc = self.nc
        fin = {}
        for sem, val in self.out_tokens:
            fin[sem] = max(fin.get(sem, 0), val)
        ops = self.ops
        sems = self.sems

        def emit(engname):
            def f(e):
                for waits, fn, inc in ops[engname]:
                    for sem, val in waits:
                        e.wait_ge(sems[sem], val)
                    ins = fn(e)
                    if inc is not None:
                        ins.then_inc(sems[inc[0]], inc[1])
                if engname == 'sp':
                    for sem, val in fin.items():
                        e.wait_ge(sems[sem], val)
            return f

        with nc.allow_low_precision("fp32r (11-bit mantissa) rounding of single-pass matmul operands"), nc.Block() as block:
            block.sync(emit('sp'))
            block.tensor(emit('pe'))
            block.scalar(emit('act'))
            block.vector(emit('dve'))
            block.gpsimd(emit('pool'))
        self.es.close()
        return nc


D = 2048
S = 2048
B = 4
NTOK = 1024
BW = 1024
DFF = 5632
NE = 8
NCAT = 13824
GATE_CHUNK0 = 60
ALPHA = 4 ** 0.25
LN_EPS = 1e-5


ROUND_F32R = [False]


def I(method, *args, **kw):
    if ROUND_F32R[0] and method in ("activation", "copy", "tensor_copy", "tensor_tensor", "tensor_scalar", "tensor_scalar_mul", "scalar_tensor_tensor", "tensor_scalar_max", "reciprocal", "memset") and args:
        out = args[0]
        if out.dtype == F32 and type(out.tensor).__name__.startswith("SB"):
            args = (out.bitcast(F32R),) + tuple(args[1:])
    return lambda e: getattr(e, method)(*args, **kw)


def mm(ps_ap, lhsT, rhs, start, stop, r=False):
    if r and USE_F32R:
        lhsT = lhsT.bitcast(F32R)
        rhs = rhs.bitcast(F32R)
    return I('matmul', ps_ap, lhsT, rhs, start=start, stop=stop)


class Gemm:
    def __init__(self, p, KC, wcols=512, nbuf=3, q='pool'):
        self.p = p
        self.KC = KC
        self.wcols = wcols
        self.wb = [p.sb([128, KC, wcols], BF16) for _ in range(nbuf)]
        self.i = 0
        self.q = q

    def load(self, w_ap, c0, ncols):
        wt = self.wb[self.i % len(self.wb)]
        self.i += 1
        wv = w_ap.rearrange("(c p) n -> p c n", p=128)
        self.p.dma(self.q, wt[:, :, 0:ncols], wv[:, :, c0:c0 + ncols], writes=[wt])
        return wt


def build_proj():
    p = Prog()
    KC = D // 128
    xT = p.dram("xT", [D, NTOK], F32, "ExternalInput")
    w = p.dram("w", [D, NCAT], F32, "ExternalInput")
    oT = p.dram("oT", [NCAT, NTOK], F32, "ExternalOutput")
    xs = p.sb([128, KC, NTOK], BF16, "xs")
    xv = xT.rearrange("(c p) t -> p c t", p=128)
    for i in range(4):
        p.dma('pool', xs[:, i * 4:(i + 1) * 4, :], xv[:, i * 4:(i + 1) * 4, :], writes=[xs])
    g = Gemm(p, KC)
    pss = [p.ps([128, 512], F32) for _ in range(6)]
    ob = [p.sb([128, NTOK], F32) for _ in range(3)]
    pi = 0
    for nb in range(NCAT // 512):
        wt = g.load(w, nb * 512, 512)
        for j in range(4):
            n = nb * 4 + j
            o = ob[n % 3]
            for th in range(NTOK // 512):
                ps = pss[pi % 6]
                pi += 1
                for k in range(KC):
                    p.op('pe', mm(ps[:, :], wt[:, k, j * 128:(j + 1) * 128], xs[:, k, th * 512:(th + 1) * 512],
                                  k == 0, k == KC - 1), reads=[wt, xs], writes=[ps])
                osl = o[:, th * 512:(th + 1) * 512]
                if n >= GATE_CHUNK0:
                    p.op('act', I('activation', osl, ps[:, :], AF.Sigmoid),
                         reads=[ps], writes=[o])
                elif th == 0:
                    p.op('act', I('copy', osl, ps[:, :]), reads=[ps], writes=[o])
                else:
                    p.op('dve', I('tensor_copy', osl, ps[:, :]), reads=[ps], writes=[o])
            p.dma('sp', oT[n * 128:(n + 1) * 128, :], o[:, :], reads=[o], is_output=True)
    return p.build()


_W_IN_COLS = 13664


def make_wcat(w_in_l, w_vres_l):
    z = lambda n: np.zeros((D, n), np.float32)
    segs = [
        w_in_l[:, 0:3072],
        w_in_l[:, 3072:6144],
        w_in_l[:, 6144:6432],
        w_vres_l if w_vres_l is not None else z(32),
        z(64),
        w_in_l[:, 6432:7520],
        z(64),
        w_in_l[:, 7520:13664],
    ]
    return np.ascontiguousarray(np.concatenate(segs, axis=1))


_NC_CACHE = {}


def get_nc(name, builder):
    if name not in _NC_CACHE:
        _NC_CACHE[name] = builder()
    return _NC_CACHE[name]


def launch(name, builder, in_maps):
    nc = get_nc(name, builder)
    res = run_bass_kernel_spmd(nc, in_maps, core_ids=list(range(len(in_maps))))
    return res.results


BIG = 30000.0


def const_tables():
    c = {}
    c["ident"] = np.eye(128, dtype=np.float32)
    k = np.arange(128)[:, None]
    q = np.arange(128)[None, :]
    c["tri"] = np.where(k <= q, 0.0, -BIG).astype(np.float32)
    sel = np.zeros((8, 8, 128), np.float32)
    for n in range(8):
        sel[n, n, :] = 1.0
    c["selE"] = sel.reshape(8, 8 * 128)
    pos = np.arange(S, dtype=np.float32)

    def tabs(dim):
        inv = (1.0 / (10000.0 ** (np.arange(0, dim, 2, dtype=np.float32) / dim))).astype(np.float32)
        ang = (pos[:, None] * inv[None, :]).astype(np.float32)
        co, si = np.cos(ang).astype(np.float32).T, np.sin(ang).astype(np.float32).T
        return (np.ascontiguousarray(np.concatenate([co, co], 0)),
                np.ascontiguousarray(np.concatenate([-si, si], 0)))
    c["cosA"], c["sinA"] = tabs(128)
    c["cosM"], c["sinM"] = tabs(64)
    gpen = np.zeros((16, 8), np.float32)
    own = np.zeros((16, 8), np.float32)
    for i in range(16):
        gpen[i, i // 2:] = -1e30
        own[i, i // 2] = 1.0
    c["gpen"] = np.ascontiguousarray(np.broadcast_to(gpen.reshape(1, 128), (128, 128)))
    c["ownm"] = np.ascontiguousarray(np.broadcast_to(own.reshape(1, 128), (128, 128)))
    return c


class Ctx:
    def __init__(self, p, cd):
        self.p = p
        self.cd = cd
        self.pb = [p.ps([128, 512], F32, f"pb{i}") for i in range(8)]
        self.ident = self.load_const("ident", [128, 128])
        self.ones = p.sb([128, 128], F32, "ones")
        self.ones_bf = p.sb([128, 128], BF16, "ones_bf")
        p.op('pool', I('memset', self.ones[:, :], 1.0), writes=[self.ones])
        p.op('pool', I('memset', self.ones_bf[:, :], 1.0), writes=[self.ones_bf])

    def load_const(self, name, shape, dt=F32, tag=""):
        t = self.p.sb(shape, dt, "k_" + name + tag + ("_bf" if dt == BF16 else ("_r" if dt == F32R else "")))
        src = self.cd[name]
        if dt == F32:
            self.p.dma('sp', t[:, :], src, writes=[t])
        else:
            self.p.dma('pool', t[:, :], src, writes=[t])
        return t


NSLAB = 21


class View:
    def __init__(self, tile, ap, rng, state=None):
        self.tile = tile
        self.ap = ap
        self.name = tile.name
        self.rng = rng
        if state is None:
            self.w, self.r = {}, {}
            tile.views.append(self)
        else:
            self.w, self.r = state

    def derive(self, ap):
        return View(self.tile, ap, self.rng, (self.w, self.r))

    def __getitem__(self, k):
        return self.ap[k]

    def _overlapping(self):
        r0, r1, b0, b1 = self.rng
        for v in self.tile.views:
            q0, q1, c0, c1 = v.rng
            if q0 < r1 and r0 < q1 and c0 < b1 and b0 < c1:
                yield v

    def all_w(self):
        for v in self._overlapping():
            yield from v.w.items()

    def all_r(self):
        for v in self._overlapping():
            yield from v.r.items()


class Slabs:
    def __init__(self, p):
        self.t = [p.sb([128, 2048], F32, f"slab{i}") for i in range(NSLAB)]
        for t in self.t:
            t.views = []
        self.cache = {}

    def _get(self, key, mk):
        if key not in self.cache:
            self.cache[key] = mk()
        return self.cache[key]

    def f32(self, i, rows=128, cols=2048, off=0):
        return self._get((i, "f", rows, cols, off),
                         lambda: View(self.t[i], self.t[i].h[0:rows, off:off + cols], (0, rows, off * 4, (off + cols) * 4)))

    def bf(self, i, rows=128, cols=4096, off=0):
        return self._get((i, "b", rows, cols, off),
                         lambda: View(self.t[i], self.t[i].h.bitcast(BF16)[0:rows, off:off + cols], (0, rows, off * 2, (off + cols) * 2)))


class AttnRes:
    def __init__(self, p, cx, sl):
        self.tri = cx.load_const("tri", [128, 128], BF16)
        self.selE = cx.load_const("selE", [8, 1024], BF16)
        self.ident_bf = cx.load_const("ident", [128, 128], BF16)
        self.RTb = p.sb([8, S], BF16, "a_RTb")
        self.gpen = cx.load_const("gpen", [128, 128])
        self.ownm = cx.load_const("ownm", [128, 128])
        self.raw = [sl.f32(0), sl.f32(1)]
        self.rot = [sl.f32(2), sl.f32(3)]
        self.t1 = sl.f32(4)
        self.t2 = sl.f32(5)
        self.qf = sl.f32(6)
        self.kf = sl.f32(7)
        self.qb = [sl.bf(8, cols=2048), sl.bf(8, cols=2048, off=2048)]
        self.kb = [sl.bf(9, cols=2048), sl.bf(9, cols=2048, off=2048)]
        vv = sl.bf(10, cols=2048)
        self.v = vv.derive(vv.ap.rearrange("p (b d) -> p b d", d=128))
        self.pt = [sl.bf(10, cols=512, off=2048 + 512 * i) for i in range(3)]
        self.krow = sl.f32(11, rows=1)
        self.RT = sl.f32(12, rows=8)
        self.rs = [sl.f32(13, cols=512, off=0), sl.f32(13, cols=512, off=512)]
        self.o = [sl.f32(13, cols=512, off=1024), sl.f32(13, cols=512, off=1536)]
        self.cos = sl.f32(14)
        self.sin = sl.f32(15)
        self.small = {n: p.sb([128, 128], F32, "a_s_" + n) for n in
                      ("qq", "nb", "gate", "mx", "sel", "R", "kmT", "km2", "kmbc", "qq2")}


def rope_fm(p, out_f, raw, rot, cos, sin, t1, t2, nrow=128):
    p.op('dve', I('tensor_tensor', t1[0:nrow, :], raw[0:nrow, :], cos[0:nrow, :], ALU.mult),
         reads=[raw, cos], writes=[t1])
    p.op('pool', I('tensor_tensor', t2[0:nrow, :], rot[0:nrow, :], sin[0:nrow, :], ALU.mult),
         reads=[rot, sin], writes=[t2])
    p.op('dve', I('tensor_tensor', out_f[0:nrow, :], t1[0:nrow, :], t2[0:nrow, :], ALU.add),
         reads=[t1, t2], writes=[out_f])


def attn_core(p, cx, ar, qparts, kparts, scale, out_dram, out_tile, gating):
    pb = cx.pb
    sm = ar.small
    sq = ar.t1
    first = True
    for (_, nr, qf) in qparts:
        p.op('act', I('activation', sq[0:nr, :], qf[0:nr, :], AF.Square),
             reads=[qf], writes=[sq])
        for i in range(16):
            p.op('pe', mm(pb[0][:, i:i + 1], sq[0:nr, i * 128:(i + 1) * 128], cx.ones[0:nr, 0:1],
                          True, True), reads=[sq, cx.ones], writes=[pb[0]])
        if first:
            p.op('dve', I('tensor_copy', sm["qq"][:, 0:16], pb[0][:, 0:16]), reads=[pb[0]], writes=[sm["qq"]])
        else:
            p.op('dve', I('tensor_tensor', sm["qq"][:, 0:16], sm["qq"][:, 0:16], pb[0][:, 0:16], ALU.add),
                 reads=[pb[0], sm["qq"]], writes=[sm["qq"]])
        first = False
    sk = ar.t2
    for pi_, (_, nr, kf) in enumerate(kparts):
        p.op('act', I('activation', sk[0:nr, :], kf[0:nr, :], AF.Square),
             reads=[kf], writes=[sk])
        for c in range(4):
            p.op('pe', mm(pb[1 + c][0:1, :], cx.ones[0:nr, 0:1], sk[0:nr, c * 512:(c + 1) * 512],
                          True, True), reads=[sk, cx.ones], writes=[pb[1 + c]])
        for c in range(4):
            if pi_ == 0:
                p.op('dve', I('tensor_copy', ar.krow[0:1, c * 512:(c + 1) * 512], pb[1 + c][0:1, :]),
                     reads=[pb[1 + c]], writes=[ar.krow])
            else:
                p.op('dve', I('tensor_tensor', ar.krow[0:1, c * 512:(c + 1) * 512],
                                                          ar.krow[0:1, c * 512:(c + 1) * 512], pb[1 + c][0:1, :], ALU.add),
                     reads=[pb[1 + c], ar.krow], writes=[ar.krow])
    p.op('dve', I('reduce_max', sm["km2"][0:1, 0:1], ar.krow[0:1, :], AX.X), reads=[ar.krow], writes=[sm["km2"]])
    p.op('pe', mm(pb[0][:, 32:33], cx.ones[0:1, :], sm["km2"][0:1, 0:1], True, True),
         reads=[cx.ones, sm["km2"]], writes=[pb[0]])
    p.op('dve', I('tensor_copy', sm["kmbc"][:, 0:1], pb[0][:, 32:33]), reads=[pb[0]], writes=[sm["kmbc"]])
    p.op('dve', I('tensor_scalar_mul', sm["qq2"][:, 0:16], sm["qq"][:, 0:16], sm["kmbc"][:, 0:1]),
         reads=[sm["qq"], sm["kmbc"]], writes=[sm["qq2"]])
    p.op('act', I('activation', sm["nb"][:, 0:16], sm["qq2"][:, 0:16], AF.Sqrt),
         reads=[sm["qq2"]], writes=[sm["nb"]])
    R3 = sm["R"][:, :].rearrange("p (i n) -> p i n", n=8)
    nb3 = sm["nb"][:, 0:16].unsqueeze(2).to_broadcast([128, 16, 8])
    if gating:
        kf = kparts[0][2]
        qf = qparts[0][2]
        p.op('dve', I('reduce_sum', sm["kmT"][:, 0:8], kf[:, :].rearrange("p (n k) -> p n k", k=256), AX.X),
             reads=[kf], writes=[sm["kmT"]])
        p.op('dve', I('tensor_scalar_mul', sm["kmT"][:, 0:8], sm["kmT"][:, 0:8], 1.0 / 256.0),
             reads=[sm["kmT"]], writes=[sm["kmT"]])
        for i in range(16):
            p.op('pe', mm(pb[5][:, i * 8:(i + 1) * 8], qf[:, i * 128:(i + 1) * 128], sm["kmT"][:, 0:8], True, True),
                 reads=[qf, sm["kmT"]], writes=[pb[5]])
        p.op('dve', I('tensor_tensor', sm["gate"][:, :], pb[5][:, 0:128], ar.gpen[:, :], ALU.add),
             reads=[pb[5], ar.gpen], writes=[sm["gate"]])
        for i in range(16):
            p.op('dve', I('max', sm["mx"][:, i * 8:(i + 1) * 8], sm["gate"][:, i * 8:(i + 1) * 8]),
                 reads=[sm["gate"]], writes=[sm["mx"]])
        g3 = sm["gate"][:, :].rearrange("p (i n) -> p i n", n=8)
        thr3 = sm["mx"][:, :].rearrange("p (i n) -> p i n", n=8)[:, :, 2:3].to_broadcast([128, 16, 8])
        s3 = sm["sel"][:, :].rearrange("p (i n) -> p i n", n=8)
        p.op('dve', I('tensor_tensor', s3, g3, thr3, ALU.is_ge), reads=[sm["gate"], sm["mx"]], writes=[sm["sel"]])
        p.op('dve', I('tensor_tensor', sm["sel"][:, :], sm["sel"][:, :], ar.ownm[:, :], ALU.max),
             reads=[sm["sel"], ar.ownm], writes=[sm["sel"]])
        p.op('dve', I('tensor_scalar', sm["R"][:, :], sm["sel"][:, :], -1.0, BIG, ALU.add, ALU.mult),
             reads=[sm["sel"]], writes=[sm["R"]])
        p.op('dve', I('tensor_tensor', R3, R3, nb3, ALU.subtract), reads=[sm["R"], sm["nb"]], writes=[sm["R"]])
    else:
        p.op('dve', I('memset', sm["R"][:, :], 0.0), writes=[sm["R"]])
        p.op('dve', I('tensor_tensor', R3, R3, nb3, ALU.subtract), reads=[sm["R"], sm["nb"]], writes=[sm["R"]])
    for i in range(16):
        bank = pb[1 + i // 4]
        p.op('pe', I('transpose', bank[0:8, (i % 4) * 128:(i % 4 + 1) * 128],
                                                          sm["R"][:, i * 8:(i + 1) * 8], cx.ident[:, :]),
             reads=[sm["R"], cx.ident], writes=[bank])
    for c in range(4):
        p.op('dve', I('tensor_copy', ar.RTb[0:8, c * 512:(c + 1) * 512], pb[1 + c][0:8, :]),
             reads=[pb[1 + c]], writes=[ar.RTb])
    steps = []
    sti = 0
    for c in range(4):
        nj = 4 * c + 4
        for j in range(nj):
            steps.append((c, j, nj, sti))
            sti += 1

    def stage1(c, j, nj, si):
        q0 = max(0, j - 4 * c) * 128
        cols = slice(q0, 512)
        gcols = slice(c * 512 + q0, (c + 1) * 512)
        st = pb[si % 4]
        pt = ar.pt[si % 3]
        for pi_, ((qb, nr, _), (kb, _, _)) in enumerate(zip(qparts, kparts)):
            p.op('pe', mm(st[:, cols], kb[0:nr, j * 128:(j + 1) * 128], qb[0:nr, gcols], pi_ == 0, False),
                 reads=[kb, qb], writes=[st])
        n = j // 2
        diag = j >= 4 * c
        p.op('pe', mm(st[:, cols], ar.selE[0:8, n * 128:(n + 1) * 128], ar.RTb[0:8, gcols], False, not diag),
             reads=[ar.selE, ar.RTb], writes=[st])
        if diag:
            p.op('pe', mm(st[:, q0:q0 + 128], ar.ident_bf[:, :], ar.tri[:, :], False, True),
                 reads=[ar.ident_bf, ar.tri], writes=[st])
        p.op('act', I('activation', pt[:, cols], st[:, cols], AF.Exp, scale=scale), reads=[st], writes=[pt])

    def stage2(c, j, nj, si):
        q0 = max(0, j - 4 * c) * 128
        cols = slice(q0, 512)
        OT = pb[4 + (c % 2)]
        SM = pb[6 + (c % 2)]
        pt = ar.pt[si % 3]
        p.op('pe', mm(OT[:, cols], ar.v[:, j, :], pt[:, cols], j == 0, j == nj - 1), reads=[ar.v, pt], writes=[OT])
        p.op('pe', mm(SM[:, cols], cx.ones_bf[:, :], pt[:, cols], j == 0, j == nj - 1),
             reads=[cx.ones_bf, pt], writes=[SM])
        if j == nj - 1:
            rs = ar.rs[c % 2]
            o = ar.o[c % 2]
            p.op('dve', I('reciprocal', rs[:, :], SM[:, :]), reads=[SM], writes=[rs])
            p.op('dve', I('tensor_tensor', o[:, :], OT[:, :], rs[:, :], ALU.mult), reads=[OT, rs], writes=[o])
            p.dma('sp', out_dram[:, c * 512:(c + 1) * 512], o[:, :], reads=[o], writes=[out_tile],
                  is_output=(out_tile.name == "EXT"))

    LOOK = 2
    for i in range(len(steps) + LOOK):
        if i < len(steps):
            stage1(*steps[i])
        if i >= LOOK:
            stage2(*steps[i - LOOK])


def load_rot(p, q, dst, src, half):
    p.dma(q, dst[0:half, :], src[half:2 * half, :], writes=[dst])
    p.dma(q, dst[half:2 * half, :], src[0:half, :], writes=[dst])


def moba_head(p, cx, ar, cosA, sinA, q_src, k_src, v_src, src_tile, out_dram, out_tile, v_tile=None):
    for (src, ff, bb, ri) in ((q_src, ar.qf, ar.qb[0], 0), (k_src, ar.kf, ar.kb[0], 1)):
        raw, rot = ar.raw[ri], ar.rot[ri]
        p.dma('sp', raw[:, :], src, reads=[src_tile], writes=[raw])
        p.dma('act', rot[0:64, :], src[64:128, :], reads=[src_tile], writes=[rot])
        p.dma('act', rot[64:128, :], src[0:64, :], reads=[src_tile], writes=[rot])
        rope_fm(p, ff, raw, rot, cosA, sinA, ar.t1, ar.t2)
        p.op('act', I('copy', bb[:, :], ff[:, :]), reads=[ff], writes=[bb])
    p.dma('pool', ar.v[:, :, :], v_src.rearrange("(b p) d -> p b d", p=128), reads=[v_tile or src_tile], writes=[ar.v])
    attn_core(p, cx, ar, [(ar.qb[0], 128, ar.qf)], [(ar.kb[0], 128, ar.kf)], 128 ** -0.5, out_dram, out_tile, True)


RMS_EPS = 1e-6
ROW_MLA = 6528
ROW_GATE = 7680


class MlaRes:
    def __init__(self, p, sl):
        self.xnq = [sl.bf(16 + c // 2, cols=2048, off=(c % 2) * 2048) for c in range(4)]
        self.xnkv = [sl.bf(18 + c // 2, cols=2048, off=(c % 2) * 2048) for c in range(4)]
        self.kpe_f = sl.f32(20, rows=64)
        self.kpe_b = p.sb([64, S], BF16, "kpe_b")
        self.wq = p.sb([128, 4, 256], BF16, "m_wq")
        self.wkv = p.sb([128, 4, 256], BF16, "m_wkv")
        self.gq = p.sb([128, 4], F32, "m_gq")
        self.gkv = p.sb([128, 4], F32, "m_gkv")
        self.row = p.sb([1, 512], F32, "m_row")
        self.eps = p.sb([1, 1], F32, "m_eps")
        p.op('pool', I('memset', self.eps[:, :], RMS_EPS), writes=[self.eps])


def mla_stage(p, cx, ar, mr, projT, proj_t, w_uq, w_ukv, qn, kvn, cosM, sinM, ysT, ys_t, row0, heads=range(8)):
    pb = cx.pb
    p.dma('sp', ar.cos[0:64, :], cosM, writes=[ar.cos])
    p.dma('sp', ar.sin[0:64, :], sinM, writes=[ar.sin])
    p.dma('sp', mr.gq[:, :], qn.rearrange("(c p) -> p c", p=128), writes=[mr.gq], allow_slow_non_contiguous=True)
    p.dma('sp', mr.gkv[:, :], kvn.rearrange("(c p) -> p c", p=128), writes=[mr.gkv], allow_slow_non_contiguous=True)
    xt = ar.raw[0].derive(ar.raw[0].ap.rearrange("p (c t) -> p c t", c=4))
    sq = ar.raw[1].derive(ar.raw[1].ap.rearrange("p (c t) -> p c t", c=4))
    for (r0, g, xn) in ((ROW_MLA, mr.gq, mr.xnq), (ROW_MLA + 512, mr.gkv, mr.xnkv)):
        for ch in range(4):
            cs = slice(ch * 512, (ch + 1) * 512)
            p.dma('sp', xt[:, :, :], projT[r0:r0 + 512, cs].rearrange("(c p) t -> p c t", p=128),
                  reads=[proj_t], writes=[xt])
            p.op('act', I('activation', sq[:, :, :], xt[:, :, :], AF.Square), reads=[xt], writes=[sq])
            for c in range(4):
                p.op('pe', mm(pb[0][0:1, :], cx.ones[:, 0:1], sq[:, c, :], c == 0, c == 3),
                     reads=[cx.ones, sq], writes=[pb[0]])
            p.op('act', I('activation', mr.row[0:1, :], pb[0][0:1, :], AF.Sqrt, bias=mr.eps[0:1, 0:1],
                                               scale=1.0 / 512.0), reads=[pb[0], mr.eps], writes=[mr.row])
            p.op('dve', I('reciprocal', mr.row[0:1, :], mr.row[0:1, :]), reads=[mr.row], writes=[mr.row])
            p.op('pe', mm(pb[1][:, :], cx.ones[0:1, :], mr.row[0:1, :], True, True),
                 reads=[cx.ones, mr.row], writes=[pb[1]])
            for c in range(4):
                p.op('dve', I('scalar_tensor_tensor',
                    xn[c][:, cs], xt[:, c, :], g[:, c:c + 1], pb[1][:, :], ALU.mult, ALU.mult),
                    reads=[xt, g, pb[1]], writes=[xn[c]])
    rk = ROW_MLA + 1024
    p.dma('sp', ar.raw[0][0:64, :], projT[rk:rk + 64, :], reads=[proj_t], writes=[ar.raw[0]])
    p.dma('act', ar.rot[0][0:32, :], projT[rk + 32:rk + 64, :], reads=[proj_t], writes=[ar.rot[0]])
    p.dma('act', ar.rot[0][32:64, :], projT[rk:rk + 32, :], reads=[proj_t], writes=[ar.rot[0]])
    rope_fm(p, mr.kpe_f, ar.raw[0], ar.rot[0], ar.cos, ar.sin, ar.t1, ar.t2, nrow=64)
    p.op('act', I('copy', mr.kpe_b[:, :], mr.kpe_f[0:64, :]), reads=[mr.kpe_f], writes=[mr.kpe_b])
    uqv = w_uq.rearrange("(c p) n -> p c n", p=128)
    ukvv = w_ukv.rearrange("(c p) n -> p c n", p=128)
    bi = 0
    for h in heads:
        b0 = h * 192
        p.dma('pool', mr.wq[:, :, 0:192], uqv[:, :, b0:b0 + 192], writes=[mr.wq])
        p.dma('pool', mr.wq[:, :, 192:224], uqv[:, :, b0 + 160:b0 + 192], writes=[mr.wq])
        p.dma('pool', mr.wq[:, :, 224:256], uqv[:, :, b0 + 128:b0 + 160], writes=[mr.wq])
        p.dma('pool', mr.wkv[:, :, :], ukvv[:, :, h * 256:(h + 1) * 256], writes=[mr.wkv])

        def proj_fm(wt, c0, ncol, xn, dst, nrow):
            nonlocal bi
            for ch in range(4):
                ps = pb[bi % 4]
                bi += 1
                cs = slice(ch * 512, (ch + 1) * 512)
                for c in range(4):
                    p.op('pe', mm(ps[0:nrow, :], wt[:, c, c0:c0 + ncol], xn[c][:, cs], c == 0, c == 3),
                         reads=[wt, xn[c]], writes=[ps])
                p.op('act', I('copy', dst[0:nrow, cs], ps[0:nrow, :]), reads=[ps], writes=[dst])

        proj_fm(mr.wq, 0, 128, mr.xnq, ar.qf, 128)
        p.op('dve', I('tensor_copy', ar.qb[0][:, :], ar.qf[:, :]), reads=[ar.qf], writes=[ar.qb[0]])
        proj_fm(mr.wq, 128, 64, mr.xnq, ar.raw[0], 64)
        proj_fm(mr.wq, 192, 64, mr.xnq, ar.rot[0], 64)
        rope_fm(p, ar.raw[1], ar.raw[0], ar.rot[0], ar.cos, ar.sin, ar.t1, ar.t2, nrow=64)
        p.op('dve', I('tensor_copy', ar.qb[1][0:64, :], ar.raw[1][0:64, :]), reads=[ar.raw[1]], writes=[ar.qb[1]])
        proj_fm(mr.wkv, 0, 128, mr.xnkv, ar.kf, 128)
        p.op('dve', I('tensor_copy', ar.kb[0][:, :], ar.kf[:, :]), reads=[ar.kf], writes=[ar.kb[0]])
        for g4 in range(4):
            ps = pb[4 + g4 % 2]
            for t4 in range(4):
                tb = g4 * 4 + t4
                for c in range(4):
                    p.op('pe', mm(ps[:, t4 * 128:(t4 + 1) * 128], mr.xnkv[c][:, tb * 128:(tb + 1) * 128],
                                  mr.wkv[:, c, 128:256], c == 0, c == 3), reads=[mr.xnkv[c], mr.wkv], writes=[ps])
            p.op('act', I('copy',
                ar.v[:, g4 * 4:(g4 + 1) * 4, :], ps[:, :].rearrange("p (b d) -> p b d", d=128)),
                reads=[ps], writes=[ar.v])
        attn_core(p, cx, ar, [(ar.qb[0], 128, ar.qf), (ar.qb[1], 64, ar.raw[1])],
                  [(ar.kb[0], 128, ar.kf), (mr.kpe_b, 64, mr.kpe_f)], 192 ** -0.5,
                  ysT[row0 + h * 128:row0 + (h + 1) * 128, :], ys_t, False)


ROW_RWKV = 3072
ROW_LORA = 6144
GN_EPS = 64e-5
USE_F32R = False


def RV(ap):
    return ap.bitcast(F32R) if USE_F32R else ap


def rwkv_tables():
    c = {}
    t = np.arange(128)
    same = (t[:, None] // 32) == (t[None, :] // 32)
    strict = (same & (t[:, None] < t[None, :])).astype(np.float32)
    incl = (same & (t[:, None] <= t[None, :])).astype(np.float32)
    c["mask5"] = np.ascontiguousarray(np.stack([strict, strict.T, strict, incl, incl], 1).reshape(128, 640))
    cm = (t[:, None] // 32 == np.arange(4)[None, :]).astype(np.float32)
    c["cmask"] = np.ascontiguousarray(cm)
    c["tmask"] = np.ascontiguousarray(np.broadcast_to(cm.T.reshape(1, 512), (64, 512)))
    blk = np.zeros((128, 128), np.float32)
    blk[:64, :64] = 1
    blk[64:, 64:] = 1
    c["blk"] = blk
    return c


class RwkvRes:
    def __init__(self, p, cx, sl):
        self.mask5 = cx.load_const("mask5", [128, 640])
        self.cmask = cx.load_const("cmask", [128, 4])
        self.tmask = cx.load_const("tmask", [64, 512])
        self.blk = cx.load_const("blk", [128, 128], F32R if USE_F32R else F32)
        self.ident_r = cx.load_const("ident", [128, 128], F32R) if USE_F32R else cx.ident
        self.sl = sl
        s = lambda n, shape: p.sb(shape, F32, "r_" + n)

        def v3(i, rows, off, a, b_):
            vw = sl.f32(i, rows=rows, cols=a * b_, off=off)
            return vw.derive(vw.ap.rearrange("p (a b) -> p a b", b=b_))

        self.prm = s("prm", [128, 16, 8])
        self.lw1 = sl.f32(19, cols=1024, off=0)
        self.lw2 = sl.f32(19, cols=1024, off=1024)
        self.lw3 = sl.f32(20, rows=64, cols=1024, off=0)
        self.cst = s("cst", [128, 4])
        for i, v in enumerate((-0.5, GN_EPS, 1.0)):
            p.op('pool', I('memset', self.cst[:, i:i + 1], v), writes=[self.cst])
        self.pc = s("pc", [128, 64])
        self.mul = s("mul", [128, 3])
        self.cumc = s("cumc", [128, 64])
        def b3(i, rows, off, a, b_):
            vw = sl.bf(i, rows=rows, cols=a * b_, off=off)
            return vw.derive(vw.ap.rearrange("p (a b) -> p a b", b=b_))

        self.ident_bf = cx.load_const("ident", [128, 128], BF16, tag="_rw")
        self.tm = [b3(9, 128, 640 * i, 5, 128) for i in range(2)]
        self.dg = [v3(18, 128, 512 * i, 4, 128) for i in range(2)]
        self.z = [v3(18, 64, 1024 + 512 * i, 8, 64) for i in range(2)]
        self.hd = []
        for hh in range(2):
            sa, sb_ = ((1, 3), (7, 8))[hh]
            d = {}
            d["am"] = b3(sa, 128, 0, 5, 128)
            d["utm"] = b3(sa, 128, 640, 4, 128)
            d["pt"] = sl.bf(sa, cols=128, off=1152)
            d["pt2"] = sl.bf(sa, cols=128, off=1280)
            d["ut"] = sl.bf(sa, cols=128, off=1408)
            d["xa"] = b3(sa, 128, 1536, 2, 128)
            d["xb"] = b3(sa, 128, 1792, 2, 128)
            d["w0"] = sl.bf(sa, cols=64, off=2048)
            d["vm"] = b3(sa, 128, 2112, 4, 64)
            d["qtm"] = v3(sb_, 64, 0, 4, 128)
            d["gc"] = v3(sb_, 64, 512, 4, 64)
            d["mt"] = v3(sb_, 64, 768, 4, 64)
            self.hd.append(d)
        self.ysb = s("ysb", [128, 128])
        self.ysq = s("ysq", [128, 128])
        self.st = {n: s("st_" + n, [128, 2]) for n in ("s1", "s2", "m", "msq", "var", "sd", "rstd")}
        self.yn = s("yn", [128, 128])
        self.yo = s("yo", [128, 128])


PRM_NAMES = ["mu_r", "mu_k", "mu_v", "w0", "a0", "v0", "k_k", "k_a", "r_k", "gn_g", "gn_b", "negw0", "omka"]


def rwkv_stage(p, cx, rr, projT, proj_t, prm_d, ysT, ys_t, row0, vfT, vf_t, has_vres, gs=range(8), dbg=None):
    sl = rr.sl
    pb = cx.pb
    S_ = sl.f32
    ext_t = Tile(None, "EXT")
    ROUND_F32R[0] = USE_F32R

    def dump(name, view, sl_):
        if dbg is not None and name in dbg:
            p.dma('sp', dbg.pop(name), view[sl_], reads=[view], writes=[ext_t], is_output=True)
    PI = {n: i for i, n in enumerate(PRM_NAMES)}

    def prm(n, g):
        return rr.prm[:, PI[n], g:g + 1]

    nsl = lambda ap: ap.rearrange("(g p) -> p g", p=128)
    for n, src in (("mu_r", prm_d["mu"][0:1024]), ("mu_k", prm_d["mu"][1024:2048]), ("mu_v", prm_d["mu"][2048:3072]),
                   ("w0", prm_d["w0"]), ("a0", prm_d["a0"]), ("v0", prm_d["v0"]), ("k_k", prm_d["k_k"]),
                   ("k_a", prm_d["k_a"]), ("r_k", prm_d["r_k"]), ("gn_g", prm_d["gn_g"]), ("gn_b", prm_d["gn_b"])):
        p.dma('sp', rr.prm[:, PI[n], :], nsl(src), writes=[rr.prm], allow_slow_non_contiguous=True)
    p.op('dve', I('tensor_scalar_mul', rr.prm[:, PI["negw0"], :], rr.prm[:, PI["w0"], :], -1.0),
         reads=[rr.prm], writes=[rr.prm])
    p.op('dve', I('tensor_scalar', rr.prm[:, PI["omka"], :], rr.prm[:, PI["k_a"], :], -1.0, 1.0,
                                          ALU.mult, ALU.add), reads=[rr.prm], writes=[rr.prm])
    p.dma('pool' if USE_F32R else 'sp', RV(rr.lw1[0:64, :]), prm_d["w_lora"], writes=[rr.lw1])
    p.dma('pool' if USE_F32R else 'sp', RV(rr.lw1[64:128, :]), prm_d["a_lora"], writes=[rr.lw1])
    p.dma('pool' if USE_F32R else 'sp', RV(rr.lw2[:, :]), prm_d["g_lora"][0:128, :], writes=[rr.lw2])
    p.dma('pool' if USE_F32R else 'sp', RV(rr.lw3[0:32, :]), prm_d["g_lora"][128:160, :], writes=[rr.lw3])
    p.dma('pool' if USE_F32R else 'sp', RV(rr.lw3[32:64, :]), prm_d["v_lora"], writes=[rr.lw3])
    mul = rr.mul
    p.dma('sp', mul[:, 0:1], prm_d["mu"][3072:3200].rearrange("(p o) -> p o", o=1), writes=[mul])
    p.dma('sp', mul[:, 1:2], prm_d["mu"][3200:3328].rearrange("(p o) -> p o", o=1), writes=[mul])
    p.dma('sp', mul[0:32, 2:3], prm_d["mu"][3328:3360].rearrange("(p o) -> p o", o=1), writes=[mul])
    p.dma('sp', mul[32:64, 2:3], prm_d["mu_v"].rearrange("(p o) -> p o", o=1), writes=[mul])

    T1, T2, T3 = S_(3), S_(4), S_(18)

    def shift_load(dst, r0, nrow, mu_ap):
        H, Pv = T1, T2
        p.dma('sp', H[0:nrow, :], projT[r0:r0 + nrow, :], reads=[proj_t], writes=[H])
        p.dma('act', Pv[0:nrow, 1:S], projT[r0:r0 + nrow, 0:S - 1], reads=[proj_t], writes=[Pv])
        p.op('pool', I('memset', Pv[0:nrow, 0:1], 0.0), writes=[Pv])
        p.op('pool', I('tensor_tensor', Pv[0:nrow, :], Pv[0:nrow, :], H[0:nrow, :], ALU.subtract),
             reads=[Pv, H], writes=[Pv])
        p.op('dve', I('scalar_tensor_tensor', dst[0:nrow, :], Pv[0:nrow, :], mu_ap, H[0:nrow, :],
                                                     ALU.mult, ALU.add), reads=[Pv, H, mul, rr.prm], writes=[dst])

    XI1, XI2, XI3 = S_(14), S_(15), S_(16)
    shift_load(XI1, ROW_LORA, 128, mul[:, 0:1])
    p.op('act', I('activation', XI1[0:64, :], XI1[0:64, :], AF.Tanh), reads=[XI1], writes=[XI1])
    shift_load(XI2, ROW_LORA + 128, 128, mul[:, 1:2])
    p.op('act', I('activation', XI2[:, :], XI2[:, :], AF.Sigmoid), reads=[XI2], writes=[XI2])
    shift_load(XI3, ROW_LORA + 256, 64, mul[0:64, 2:3])
    p.op('act', I('activation', XI3[0:32, :], XI3[0:32, :], AF.Sigmoid), reads=[XI3], writes=[XI3])

    R, K, V, A, KKN, KM, CUM, U = S_(0), S_(1), S_(2), S_(7), S_(8), S_(9), S_(10), S_(11)
    PINV, PPREV, DE, G, BON, YB = S_(4), S_(5), S_(6), S_(12), S_(13), S_(17)
    Pt = S_(3)
    bk = [0]

    def lora_mm(dst_fn, parts, g):
        for ch in range(4):
            ps = pb[2 + bk[0] % 6]
            bk[0] += 1
            cs = slice(ch * 512, (ch + 1) * 512)
            for i, (lw, xi, r0, r1) in enumerate(parts):
                p.op('pe', mm(ps[:, :], lw[r0:r1, g * 128:(g + 1) * 128], xi[r0:r1, cs], i == 0, i == len(parts) - 1, r=True),
                     reads=[lw, xi], writes=[ps])
            dst_fn(ps, cs)

    for g in gs:
        fr = ROW_RWKV + g * 128
        shift_load(R, fr, 128, prm("mu_r", g))
        shift_load(K, fr + 1024, 128, prm("mu_k", g))
        shift_load(V, fr + 2048, 128, prm("mu_v", g))
        if not has_vres:
            p.dma('sp', vfT[g * 128:(g + 1) * 128, :], V[:, :], reads=[V], writes=[vf_t])
        lora_mm(lambda ps, cs: p.op('act', I('activation', U[:, cs], ps[:, :], AF.Identity,
                                                                  bias=prm("negw0", g), scale=-1.0),
                                    reads=[ps, rr.prm], writes=[U]), [(rr.lw1, XI1, 0, 64)], g)
        p.op('act', I('activation', T1[:, :], U[:, :], AF.Abs), reads=[U], writes=[T1])
        p.op('act', I('activation', T1[:, :], T1[:, :], AF.Exp, scale=-1.0), reads=[T1], writes=[T1])
        p.op('act', I('activation', T1[:, :], T1[:, :], AF.Ln, bias=rr.cst[:, 2:3]), reads=[T1, rr.cst], writes=[T1])
        p.op('dve', I('scalar_tensor_tensor', T1[:, :], U[:, :], 0.0, T1[:, :], ALU.max, ALU.add),
             reads=[U, T1], writes=[T1])
        p.op('act', I('activation', T1[:, :], T1[:, :], AF.Exp, bias=rr.cst[:, 0:1], scale=-1.0),
             reads=[T1, rr.cst], writes=[T1])
        p.op('dve', I('tensor_scalar_mul', U[:, :], T1[:, :], -1.0), reads=[T1], writes=[U])
        src = U
        bufs = [CUM, T1]
        bi = 0
        for sft in (1, 2, 4, 8, 16):
            dst = bufs[bi % 2]
            bi += 1
            s3 = src[:, :].rearrange("p (c k) -> p c k", k=32)
            d3 = dst[:, :].rearrange("p (c k) -> p c k", k=32)
            p.op('dve', I('tensor_tensor', d3[:, :, sft:], s3[:, :, sft:], s3[:, :, :32 - sft], ALU.add),
                 reads=[src], writes=[dst])
            p.op('pool', I('tensor_copy', d3[:, :, :sft], s3[:, :, :sft]),
                 reads=[src], writes=[dst])
            src = dst
        c3 = CUM[:, :].rearrange("p (c k) -> p c k", k=32)
        p.op('dve', I('tensor_copy', rr.cumc[:, :], c3[:, :, 31]), reads=[CUM], writes=[rr.cumc])
        p.op('act', I('activation', rr.pc[:, :], rr.cumc[:, :], AF.Exp), reads=[rr.cumc], writes=[rr.pc])
        p.op('act', I('activation', Pt[:, :], CUM[:, :], AF.Exp), reads=[CUM], writes=[Pt])
        p.op('act', I('activation', PINV[:, :], CUM[:, :], AF.Exp, scale=-1.0), reads=[CUM], writes=[PINV])
        p.op('dve', I('tensor_tensor', PPREV[:, :], CUM[:, :], U[:, :], ALU.subtract), reads=[CUM, U], writes=[PPREV])
        p.op('act', I('activation', PPREV[:, :], PPREV[:, :], AF.Exp), reads=[PPREV], writes=[PPREV])
        d3 = DE[:, :].rearrange("p (c k) -> p c k", k=32)
        p.op('pool', I('tensor_tensor', d3, rr.cumc[:, :].unsqueeze(2).to_broadcast([128, 64, 32]), c3, ALU.subtract),
             reads=[rr.cumc, CUM], writes=[DE])
        p.op('act', I('activation', DE[:, :], DE[:, :], AF.Exp), reads=[DE], writes=[DE])
        lora_mm(lambda ps, cs: p.op('act', I('activation', A[:, cs], ps[:, :], AF.Sigmoid, bias=prm("a0", g)),
                                    reads=[ps, rr.prm], writes=[A]), [(rr.lw1, XI1, 64, 128)], g)
        lora_mm(lambda ps, cs: p.op('act', I('copy', G[:, cs], ps[:, :]), reads=[ps], writes=[G]),
                [(rr.lw2, XI2, 0, 128), (rr.lw3, XI3, 0, 32)], g)
        if has_vres:
            lora_mm(lambda ps, cs: p.op('act', I('activation', T3[:, cs], ps[:, :], AF.Sigmoid, bias=prm("v0", g)),
                                        reads=[ps, rr.prm], writes=[T3]), [(rr.lw3, XI3, 32, 64)], g)
            p.dma('sp', U[:, :], vfT[g * 128:(g + 1) * 128, :], reads=[vf_t], writes=[U])
            p.op('dve', I('tensor_tensor', U[:, :], U[:, :], V[:, :], ALU.subtract), reads=[U, V], writes=[U])
            p.op('pool', I('tensor_tensor', U[:, :], U[:, :], T3[:, :], ALU.mult), reads=[U, T3], writes=[U])
            p.op('dve', I('tensor_tensor', V[:, :], V[:, :], U[:, :], ALU.add), reads=[U, V], writes=[V])
        p.op('dve', I('tensor_scalar_mul', KKN[:, :], K[:, :], prm("k_k", g)), reads=[K, rr.prm], writes=[KKN])
        p.op('act', I('activation', T3[:, :], KKN[:, :], AF.Square), reads=[KKN], writes=[T3])
        for ch in range(4):
            ps = pb[2 + bk[0] % 6]
            bk[0] += 1
            cs = slice(ch * 512, (ch + 1) * 512)
            p.op('pe', mm(ps[:, :], rr.blk[:, :], T3[:, cs], True, True, r=True), reads=[rr.blk, T3], writes=[ps])
            p.op('act', I('activation', U[:, cs], ps[:, :], AF.Sqrt), reads=[ps], writes=[U])
        p.op('dve', I('tensor_scalar_max', U[:, :], U[:, :], 1e-12), reads=[U], writes=[U])
        p.op('dve', I('reciprocal', U[:, :], U[:, :]), reads=[U], writes=[U])
        p.op('pool', I('tensor_tensor', KKN[:, :], KKN[:, :], U[:, :], ALU.mult), reads=[KKN, U], writes=[KKN])
        p.op('dve', I('tensor_scalar', T3[:, :], A[:, :], prm("k_a", g), prm("omka", g), ALU.mult, ALU.add),
             reads=[A, rr.prm], writes=[T3])
        p.op('pool', I('tensor_tensor', KM[:, :], K[:, :], T3[:, :], ALU.mult), reads=[K, T3], writes=[KM])
        p.op('dve', I('scalar_tensor_tensor', T3[:, :], R[:, :], prm("r_k", g), KM[:, :], ALU.mult, ALU.mult),
             reads=[R, KM, rr.prm], writes=[T3])
        for ch in range(4):
            ps = pb[2 + bk[0] % 6]
            bk[0] += 1
            cs = slice(ch * 512, (ch + 1) * 512)
            p.op('pe', mm(ps[:, :], rr.blk[:, :], T3[:, cs], True, True, r=True), reads=[rr.blk, T3], writes=[ps])
            p.op('dve', I('tensor_tensor', BON[:, cs], ps[:, :], V[:, cs], ALU.mult),
                 reads=[ps, V], writes=[BON])
        RB, AB, KT, BT, BHT, KHT = R, PPREV, U, PINV, DE, CUM
        p.op('pool', I('tensor_tensor', RB[:, :], R[:, :], Pt[:, :], ALU.mult), reads=[R, Pt], writes=[RB])
        p.op('dve', I('scalar_tensor_tensor', AB[:, :], KKN[:, :], -1.0, PPREV[:, :], ALU.mult, ALU.mult),
             reads=[KKN, PPREV], writes=[AB])
        p.op('pool', I('tensor_tensor', KT[:, :], KM[:, :], PINV[:, :], ALU.mult), reads=[KM, PINV], writes=[KT])
        p.op('dve', I('tensor_tensor', T3[:, :], KKN[:, :], A[:, :], ALU.mult), reads=[KKN, A], writes=[T3])
        p.op('pool', I('tensor_tensor', BT[:, :], T3[:, :], PINV[:, :], ALU.mult), reads=[T3, PINV], writes=[BT])
        p.op('dve', I('tensor_tensor', KHT[:, :], KM[:, :], DE[:, :], ALU.mult), reads=[KM, DE], writes=[KHT])
        p.op('pool', I('tensor_tensor', BHT[:, :], T3[:, :], DE[:, :], ALU.mult), reads=[T3, DE], writes=[BHT])
        for nm_, vw_ in (("RB", RB), ("AB", AB), ("KT", KT), ("BT", BT), ("BHT", BHT), ("KHT", KHT), ("V", V), ("G", G), ("BON", BON)):
            dump(nm_, vw_, (slice(None), slice(None)))
        dump("PC", rr.pc, (slice(None), slice(None)))
        for hh in range(2):
            p.op('pool', I('memset', rr.z[hh][:, 0, :], 0.0), writes=[rr.z[hh]])

        for tb in range(16):
            ts_ = slice(tb * 128, (tb + 1) * 128)
            tm = rr.tm[tb % 2]
            dg = rr.dg[tb % 2]
            for i, X in enumerate((V, AB, BHT, KHT, RB)):
                bank, c0 = (pb[0], i * 128) if i < 4 else (pb[1], 0)
                p.op('pe', I('transpose', bank[:, c0:c0 + 128], X[:, ts_], cx.ident[:, :]),
                     reads=[X, cx.ident], writes=[bank])
            p.op('act', I('copy', tm[:, 0:4, :], pb[0][:, :].rearrange("p (a b) -> p a b", b=128)),
                 reads=[pb[0]], writes=[tm])
            p.op('dve', I('tensor_copy', tm[:, 4, :], pb[1][:, 0:128]), reads=[pb[1]], writes=[tm])
            p.op('pool', I('tensor_tensor',
                dg[:, :, :], cx.ident[:, :].unsqueeze(1).to_broadcast([128, 4, 128]),
                rr.pc[:, tb * 4:(tb + 1) * 4].unsqueeze(2).to_broadcast([128, 4, 128]), ALU.mult),
                reads=[cx.ident, rr.pc], writes=[dg])
            H = []
            for hh in range(2):
                b = 64 * hh
                d = rr.hd[hh]
                H.append(dict(d=d, b=b, fs=slice(b, b + 64), hb=[pb[2 + 3 * hh], pb[3 + 3 * hh], pb[4 + 3 * hh]]))

            def fm(X, h):
                return X[h["fs"], ts_]

            for h in H:
                hb0, hb1 = h["hb"][0], h["hb"][1]
                pairs = ((BT, AB), (AB, BT), (KT, AB), (BT, RB), (KT, RB))
                for i, (L, Rr) in enumerate(pairs):
                    dstp = hb0[:, i * 128:(i + 1) * 128] if i < 4 else hb1[:, 0:128]
                    p.op('pe', mm(dstp, fm(L, h), fm(Rr, h), True, True, r=True), reads=[L, Rr], writes=[hb0 if i < 4 else hb1])
                am = h["d"]["am"]
                p.op('dve', I('tensor_tensor',
                    am[:, 0:4, :], hb0[:, :].rearrange("p (a b) -> p a b", b=128),
                    rr.mask5[:, 0:512].rearrange("p (a b) -> p a b", b=128), ALU.mult),
                    reads=[hb0, rr.mask5], writes=[am])
                p.op('dve', I('tensor_tensor', am[:, 4, :], hb1[:, 0:128], rr.mask5[:, 512:640], ALU.mult),
                     reads=[hb1, rr.mask5], writes=[am])
                p.op('dve', I('tensor_tensor', h["d"]["pt"][:, :], am[:, 0, :], cx.ident[:, :], ALU.add),
                     reads=[am, cx.ident], writes=[h["d"]["pt"]])
                h["xt"], h["x"] = am[:, 0, :], am[:, 1, :]
                h["xt_t"] = h["x_t"] = am
                h["pt"], h["pt_o"] = h["d"]["pt"], h["d"]["pt2"]
            for lvl in range(1, 5):
                for h in H:
                    hb1 = h["hb"][1]
                    xn_t = h["d"]["xa"] if lvl % 2 == 1 else h["d"]["xb"]
                    p.op('pe', mm(hb1[:, 128:256], h["xt"], h["x"], True, True, r=True), reads=[h["x_t"]], writes=[hb1])
                    if lvl < 4:
                        p.op('pe', mm(hb1[:, 256:384], h["x"], h["xt"], True, True, r=True), reads=[h["x_t"]], writes=[hb1])
                        p.op('act', I('copy',
                            xn_t[:, :, :], hb1[:, 128:384].rearrange("p (a b) -> p a b", b=128)), reads=[hb1], writes=[xn_t])
                    else:
                        p.op('act', I('copy', xn_t[:, 0, :], hb1[:, 128:256]),
                             reads=[hb1], writes=[xn_t])
                    h["x"], h["xt"], h["x_t"] = xn_t[:, 0, :], xn_t[:, 1, :], xn_t
                for h in H:
                    hb1 = h["hb"][1]
                    p.op('pe', mm(hb1[:, 384:512], h["x"], h["pt"][:, :], True, True, r=True), reads=[h["x_t"], h["pt"]], writes=[hb1])
                    p.op('dve', I('tensor_tensor', h["pt_o"][:, :], hb1[:, 384:512], h["pt"][:, :], ALU.add),
                         reads=[hb1, h["pt"]], writes=[h["pt_o"]])
                    h["pt"], h["pt_o"] = h["pt_o"], h["pt"]
            for h in H:
                d, hb2, fsl = h["d"], h["hb"][2], slice(h["b"], h["b"] + 64)
                p.op('pe', mm(hb2[:, 0:64], d["am"][:, 2, :], tm[:, 0, fsl], True, True, r=True), reads=[d["am"], tm], writes=[hb2])
                p.op('act', I('copy', d["w0"][:, :], hb2[:, 0:64]), reads=[hb2], writes=[d["w0"]])
            for h in H:
                d, hb2, fsl = h["d"], h["hb"][2], slice(h["b"], h["b"] + 64)
                p.op('pe', mm(hb2[:, 64:128], h["pt"][:, :], d["w0"][:, :], True, True, r=True), reads=[h["pt"], d["w0"]], writes=[hb2])
                p.op('pe', mm(hb2[:, 128:192], h["pt"][:, :], tm[:, 1, fsl], True, True, r=True), reads=[h["pt"], tm], writes=[hb2])
                p.op('act', I('copy', d["ut"][:, :], hb2[:, 64:192]), reads=[hb2], writes=[d["ut"]])
                p.op('dve', I('tensor_tensor',
                    d["utm"][:, :, :], d["ut"][:, :].unsqueeze(1).to_broadcast([128, 4, 128]),
                    rr.cmask[:, :].unsqueeze(2).to_broadcast([128, 4, 128]), ALU.mult),
                    reads=[d["ut"], rr.cmask], writes=[d["utm"]])
                p.op('pool', I('tensor_tensor',
                    d["vm"][:, :, :], tm[:, 0, fsl].unsqueeze(1).to_broadcast([128, 4, 64]),
                    rr.cmask[:, :].unsqueeze(2).to_broadcast([128, 4, 64]), ALU.mult),
                    reads=[tm, rr.cmask], writes=[d["vm"]])
            for h in H:
                d, hb1, hb2, fsl = h["d"], h["hb"][1], h["hb"][2], slice(h["b"], h["b"] + 64)
                for c in range(4):
                    p.op('pe', mm(hb1[0:64, 128 + c * 64:192 + c * 64], tm[:, 2, fsl], d["utm"][:, c, 0:64], True, False, r=True),
                         reads=[tm, d["utm"]], writes=[hb1])
                    p.op('pe', mm(hb1[0:64, 128 + c * 64:192 + c * 64], tm[:, 3, fsl], d["vm"][:, c, :], False, True, r=True),
                         reads=[tm, d["vm"]], writes=[hb1])
                    p.op('pe', mm(hb2[0:64, 192 + c * 64:256 + c * 64], d["utm"][:, c, 64:128], tm[:, 2, fsl], True, False, r=True),
                         reads=[tm, d["utm"]], writes=[hb2])
                    p.op('pe', mm(hb2[0:64, 192 + c * 64:256 + c * 64], rr.ident_r[:, fsl], dg[:, c, fsl], False, True, r=True),
                         reads=[rr.ident_r, dg], writes=[hb2])
                p.op('act', I('copy', d["gc"][:, :, :], hb1[0:64, 128:384].rearrange("p (a b) -> p a b", b=64)),
                     reads=[hb1], writes=[d["gc"]])
                p.op('dve', I('tensor_copy', d["mt"][:, :, :], hb2[0:64, 192:448].rearrange("p (a b) -> p a b", b=64)),
                     reads=[hb2], writes=[d["mt"]])
            ybank = pb[1]
            for h in H:
                d, hb1, fsl = h["d"], h["hb"][1], slice(h["b"], h["b"] + 64)
                p.op('pe', mm(hb1[0:64, 384:512], tm[:, 4, fsl], rr.ident_bf[:, :], True, False), reads=[tm, rr.ident_bf], writes=[hb1])
                p.op('pe', mm(hb1[0:64, 384:512], d["ut"][:, 64:128], d["am"][:, 3, :], False, True, r=True), reads=[d["ut"], d["am"]], writes=[hb1])
                p.op('dve', I('tensor_tensor',
                    d["qtm"][:, :, :], hb1[0:64, 384:512].unsqueeze(1).to_broadcast([64, 4, 128]),
                    rr.tmask[:, :].rearrange("p (a b) -> p a b", b=128), ALU.mult), reads=[hb1, rr.tmask], writes=[d["qtm"]])
            yacc = pb[0]
            for hi, h in enumerate(H):
                d, fsl = h["d"], slice(h["b"], h["b"] + 64)
                yp = yacc[:, hi * 64:(hi + 1) * 64]
                p.op('pe', mm(yp, d["am"][:, 3, :], d["ut"][:, 0:64], hi == 0, False, r=True), reads=[d["am"], d["ut"]], writes=[yacc])
                p.op('pe', mm(yp, d["am"][:, 4, :], tm[:, 0, fsl], False, False, r=True), reads=[d["am"], tm], writes=[yacc])
            if tb == 0:
                d0 = H[0]["d"]
                dump("tm", tm, (slice(None), slice(None), slice(None)))
                dump("am", d0["am"], (slice(None), slice(None), slice(None)))
                dump("pt", H[0]["pt"], (slice(None), slice(None)))
                dump("ut", d0["ut"], (slice(None), slice(None)))
                dump("gc", d0["gc"], (slice(None), slice(None), slice(None)))
                dump("mt", d0["mt"], (slice(None), slice(None), slice(None)))
                dump("qtm", d0["qtm"], (slice(None), slice(None), slice(None)))
            for c in range(4):
                sidx = tb * 4 + c
                for hi, h in enumerate(H):
                    d = h["d"]
                    z = rr.z[hi]
                    zp = ybank[0:64, 256 + hi * 64:320 + hi * 64]
                    p.op('pe', mm(zp, d["mt"][:, c, :], z[:, sidx % 8, :], True, True, r=True), reads=[d["mt"], z], writes=[ybank])
                    p.op('dve', I('tensor_tensor',
                        z[:, (sidx + 1) % 8, :], zp, d["gc"][:, c, :], ALU.add), reads=[ybank, d["gc"]], writes=[z])
            for hi, h in enumerate(H):
                d = h["d"]
                z = rr.z[hi]
                yp = yacc[:, hi * 64:(hi + 1) * 64]
                for c in range(4):
                    sidx = tb * 4 + c
                    p.op('pe', mm(yp, d["qtm"][:, c, :], z[:, sidx % 8, :], False, c == 3, r=True), reads=[d["qtm"], z], writes=[yacc])
            y3 = rr.ysb[:, :].rearrange("p (a b) -> p a b", b=64)
            q3 = rr.ysq[:, :].rearrange("p (a b) -> p a b", b=64)
            st = rr.st
            p.op('act', I('copy', rr.ysb[:, :], yacc[:, 0:128]), reads=[yacc], writes=[rr.ysb])
            p.op('act', I('activation', rr.ysq[:, :], rr.ysb[:, :], AF.Square), reads=[rr.ysb], writes=[rr.ysq])
            if tb == 0:
                dump("ysb", rr.ysb, (slice(None), slice(None)))
            p.op('dve', I('reduce_sum', st["s1"][:, :], y3, AX.X), reads=[rr.ysb], writes=[st["s1"]])
            p.op('dve', I('reduce_sum', st["s2"][:, :], q3, AX.X), reads=[rr.ysq], writes=[st["s2"]])
            p.op('dve', I('tensor_scalar_mul', st["m"][:, :], st["s1"][:, :], 1.0 / 64), reads=[st["s1"]], writes=[st["m"]])
            p.op('dve', I('tensor_tensor', st["msq"][:, :], st["m"][:, :], st["m"][:, :], ALU.mult), reads=[st["m"]], writes=[st["msq"]])
            p.op('dve', I('scalar_tensor_tensor', st["var"][:, :], st["s2"][:, :], 1.0 / 64, st["msq"][:, :], ALU.mult, ALU.subtract),
                 reads=[st["s2"], st["msq"]], writes=[st["var"]])
            p.op('act', I('activation', st["sd"][:, :], st["var"][:, :], AF.Sqrt, bias=rr.cst[:, 1:2]),
                 reads=[st["var"], rr.cst], writes=[st["sd"]])
            p.op('dve', I('reciprocal', st["rstd"][:, :], st["sd"][:, :]), reads=[st["sd"]], writes=[st["rstd"]])
            n3 = rr.yn[:, :].rearrange("p (a b) -> p a b", b=64)
            p.op('dve', I('tensor_tensor', n3, y3, st["m"][:, :].unsqueeze(2).to_broadcast([128, 2, 64]), ALU.subtract),
                 reads=[rr.ysb, st["m"]], writes=[rr.yn])
            p.op('dve', I('tensor_tensor', n3, n3, st["rstd"][:, :].unsqueeze(2).to_broadcast([128, 2, 64]), ALU.mult),
                 reads=[rr.yn, st["rstd"]], writes=[rr.yn])
            p.op('pe', I('transpose', ybank[:, 384:512], rr.yn[:, :], cx.ident[:, :]), reads=[rr.yn, cx.ident], writes=[ybank])
            p.op('act', I('activation', rr.yo[:, :], ybank[:, 384:512], AF.Identity, bias=prm("gn_b", g), scale=prm("gn_g", g)),
                 reads=[ybank, rr.prm], writes=[rr.yo])
            p.op('dve', I('tensor_tensor', rr.yo[:, :], rr.yo[:, :], BON[:, ts_], ALU.add), reads=[rr.yo, BON], writes=[rr.yo])
            p.op('dve', I('tensor_tensor', YB[:, ts_], rr.yo[:, :], G[:, ts_], ALU.mult), reads=[rr.yo, G], writes=[YB])
        p.dma('sp', ysT[row0 + g * 128:row0 + (g + 1) * 128, :], YB[:, :], reads=[YB], writes=[ys_t],
              is_output=(ys_t.name == "EXT"))
    ROUND_F32R[0] = False


def wview(w_ap):
    return w_ap.rearrange("(c p) n -> p c n", p=128)


def bfv(sl, i0, nslab, a, b_):
    per = 4096 // b_
    out = []
    for j in range(nslab):
        vw = sl.bf(i0 + j, cols=per * b_)
        out.append(vw.derive(vw.ap.rearrange("p (a b) -> p a b", b=b_)))
    return out, per


class KView:
    def __init__(self, sl, i0, KC, T):
        self.views, self.per = bfv(sl, i0, (KC * T + 4095) // 4096, KC, T)
        self.KC = KC

    def k(self, k):
        v = self.views[k // self.per]
        return v, v[:, k % self.per, :]

    def tiles(self):
        return self.views


def load_kview(p, kv, src_T, c0, T, src_t, q='pool'):
    sv = src_T.rearrange("(c p) t -> p c t", p=128)
    for j, v in enumerate(kv.views):
        k0 = j * kv.per
        k1 = min(kv.KC, k0 + kv.per)
        p.dma(q, v[:, 0:k1 - k0, :], sv[:, k0:k1, c0:c0 + T], reads=[src_t], writes=[v])


def proj_stage(p, cx, sl, xT, xT_t, w, projT, proj_t, v_tm, vtm_t):
    pb = cx.pb
    KC = 16
    wv = wview(w)
    pi = 0
    for tt in range(2):
        t0 = tt * 1024
        xs = KView(sl, 0, KC, 1024)
        load_kview(p, xs, xT, t0, 1024, xT_t)
        wb = []
        for i in range(3):
            a, b_ = sl.bf(4 + 2 * i), sl.bf(5 + 2 * i)
            wb.append([a.derive(a.ap.rearrange("p (c n) -> p c n", n=512)),
                       b_.derive(b_.ap.rearrange("p (c n) -> p c n", n=512))])
        ob = [sl.f32(10, cols=1024, off=0), sl.f32(10, cols=1024, off=1024), sl.f32(11, cols=1024, off=0)]
        oi = 0
        for nb in range(NCAT // 512):
            wt = wb[nb % 3]
            for hf in range(2):
                p.dma('pool', wt[hf][:, :, :], wv[:, hf * 8:(hf + 1) * 8, nb * 512:(nb + 1) * 512], writes=[wt[hf]])
            if nb in (4, 5):
                for tb in range(8):
                    ps = pb[pi % 6]
                    pi += 1
                    for k in range(KC):
                        xv, xa = xs.k(k)
                        p.op('pe', mm(ps[:, :], xa[:, tb * 128:(tb + 1) * 128], wt[k // 8][:, k % 8, :], k == 0, k == KC - 1),
                             reads=[xv, wt[k // 8]], writes=[ps])
                    o = ob[oi % 3]
                    oi += 1
                    p.op('act', I('copy', o[:, 0:512], ps[:, :]), reads=[ps], writes=[o])
                    p.dma('sp', v_tm[t0 + tb * 128:t0 + (tb + 1) * 128, (nb - 4) * 512:(nb - 3) * 512], o[:, 0:512],
                          reads=[o], writes=[vtm_t])
                continue
            for j in range(4):
                n = nb * 4 + j
                o = ob[oi % 3]
                oi += 1
                for th in range(2):
                    ps = pb[pi % 6]
                    pi += 1
                    for k in range(KC):
                        xv, xa = xs.k(k)
                        p.op('pe', mm(ps[:, :], wt[k // 8][:, k % 8, j * 128:(j + 1) * 128], xa[:, th * 512:(th + 1) * 512],
                                      k == 0, k == KC - 1), reads=[xv, wt[k // 8]], writes=[ps])
                    osl = o[:, th * 512:(th + 1) * 512]
                    if n >= GATE_CHUNK0:
                        p.op('act', I('activation', osl, ps[:, :], AF.Sigmoid), reads=[ps], writes=[o])
                    elif th == 0:
                        p.op('act', I('copy', osl, ps[:, :]), reads=[ps], writes=[o])
                    else:
                        p.op('dve', I('tensor_copy', osl, ps[:, :]), reads=[ps], writes=[o])
                p.dma('sp', projT[n * 128:(n + 1) * 128, t0:t0 + 1024], o[:, :], reads=[o], writes=[proj_t])


class LnRes:
    def __init__(self, p):
        self.st = {n: p.sb([128, 1], F32, "ln_" + n) for n in ("s1", "s2", "m", "msq", "var", "sd", "rstd")}
        self.eps = p.sb([128, 1], F32, "ln_eps")
        p.op('pool', I('memset', self.eps[:, :], LN_EPS), writes=[self.eps])


def resid_ln(p, cx, sl, lr, f_tm, f_t, x_tm, x_t, g_d, b_d, out_tm, out_t, outT, outT_t, final=False):
    pb = cx.pb
    G, Bv = sl.f32(0), sl.f32(1)
    p.dma('sp', G[:, :], g_d.partition_broadcast(128), writes=[G])
    p.dma('sp', Bv[:, :], b_d.partition_broadcast(128), writes=[Bv])
    st = lr.st
    for tb in range(S // 128):
        rs_ = slice(tb * 128, (tb + 1) * 128)
        X, F, Y, Q = sl.f32(2 + (tb % 2)), sl.f32(4 + (tb % 2)), sl.f32(6 + (tb % 2)), sl.f32(8)
        p.dma('sp', X[:, :], x_tm[rs_, :], reads=[x_t], writes=[X])
        p.dma('act', F[:, :], f_tm[rs_, :], reads=[f_t], writes=[F])
        p.op('dve', I('scalar_tensor_tensor', Y[:, :], X[:, :], ALPHA, F[:, :], ALU.mult, ALU.add), reads=[X, F], writes=[Y])
        p.op('dve', I('reduce_sum', st["s1"][:, :], Y[:, :], AX.X), reads=[Y], writes=[st["s1"]])
        p.op('act', I('activation', Q[:, :], Y[:, :], AF.Square), reads=[Y], writes=[Q])
        p.op('dve', I('reduce_sum', st["s2"][:, :], Q[:, :], AX.X), reads=[Q], writes=[st["s2"]])
        p.op('dve', I('tensor_scalar_mul', st["m"][:, :], st["s1"][:, :], 1.0 / D), reads=[st["s1"]], writes=[st["m"]])
        p.op('dve', I('tensor_tensor', st["msq"][:, :], st["m"][:, :], st["m"][:, :], ALU.mult), reads=[st["m"]], writes=[st["msq"]])
        p.op('dve', I('scalar_tensor_tensor', st["var"][:, :], st["s2"][:, :], 1.0 / D, st["msq"][:, :], ALU.mult, ALU.subtract),
             reads=[st["s2"], st["msq"]], writes=[st["var"]])
        p.op('act', I('activation', st["sd"][:, :], st["var"][:, :], AF.Sqrt, bias=lr.eps[:, 0:1]), reads=[st["var"], lr.eps], writes=[st["sd"]])
        p.op('dve', I('reciprocal', st["rstd"][:, :], st["sd"][:, :]), reads=[st["sd"]], writes=[st["rstd"]])
        p.op('dve', I('tensor_scalar', Y[:, :], Y[:, :], st["m"][:, 0:1], st["rstd"][:, 0:1], ALU.subtract, ALU.mult),
             reads=[Y, st["m"], st["rstd"]], writes=[Y])
        p.op('pool', I('tensor_tensor', Y[:, :], Y[:, :], G[:, :], ALU.mult), reads=[Y, G], writes=[Y])
        p.op('dve', I('tensor_tensor', Y[:, :], Y[:, :], Bv[:, :], ALU.add), reads=[Y, Bv], writes=[Y])
        p.dma('sp', out_tm[rs_, :], Y[:, :], reads=[Y], writes=[out_t], is_output=final)
        if outT is not None:
            OT = sl.f32(9 + (tb % 2))
            o3 = OT[:, :].rearrange("p (c t) -> p c t", t=128)
            for c4 in range(4):
                ps = pb[c4 % 4]
                for j in range(4):
                    c = c4 * 4 + j
                    p.op('pe', I('transpose', ps[:, j * 128:(j + 1) * 128], Y[:, c * 128:(c + 1) * 128], cx.ident[:, :]),
                         reads=[Y, cx.ident], writes=[ps])
                p.op('act' if c4 % 2 == 0 else 'dve', I('copy' if c4 % 2 == 0 else 'tensor_copy', o3[:, c4 * 4:(c4 + 1) * 4, :],
                                                        ps[:, :].rearrange("p (c t) -> p c t", t=128)), reads=[ps], writes=[OT])
            p.dma('act', outT.rearrange("(c p) t -> p c t", p=128)[:, :, rs_], o3, reads=[OT], writes=[outT_t])


def mix_stage(p, cx, sl, ysT, ys_t, projT, proj_t, bo, wo, f_tm, f_t):
    pb = cx.pb
    pi = 0
    for tt in range(2):
        t0 = tt * 1024
        ys = KView(sl, 0, 24, 1024)
        load_kview(p, ys, ysT, t0, 1024, ys_t)
        acc = KView(sl, 6, 16, 1024)
        for db in range(4):
            wts = []
            for n in range(3):
                vw = sl.bf(10 + (db % 2) * 3 + n)
                wt = vw.derive(vw.ap.rearrange("p (c n) -> p c n", n=512))
                p.dma('pool', wt[:, :, :], wview(bo[n])[:, :, db * 512:(db + 1) * 512], writes=[wt])
                wts.append(wt)
            for j in range(4):
                dc = db * 4 + j
                av, aa = acc.k(dc)
                for th in range(2):
                    cs = slice(th * 512, (th + 1) * 512)
                    gsl = sl.f32(16 + (pi % 2))
                    g3 = gsl.derive(gsl.ap[:, 0:1536].rearrange("p (n t) -> p n t", t=512))
                    tmp = gsl.derive(gsl.ap[:, 1536:2048])
                    for n in range(3):
                        r0 = ROW_GATE + n * 2048 + dc * 128
                        p.dma('sp', g3[:, n, :], projT[r0:r0 + 128, t0 + th * 512:t0 + (th + 1) * 512], reads=[proj_t], writes=[g3])
                    pss = []
                    for n in range(3):
                        ps = pb[pi % 8]
                        pi += 1
                        for k in range(8):
                            yv, ya = ys.k(n * 8 + k)
                            p.op('pe', mm(ps[:, :], wts[n][:, k, j * 128:(j + 1) * 128], ya[:, cs], k == 0, k == 7),
                                 reads=[wts[n], yv], writes=[ps])
                        pss.append(ps)
                    p.op('dve', I('tensor_tensor', tmp[:, :], pss[0][:, :], g3[:, 0, :], ALU.mult), reads=[pss[0], g3], writes=[tmp])
                    p.op('dve', I('tensor_tensor', g3[:, 1, :], pss[1][:, :], g3[:, 1, :], ALU.mult), reads=[pss[1], g3], writes=[g3])
                    p.op('dve', I('tensor_tensor', g3[:, 2, :], pss[2][:, :], g3[:, 2, :], ALU.mult), reads=[pss[2], g3], writes=[g3])
                    p.op('pool', I('tensor_tensor', tmp[:, :], tmp[:, :], g3[:, 1, :], ALU.add), reads=[tmp, g3], writes=[tmp])
                    p.op('dve', I('tensor_tensor', aa[:, cs], tmp[:, :], g3[:, 2, :], ALU.add), reads=[tmp, g3], writes=[av])
        wos = KView(sl, 10, 16, 2048)
        wov = wview(wo)
        for jx, v in enumerate(wos.views):
            k0 = jx * wos.per
            p.dma('pool', v[:, 0:wos.per, :], wov[:, k0:k0 + wos.per, :], writes=[v])
        for tb in range(8):
            o = sl.f32(18 + (tb % 2))
            for c4 in range(4):
                ps = pb[pi % 8]
                pi += 1
                for k in range(16):
                    av, aa = acc.k(k)
                    wv_, wa = wos.k(k)
                    p.op('pe', mm(ps[:, :], aa[:, tb * 128:(tb + 1) * 128], wa[:, c4 * 512:(c4 + 1) * 512], k == 0, k == 15),
                         reads=[av, wv_], writes=[ps])
                p.op('act' if c4 % 2 == 0 else 'dve', I('copy' if c4 % 2 == 0 else 'tensor_copy', o[:, c4 * 512:(c4 + 1) * 512], ps[:, :]),
                     reads=[ps], writes=[o])
            p.dma('sp', f_tm[t0 + tb * 128:t0 + (tb + 1) * 128, :], o[:, :], reads=[o], writes=[f_t])


class MoeRes:
    def __init__(self, p):
        self.lg = p.sb([128, 8], F32, "mo_lg")
        self.mx = p.sb([128, 8], F32, "mo_mx")
        self.sel = p.sb([128, 8], F32, "mo_sel")
        self.e = p.sb([128, 8], F32, "mo_e")
        self.d = p.sb([128, 1], F32, "mo_d")
        self.comb = p.sb([128, 4, 8], F32, "mo_comb")
        self.rt = p.sb([128, 16, 8], F32, "mo_rt")
        self.combT = p.sb([8, 512], F32, "mo_combT")
        self.selE = None


def ffn_stage(p, cx, sl, mo, xT, xT_t, experts, router, f_tm, f_t):
    pb = cx.pb
    pi = 0
    moe = router is not None
    if moe:
        p.dma('sp', mo.rt[:, :, :], router.rearrange("(c p) e -> p c e", p=128), writes=[mo.rt])
        if mo.selE is None:
            mo.selE = cx.load_const("selE", [8, 1024], F32, tag="_mo")
    for tt in range(4):
        t0 = tt * 512
        xs = KView(sl, 0, 16, 512)
        load_kview(p, xs, xT, t0, 512, xT_t)
        hT = KView(sl, 2, 44, 512)
        accT = [sl.f32(16 + c // 4, cols=512, off=(c % 4) * 512) for c in range(16)]
        CB = sl.f32(20, cols=512, off=1024)
        STMP = [sl.f32(20, cols=512, off=1536), sl.f32(14, cols=512, off=0)]
        if moe:
            for tb in range(4):
                xf = sl.f32(19)
                x3 = xf.derive(xf.ap.rearrange("p (c t) -> p c t", t=128))
                p.dma('sp', x3[:, :, :], xT.rearrange("(c p) t -> p c t", p=128)[:, :, t0 + tb * 128:t0 + (tb + 1) * 128],
                      reads=[xT_t], writes=[x3])
                ps = pb[pi % 8]
                pi += 1
                for k in range(16):
                    p.op('pe', mm(ps[:, 0:8], x3[:, k, :], mo.rt[:, k, :], k == 0, k == 15), reads=[x3, mo.rt], writes=[ps])
                p.op('dve', I('tensor_copy', mo.lg[:, :], ps[:, 0:8]), reads=[ps], writes=[mo.lg])
                p.op('dve', I('max', mo.mx[:, :], mo.lg[:, :]), reads=[mo.lg], writes=[mo.mx])
                p.op('dve', I('tensor_scalar', mo.sel[:, :], mo.lg[:, :], mo.mx[:, 1:2], None, ALU.is_ge), reads=[mo.lg, mo.mx], writes=[mo.sel])
                p.op('dve', I('tensor_scalar', mo.e[:, :], mo.lg[:, :], mo.mx[:, 0:1], None, ALU.subtract), reads=[mo.lg, mo.mx], writes=[mo.e])
                p.op('act', I('activation', mo.e[:, :], mo.e[:, :], AF.Exp), reads=[mo.e], writes=[mo.e])
                p.op('dve', I('tensor_tensor', mo.d[:, :], mo.mx[:, 1:2], mo.mx[:, 0:1], ALU.subtract), reads=[mo.mx], writes=[mo.d])
                p.op('act', I('activation', mo.d[:, :], mo.d[:, :], AF.Exp), reads=[mo.d], writes=[mo.d])
                p.op('dve', I('tensor_scalar_add', mo.d[:, :], mo.d[:, :], 1.0), reads=[mo.d], writes=[mo.d])
                p.op('dve', I('reciprocal', mo.d[:, :], mo.d[:, :]), reads=[mo.d], writes=[mo.d])
                p.op('dve', I('tensor_tensor', mo.e[:, :], mo.e[:, :], mo.sel[:, :], ALU.mult), reads=[mo.e, mo.sel], writes=[mo.e])
                p.op('dve', I('tensor_scalar_mul', mo.comb[:, tb, :], mo.e[:, :], mo.d[:, 0:1]), reads=[mo.e, mo.d], writes=[mo.comb])
            pst = pb[pi % 8]
            pi += 1
            for tb in range(4):
                p.op('pe', I('transpose', pst[0:8, tb * 128:(tb + 1) * 128], mo.comb[:, tb, :], cx.ident[:, :]),
                     reads=[mo.comb, cx.ident], writes=[pst])
            p.op('dve', I('tensor_copy', mo.combT[0:8, :], pst[0:8, :]), reads=[pst], writes=[mo.combT])
        for ei, (wg, wu, wd) in enumerate(experts):
            wgv, wuv, wdv = wview(wg), wview(wu), wview(wd)
            for fb in range(DFF // 256):
                wts = []
                for wi, wvv in enumerate((wgv, wuv)):
                    vw = sl.bf(8 + (fb % 2) * 2 + wi)
                    wt = vw.derive(vw.ap.rearrange("p (c n) -> p c n", n=256))
                    p.dma('pool', wt[:, :, :], wvv[:, :, fb * 256:(fb + 1) * 256], writes=[wt])
                    wts.append(wt)
                for j in range(2):
                    fc = fb * 2 + j
                    psg, psu = pb[pi % 8], pb[(pi + 1) % 8]
                    pi += 2
                    for wi, ps in enumerate((psg, psu)):
                        for k in range(16):
                            xv, xa = xs.k(k)
                            p.op('pe', mm(ps[:, :], wts[wi][:, k, j * 128:(j + 1) * 128], xa[:, :], k == 0, k == 15),
                                 reads=[wts[wi], xv], writes=[ps])
                    tmpv = sl.f32(20, cols=512, off=(fc % 2) * 512)
                    hv, ha = hT.k(fc)
                    p.op('act', I('activation', tmpv[:, :], psg[:, :], AF.Silu), reads=[psg], writes=[tmpv])
                    p.op('dve', I('tensor_tensor', ha[:, :], tmpv[:, :], psu[:, :], ALU.mult), reads=[tmpv, psu], writes=[hv])
            if moe:
                psb = pb[pi % 8]
                pi += 1
                p.op('pe', mm(psb[:, :], mo.selE[0:8, ei * 128:(ei + 1) * 128], mo.combT[0:8, :], True, True),
                     reads=[mo.selE, mo.combT], writes=[psb])
                p.op('act', I('copy', CB[:, :], psb[:, :]), reads=[psb], writes=[CB])
            for half in range(2):
                for k in range(44):
                    rb = sl.bf(12 + (k % 8) // 4, cols=1024, off=(k % 4) * 1024)
                    p.dma('pool', rb[:, :], wd[k * 128:(k + 1) * 128, half * 1024:(half + 1) * 1024], writes=[rb])
                    hv, ha = hT.k(k)
                    for j in range(8):
                        p.op('pe', mm(pb[j][:, :], rb[:, j * 128:(j + 1) * 128], ha[:, :], k == 0, k == 43),
                             reads=[rb, hv], writes=[pb[j]])
                for j in range(8):
                    ps = pb[j]
                    av = accT[half * 8 + j]
                    if not moe:
                        p.op('act' if j % 2 == 0 else 'dve', I('copy' if j % 2 == 0 else 'tensor_copy', av[:, :], ps[:, :]),
                             reads=[ps], writes=[av])
                    elif ei == 0:
                        p.op('dve', I('tensor_tensor', av[:, :], ps[:, :], CB[:, :], ALU.mult), reads=[ps, CB], writes=[av])
                    else:
                        st_ = STMP[j % 2]
                        p.op('dve', I('tensor_tensor', st_[:, :], ps[:, :], CB[:, :], ALU.mult), reads=[ps, CB], writes=[st_])
                        p.op('pool', I('tensor_tensor', av[:, :], av[:, :], st_[:, :], ALU.add), reads=[av, st_], writes=[av])
        for tb in range(4):
            o = sl.f32(8 + (tb % 2))
            for c4 in range(4):
                ps = pb[pi % 8]
                pi += 1
                for j in range(4):
                    c = c4 * 4 + j
                    p.op('pe', I('transpose', ps[:, j * 128:(j + 1) * 128], accT[c][:, tb * 128:(tb + 1) * 128], cx.ident[:, :]),
                         reads=[accT[c], cx.ident], writes=[ps])
                p.op('act' if c4 % 2 == 0 else 'dve', I('copy' if c4 % 2 == 0 else 'tensor_copy', o[:, c4 * 512:(c4 + 1) * 512], ps[:, :]),
                     reads=[ps], writes=[o])
            p.dma('sp', f_tm[t0 + tb * 128:t0 + (tb + 1) * 128, :], o[:, :], reads=[o], writes=[f_t])


N_ACTIVE = 4


def build_full():
    p = Prog()
    ct = const_tables()
    ct.update(rwkv_tables())
    cd = {n: p.dram("c_" + n, a.shape, F32, "ExternalInput") for n, a in ct.items()}
    dr = lambda n, sh, kind="ExternalInput": p.dram(n, sh, F32, kind)
    sc = lambda n, sh: p.nc.dram_tensor(n, list(sh), F32).ap()
    x_tm = dr("x_tm", [S, D])
    xT = dr("xT", [D, S])
    out = dr("out", [S, D], "ExternalOutput")
    L = []
    for l in range(2):
        d = {}
        d["wcat"] = dr(f"wcat{l}", [D, NCAT])
        for n, sh in dict(mu=[3360], w0=[1024], a0=[1024], k_k=[1024], k_a=[1024], r_k=[1024], gn_g=[1024], gn_b=[1024],
                          w_lora=[64, 1024], a_lora=[64, 1024], g_lora=[160, 1024]).items():
            d[n] = dr(f"rw{l}_{n}", sh)
        d["mu_v"] = dr(f"rw{l}_mu_v", [32])
        d["v0"] = dr(f"rw{l}_v0", [1024])
        d["v_lora"] = dr(f"rw{l}_v_lora", [32, 1024])
        d["qn"] = dr(f"qn{l}", [512])
        d["kvn"] = dr(f"kvn{l}", [512])
        d["w_uq"] = dr(f"w_uq{l}", [512, 1536])
        d["w_ukv"] = dr(f"w_ukv{l}", [512, 2048])
        d["bo"] = [dr(f"bo{l}_{n}", [1024, D]) for n in range(3)]
        d["wo"] = dr(f"wo{l}", [D, D])
        for n in ("ln1_g", "ln1_b", "ln2_g", "ln2_b"):
            d[n] = dr(f"{n}{l}", [D])
        L.append(d)
    ffn = (dr("ffn_wg", [D, DFF]), dr("ffn_wu", [D, DFF]), dr("ffn_wd", [DFF, D]))
    router = dr("router", [D, NE])
    moe = [(dr(f"moe_wg{e}", [D, DFF]), dr(f"moe_wu{e}", [D, DFF]), dr(f"moe_wd{e}", [DFF, D])) for e in range(NE)]
    projT, v_tm, ysT, vfT = sc("projT", [NCAT, S]), sc("v_tm", [S, 1024]), sc("ysT", [3072, S]), sc("vfT", [1024, S])
    f_tm = sc("f_tm", [S, D])
    x1, x1T, x2, x2T = sc("x1", [S, D]), sc("x1T", [D, S]), sc("x2", [S, D]), sc("x2T", [D, S])
    T = lambda n: Tile(None, n)
    t_in, t_proj, t_v, t_ys, t_vf, t_f = T("in"), T("proj"), T("vtm"), T("ys"), T("vf"), T("f")
    t_x1, t_x1T, t_x2, t_x2T, t_out = T("x1"), T("x1T"), T("x2"), T("x2T"), T("EXT")

    cx = Ctx(p, cd)
    sl = Slabs(p)
    ar = AttnRes(p, cx, sl)
    mr = MlaRes(p, sl)
    rr = RwkvRes(p, cx, sl)
    lr = LnRes(p)
    mo = MoeRes(p)
    print("sbuf remaining", p.nc.sbuf_bytes_remaining)

    cur_tm, cur_t, curT, curT_t = x_tm, t_in, xT, t_in
    for l in range(2):
        d = L[l]
        proj_stage(p, cx, sl, curT, curT_t, d["wcat"], projT, t_proj, v_tm, t_v)
        p.dma('sp', ar.cos[:, :], cd["cosA"], writes=[ar.cos])
        p.dma('sp', ar.sin[:, :], cd["sinA"], writes=[ar.sin])
        for h in range(8):
            moba_head(p, cx, ar, ar.cos, ar.sin, projT[h * 128:(h + 1) * 128, :], projT[1024 + h * 128:1024 + (h + 1) * 128, :],
                      v_tm[:, h * 128:(h + 1) * 128], t_proj, ysT[h * 128:(h + 1) * 128, :], t_ys, v_tile=t_v)
        rwkv_stage(p, cx, rr, projT, t_proj, d, ysT, t_ys, 1024, vfT, t_vf, l > 0)
        mla_stage(p, cx, ar, mr, projT, t_proj, d["w_uq"], d["w_ukv"], d["qn"], d["kvn"], cd["cosM"], cd["sinM"], ysT, t_ys, 2048)
        mix_stage(p, cx, sl, ysT, t_ys, projT, t_proj, d["bo"], d["wo"], f_tm, t_f)
        resid_ln(p, cx, sl, lr, f_tm, t_f, cur_tm, cur_t, d["ln1_g"], d["ln1_b"], x1, t_x1, x1T, t_x1T)
        if l == 0:
            ffn_stage(p, cx, sl, mo, x1T, t_x1T, [ffn], None, f_tm, t_f)
            resid_ln(p, cx, sl, lr, f_tm, t_f, x1, t_x1, d["ln2_g"], d["ln2_b"], x2, t_x2, x2T, t_x2T)
            cur_tm, cur_t, curT, curT_t = x2, t_x2, x2T, t_x2T
        else:
            ffn_stage(p, cx, sl, mo, x1T, t_x1T, moe, router, f_tm, t_f)
            resid_ln(p, cx, sl, lr, f_tm, t_f, x1, t_x1, d["ln2_g"], d["ln2_b"], out, t_out, None, None, final=True)
    print("instr", {k: len(v) for k, v in p.ops.items()})
    return p.build(), ct


def kernel(**inp):
    f = lambda a: np.ascontiguousarray(np.asarray(a, dtype=np.float32))
    nc, ct = _get_full()
    shared = {"c_" + n: a for n, a in ct.items()}
    for l in range(2):
        shared[f"wcat{l}"] = make_wcat(f(inp["w_in"][l]), f(inp["w_in_vres"][l - 1]) if l > 0 else None)
        for n in ("w0", "a0", "k_k", "k_a", "gn_g", "gn_b", "w_lora", "a_lora", "g_lora"):
            shared[f"rw{l}_{n}"] = f(inp["rwkv_" + n][l])
        shared[f"rw{l}_mu"] = f(inp["rwkv_mu"][l])
        shared[f"rw{l}_r_k"] = f(inp["rwkv_r_k"][l]).reshape(1024)
        if l > 0:
            shared[f"rw{l}_mu_v"] = f(inp["rwkv_mu_vres"][l - 1])
            shared[f"rw{l}_v0"] = f(inp["rwkv_v0"][l - 1])
            shared[f"rw{l}_v_lora"] = f(inp["rwkv_v_lora"][l - 1])
        else:
            shared[f"rw{l}_mu_v"] = np.zeros(32, np.float32)
            shared[f"rw{l}_v0"] = np.zeros(1024, np.float32)
            shared[f"rw{l}_v_lora"] = np.zeros((32, 1024), np.float32)
        shared[f"qn{l}"] = f(inp["mla_q_norm"][l])
        shared[f"kvn{l}"] = f(inp["mla_kv_norm"][l])
        shared[f"w_uq{l}"] = f(inp["mla_w_uq"][l])
        shared[f"w_ukv{l}"] = f(inp["mla_w_ukv"][l])
        for n in range(3):
            shared[f"bo{l}_{n}"] = f(inp["branch_out"][l][n])
        shared[f"wo{l}"] = f(inp["w_out"][l])
        for n in ("ln1_g", "ln1_b", "ln2_g", "ln2_b"):
            shared[f"{n}{l}"] = f(inp[n][l])
    shared["ffn_wg"], shared["ffn_wu"], shared["ffn_wd"] = f(inp["ffn_wg"][0]), f(inp["ffn_wu"][0]), f(inp["ffn_wd"][0])
    shared["router"] = f(inp["moe_router"][0])
    for e in range(NE):
        shared[f"moe_wg{e}"] = f(inp["moe_wg"][0][e])
        shared[f"moe_wu{e}"] = f(inp["moe_wu"][0][e])
        shared[f"moe_wd{e}"] = f(inp["moe_wd"][0][e])
    x = f(inp["x"])
    active = [0, 1, 4, 5]
    idle = {k: np.zeros_like(v) for k, v in shared.items()}
    idle["x_tm"] = np.zeros((S, D), np.float32)
    idle["xT"] = np.zeros((D, S), np.float32)
    in_maps = []
    for c in range(8):
        if c in active:
            b = active.index(c)
            m = dict(shared)
            m["x_tm"] = x[b]
            m["xT"] = np.ascontiguousarray(x[b].T)
        else:
            m = idle
        in_maps.append(m)
    res = run_bass_kernel_spmd(nc, in_maps, core_ids=list(range(8)))
    return np.stack([res.results[c]["out"] for c in active], 0).astype(np.float32)


_FULL = []


def _get_full():
    if not _FULL:
        _FULL.append(build_full())
    return _FULL[0]
```
